# Optimizing a Trainium2 kernel written in Bass

```python
import math
import jax, jax.numpy as jnp
from jax import lax
import numpy as np

D_MODEL = 1024
BATCH = 4
SEQ = 4096
DEPTH = 2

N_META = 16
N_MIXERS = 2
POOL_WINDOWS = (2, 4, 8, 16)
N_POOL_GROUPS = len(POOL_WINDOWS)
POOL_GROUP = D_MODEL // N_POOL_GROUPS
DIFF_HEAD_DIM = 64
DIFF_HEADS = D_MODEL // (2 * DIFF_HEAD_DIM)
D_FF = 256 * ((8 * D_MODEL // 3 + 255) // 256)
Q_BLOCK = 128
EPS = 1e-6
N_POOL_LAYERS = (DEPTH + 1) // 2
N_ATTN_LAYERS = DEPTH // 2

kernel_name = 'hybrid_pool_diffattn_macaron'


def rmsnorm(x, g):
    xf = x.astype(jnp.float32)
    y = xf * lax.rsqrt(jnp.mean(xf * xf, axis=-1, keepdims=True) + EPS)
    return (y * g.astype(jnp.float32)).astype(x.dtype)


def swiglu(h, w1, w3, w2):
    return (jax.nn.silu(h @ w1) * (h @ w3)) @ w2


def lambda_init_fn(layer_idx):
    return 0.8 - 0.6 * math.exp(-0.3 * layer_idx)


def causal_multiscale_pool(h, w, b, scale):
    bsz, L, D = h.shape
    hf = h.astype(jnp.float32)
    cs = jnp.pad(jnp.cumsum(hf, axis=1), ((0, 0), (1, 0), (0, 0)))
    hg = hf.reshape(bsz, L, N_POOL_GROUPS, POOL_GROUP)
    csg = cs.reshape(bsz, L + 1, N_POOL_GROUPS, POOL_GROUP)
    t = jnp.arange(L)
    pooled = []
    for g, win in enumerate(POOL_WINDOWS):
        lo = jnp.maximum(t + 1 - win, 0)
        cnt = (t + 1 - lo).astype(jnp.float32)
        pooled.append((csg[:, 1:, g] - csg[:, lo, g]) / cnt[None, :, None])
    pooled = jnp.stack(pooled, axis=2)
    diff = (pooled - hg).astype(h.dtype)
    y = jnp.einsum('blgc,gcd->blgd', diff, w) + b
    return y.reshape(bsz, L, D) * scale


def diff_attention(h, w_qkv, w_o, lq1, lk1, lq2, lk2, subln_g, lambda_init):
    bsz, L, D = h.shape
    H, d = DIFF_HEADS, DIFF_HEAD_DIM
    q, k, v = jnp.split(h @ w_qkv, 3, axis=-1)
    q = q.reshape(bsz, L, H, 2, d)
    k = k.reshape(bsz, L, H, 2, d)
    v = v.reshape(bsz, L, H, 2 * d)
    lam = (jnp.exp(jnp.sum(lq1.astype(jnp.float32) * lk1.astype(jnp.float32)))
           - jnp.exp(jnp.sum(lq2.astype(jnp.float32) * lk2.astype(jnp.float32)))
           + lambda_init)
    n_blk = -(-L // Q_BLOCK)
    Lp = n_blk * Q_BLOCK
    qb = jnp.pad(q, ((0, 0), (0, Lp - L), (0, 0), (0, 0), (0, 0)))
    qb = qb.reshape(bsz, n_blk, Q_BLOCK, H, 2, d).transpose(1, 0, 2, 3, 4, 5)
    key_pos = jnp.arange(L)
    neg = jnp.finfo(jnp.float32).min

    def block(args):
        qi, i = args
        s = jnp.einsum('bqhcd,bkhcd->bhcqk', qi, k).astype(jnp.float32) * (d ** -0.5)
        qpos = i * Q_BLOCK + jnp.arange(Q_BLOCK)
        mask = key_pos[None, :] <= qpos[:, None]
        p = jax.nn.softmax(jnp.where(mask, s, neg), axis=-1)
        a = p[:, :, 0] - lam * p[:, :, 1]
        return jnp.einsum('bhqk,bkhe->bqhe', a.astype(v.dtype), v)

    o = lax.map(block, (qb, jnp.arange(n_blk)))
    o = o.transpose(1, 0, 2, 3, 4).reshape(bsz, Lp, H, 2 * d)[:, :L]
    o = rmsnorm(o, subln_g) * (1.0 - lambda_init)
    return o.reshape(bsz, L, H * 2 * d) @ w_o


def setup_inputs(seed: int = 0) -> dict:
    key = jax.random.key(seed)
    ks = jax.random.split(key, 20)
    f32 = jnp.float32
    nrm = lambda k, s, sc: jax.random.normal(k, s, f32) * sc
    gain = lambda k, s: 1.0 + 0.02 * jax.random.normal(k, s, f32)
    NP, NA = N_POOL_LAYERS, N_ATTN_LAYERS
    return {
        'x': jax.random.normal(ks[0], (BATCH, SEQ, D_MODEL), f32),
        'meta_tokens': nrm(ks[1], (N_META, D_MODEL), 1.0),
        'ffn_norm_pre': gain(ks[2], (DEPTH, 2, D_MODEL)),
        'ffn_norm_post': gain(ks[3], (DEPTH, 2, D_MODEL)),
        'ffn_w1': nrm(ks[4], (DEPTH, 2, D_MODEL, D_FF), D_MODEL ** -0.5),
        'ffn_w3': nrm(ks[5], (DEPTH, 2, D_MODEL, D_FF), D_MODEL ** -0.5),
        'ffn_w2': nrm(ks[6], (DEPTH, 2, D_FF, D_MODEL), D_FF ** -0.5),
        'mix_norm_pre': gain(ks[7], (DEPTH, D_MODEL)),
        'mix_norm_post': gain(ks[8], (DEPTH, D_MODEL)),
        'pool_w': nrm(ks[9], (NP, N_POOL_GROUPS, POOL_GROUP, POOL_GROUP), POOL_GROUP ** -0.5),
        'pool_b': nrm(ks[10], (NP, N_POOL_GROUPS, POOL_GROUP), 0.01),
        'pool_scale': 1.0 + 0.1 * jax.random.normal(ks[11], (NP, D_MODEL), f32),
        'attn_w_qkv': nrm(ks[12], (NA, D_MODEL, 3 * D_MODEL), D_MODEL ** -0.5),
        'attn_w_o': nrm(ks[13], (NA, D_MODEL, D_MODEL), D_MODEL ** -0.5),
        'attn_lambda_q1': nrm(ks[14], (NA, DIFF_HEAD_DIM), 0.1),
        'attn_lambda_k1': nrm(ks[15], (NA, DIFF_HEAD_DIM), 0.1),
        'attn_lambda_q2': nrm(ks[16], (NA, DIFF_HEAD_DIM), 0.1),
        'attn_lambda_k2': nrm(ks[17], (NA, DIFF_HEAD_DIM), 0.1),
        'attn_subln': gain(ks[18], (NA, 2 * DIFF_HEAD_DIM)),
    }


def reference(x, meta_tokens, ffn_norm_pre, ffn_norm_post, ffn_w1, ffn_w3, ffn_w2,
              mix_norm_pre, mix_norm_post, pool_w, pool_b, pool_scale,
              attn_w_qkv, attn_w_o, attn_lambda_q1, attn_lambda_k1, attn_lambda_q2,
              attn_lambda_k2, attn_subln):
    bsz = x.shape[0]
    meta = jnp.broadcast_to(meta_tokens[None].astype(x.dtype), (bsz, N_META, x.shape[-1]))
    h = jnp.concatenate([meta, x], axis=1)
    for i in range(DEPTH):
        f = swiglu(rmsnorm(h, ffn_norm_pre[i, 0]), ffn_w1[i, 0], ffn_w3[i, 0], ffn_w2[i, 0])
        h = h + 0.5 * rmsnorm(f, ffn_norm_post[i, 0])
        hn = rmsnorm(h, mix_norm_pre[i])
        j = i // N_MIXERS
        if i % N_MIXERS == 0:
            m = causal_multiscale_pool(hn, pool_w[j], pool_b[j], pool_scale[j])
        else:
            m = diff_attention(hn, attn_w_qkv[j], attn_w_o[j], attn_lambda_q1[j],
                               attn_lambda_k1[j], attn_lambda_q2[j], attn_lambda_k2[j],
                               attn_subln[j], lambda_init_fn(i))
        h = h + rmsnorm(m, mix_norm_post[i])
        f = swiglu(rmsnorm(h, ffn_norm_pre[i, 1]), ffn_w1[i, 1], ffn_w3[i, 1], ffn_w2[i, 1])
        h = h + 0.5 * rmsnorm(f, ffn_norm_post[i, 1])
    return h[:, N_META:]
```

```python
import math
from contextlib import ExitStack
import numpy as np
import ml_dtypes
import concourse.bass as bass
import concourse.mybir as mybir
from concourse.bass_utils import run_bass_kernel_spmd

F32 = mybir.dt.float32
BF16 = mybir.dt.bfloat16
ALU = mybir.AluOpType
AF = mybir.ActivationFunctionType

D = 1024
NCH = 8
DFF = 2816
NF = 22
NPRE = 16
NMAIN = 2048
NSLOT = NPRE + NMAIN
LSEQ = 16 + 4096
EPS = 1e-6
LAMBDA_INIT = 0.8 - 0.6 * math.exp(-0.3 * 1)
POOL_WINDOWS = (2, 4, 8, 16)
ENGS = ("pe", "act", "dve", "pool", "sp")
DBG = None


class Buf:
    __slots__ = ("name", "w", "r", "excl")

    def __init__(self, name="", excl=False):
        self.name = name
        self.w = None
        self.r = []
        self.excl = excl


class Sig:
    __slots__ = ("key", "val", "clock")

    def __init__(self, key, val, clock):
        self.key = key
        self.val = val
        self.clock = clock


class Prog:
    SEM_LIMIT = 30000

    def __init__(self, nc, stack, block):
        self.nc = nc
        self.stack = stack
        self.block = block
        self.q = {e: [] for e in ENGS}
        self.sems = {}
        self.epoch = {e: 0 for e in ENGS}
        self.cnt = {e: 0 for e in ENGS}
        self.known = {e: {} for e in ENGS}
        self.nsem = 0
        self.dma_cnt = {}
        self.last = {}
        self.coll_keys = set()

    def sem(self, key):
        s = self.sems.get(key)
        if s is None:
            s = self.stack.enter_context(self.nc.semaphore("s%d" % self.nsem))
            self.nsem += 1
            self.sems[key] = s
        return s

    def new_dma_sem(self, name):
        key = ("dma", name)
        self.sem(key)
        self.dma_cnt[key] = 0
        return key

    def _split(self, reads, writes):
        ex = [b for b in reads if b.excl]
        if ex:
            reads = [b for b in reads if not b.excl]
            writes = list(writes) + ex
        return reads, writes

    def _deps(self, eng, reads, writes):
        need = []
        for b in reads:
            if b.w is not None:
                need.append(b.w)
        for b in writes:
            if b.w is not None:
                need.append(b.w)
            need.extend(b.r)
        kn = self.known[eng]
        best = {}
        for s in need:
            if kn.get(s.key, 0) >= s.val:
                continue
            o = best.get(s.key)
            if o is None or o.val < s.val:
                best[s.key] = s
        waits = []
        for s in best.values():
            waits.append((s.key, s.val))
            for k, v in s.clock.items():
                if kn.get(k, 0) < v:
                    kn[k] = v
        return waits

    def _mark(self, sig, reads, writes):
        for b in reads:
            b.r.append(sig)
        for b in writes:
            b.w = sig
            b.r = []
        self.last[sig.key] = sig

    def _newsig(self, eng):
        if self.cnt[eng] >= self.SEM_LIMIT:
            self.epoch[eng] += 1
            self.cnt[eng] = 0
        self.cnt[eng] += 1
        key = (eng, self.epoch[eng])
        self.sem(key)
        clock = dict(self.known[eng])
        clock[key] = self.cnt[eng]
        return Sig(key, self.cnt[eng], clock)

    def op(self, eng, fn, reads=(), writes=()):
        reads, writes = self._split(reads, writes)
        waits = self._deps(eng, reads, writes)
        sig = self._newsig(eng)
        self._mark(sig, reads, writes)
        self.q[eng].append((waits, fn, sig.key, 1))
        return sig

    def group(self, eng, fns, reads=(), writes=()):
        reads, writes = self._split(reads, writes)
        waits = self._deps(eng, reads, writes)
        sig = self._newsig(eng)
        self._mark(sig, reads, writes)
        n = len(fns)
        for i, fn in enumerate(fns):
            self.q[eng].append((waits if i == 0 else [], fn,
                                sig.key if i == n - 1 else None, 1))
        return sig

    def custom(self, eng, fn, semkey, inc, reads=(), writes=()):
        waits = self._deps(eng, reads, writes)
        self.dma_cnt[semkey] += inc
        clock = dict(self.known[eng])
        clock[semkey] = self.dma_cnt[semkey]
        sig = Sig(semkey, self.dma_cnt[semkey], clock)
        self._mark(sig, reads, writes)
        self.q[eng].append((waits, fn, semkey, inc))
        return sig

    def dma(self, eng, out, in_, semkey, reads=(), writes=(), **kw):
        return self.custom(eng, lambda e: e.dma_start(out=out, in_=in_, **kw), semkey, 16,
                           reads, writes)

    def allgather(self, in_ap, out_ap, groups, reads=(), writes=()):
        key = self.fresh()
        self.coll_keys.add(key)
        return self.custom("pool", lambda e: e.collective_compute(
            "AllGather", ALU.bypass, replica_groups=groups, ins=[in_ap.opt()],
            outs=[out_ap.opt()]), key, 1, reads, writes)

    def wait_all(self, eng, sigs):
        kn = self.known[eng]
        best = {}
        for s in sigs:
            if kn.get(s.key, 0) < s.val:
                o = best.get(s.key)
                if o is None or o.val < s.val:
                    best[s.key] = s
        waits = []
        for s in best.values():
            waits.append((s.key, s.val))
            for k, v in s.clock.items():
                if kn.get(k, 0) < v:
                    kn[k] = v
        if waits:
            self.q[eng].append((waits, None, None, 0))

    def fresh(self):
        self.nfresh = getattr(self, "nfresh", 0) + 1
        return self.new_dma_sem("f%d" % self.nfresh)

    def fence(self, skip_collectives=False):
        sigs = [s for k, s in self.last.items()
                if not (skip_collectives and k in self.coll_keys)]
        for e in ENGS:
            self.wait_all(e, sigs)

    def flush(self):
        sems = self.sems

        def run(e, lst):
            for waits, fn, key, inc in lst:
                for (k, v) in waits:
                    e.wait_ge(sems[k], v)
                if fn is None:
                    continue
                ins = fn(e)
                if key is not None:
                    ins.then_inc(sems[key], inc)

        b = self.block
        for name, starter in (("pe", b.tensor), ("act", b.scalar), ("dve", b.vector),
                              ("pool", b.gpsimd), ("sp", b.sync)):
            lst = self.q[name]
            if lst:
                self.q[name] = []
                starter(lambda e, lst=lst: run(e, lst))


def _resh(ap, shape):
    if len(shape) == 1:
        return ap
    if len(shape) == 2:
        return ap.rearrange("p (a b) -> p a b", a=shape[0], b=shape[1])
    if len(shape) == 3:
        return ap.rearrange("p (a b c) -> p a b c", a=shape[0], b=shape[1], c=shape[2])
    raise ValueError(shape)


class Arena:
    def __init__(self, t, words):
        self.t = t
        self.W = words
        self.top = 0

    def alloc(self, shape, dtype):
        n = int(np.prod(shape))
        nbytes = n * (4 if dtype == F32 else 2)
        words = (nbytes + 31) // 32 * 8
        off = self.top
        self.top += words
        assert self.top <= self.W, ("arena overflow", self.top, self.W)
        return self.view(off, shape, dtype)

    def view(self, off, shape, dtype):
        n = int(np.prod(shape))
        if dtype == F32:
            ap = self.t[:, off:off + n]
        else:
            ap = self.t[:, off:off + (n + 1) // 2].bitcast(BF16)
            if n % 2:
                ap = ap[:, :n]
        return _resh(ap, shape)

    def mark(self):
        return self.top

    def release(self, m):
        self.top = m


def _vec_cols():
    cols = {}
    n = 0
    for i in range(2):
        for j in range(2):
            cols[("fpre", i, j)] = n; n += 8
            cols[("fpost", i, j)] = n; n += 8
    for i in range(2):
        cols[("mpre", i)] = n; n += 8
        cols[("mpost", i)] = n; n += 8
    cols["pool_b"] = n; n += 8
    cols["pool_scale"] = n; n += 8
    cols["subln"] = n; n += 1
    for k in ("lq1", "lk1", "lq2", "lk2"):
        cols[k] = n; n += 64
    cols["poolfix"] = n; n += 64
    cols["selA"] = n; n += 1
    cols["selB"] = n; n += 1
    return cols, n


VCOL, NV = _vec_cols()


def _fm(v):
    return np.ascontiguousarray(np.asarray(v, np.float32).reshape(8, 128).T)


def _build_vecs(inp, half):
    v = np.zeros((128, NV), np.float32)
    for i in range(2):
        for j in range(2):
            c = VCOL[("fpre", i, j)]; v[:, c:c + 8] = _fm(inp["ffn_norm_pre"][i, j])
            c = VCOL[("fpost", i, j)]; v[:, c:c + 8] = _fm(inp["ffn_norm_post"][i, j])
    for i in range(2):
        c = VCOL[("mpre", i)]; v[:, c:c + 8] = _fm(inp["mix_norm_pre"][i])
        c = VCOL[("mpost", i)]; v[:, c:c + 8] = _fm(inp["mix_norm_post"][i])
    c = VCOL["pool_b"]; v[:, c:c + 8] = _fm(np.asarray(inp["pool_b"][0]).reshape(-1))
    c = VCOL["pool_scale"]; v[:, c:c + 8] = _fm(inp["pool_scale"][0])
    c = VCOL["subln"]; v[:, c] = np.asarray(inp["attn_subln"][0], np.float32)
    for k, nm in (("lq1", "attn_lambda_q1"), ("lk1", "attn_lambda_k1"),
                  ("lq2", "attn_lambda_q2"), ("lk2", "attn_lambda_k2")):
        c = VCOL[k]; v[:, c:c + 64] = np.asarray(inp[nm][0], np.float32)[None, :]
    c = VCOL["poolfix"]
    for g, w in enumerate(POOL_WINDOWS):
        for t in range(16):
            cnt = min(t + 1, w) if half == 0 else w
            v[:, c + g * 16 + t] = 1.0 / cnt
    v[:, VCOL["selA"]] = 1.0 if half == 0 else 0.0
    v[:, VCOL["selB"]] = 0.0 if half == 0 else 1.0
    return v


class Env:
    pass


def make_env(nc, st, block, need_w=True):
    E = Env()
    E.nc = nc
    E.p = Prog(nc, st, block)
    words = 52800
    E.arena_t = st.enter_context(nc.sbuf_tensor("arena", [128, words], F32))
    E.A = Arena(E.arena_t, words)
    E.ps = [st.enter_context(nc.psum_tensor("ps%d" % i, [128, 512], F32)) for i in range(8)]
    E.psb = [Buf("ps%d" % i, excl=True) for i in range(8)]
    A = E.A
    E.ones = A.alloc([128], BF16)
    E.onesb = Buf("ones")
    E.vecs = A.alloc([NV], F32)
    E.vecsb = Buf("vecs")
    E.stsem = E.p.new_dma_sem("st")
    E.p.op("dve", lambda e: e.memset(E.ones, 1.0), writes=[E.onesb])
    return E


def load_vecs(E, vecs_dram):
    E.p.dma("sp", E.vecs, vecs_dram, E.p.fresh(), writes=[E.vecsb])


def vcol(E, key, c=0, n=1):
    o = VCOL[key] + c
    return E.vecs[:, o:o + n]


class WRing:
    def __init__(self, E, name, nslots, shape, split):
        self.E = E
        self.n = nslots
        self.split = split
        self.tiles = [E.A.alloc(shape, BF16) for _ in range(nslots)]
        self.bufs = [Buf("%s%d" % (name, i)) for i in range(nslots)]
        self.sems = [E.p.new_dma_sem("%s%d" % (name, i)) for i in range(nslots)]
        self.seq = []
        self.issued = 0
        self.used = 0

    def plan(self, srcs):
        self.seq.extend(srcs)

    def _issue(self):
        k = self.issued
        s = k % self.n
        src = self.seq[k]
        a = self.split
        dst = self.tiles[s]
        flat = dst
        nd = len(dst.shape)
        if nd == 3:
            flat = dst.rearrange("p a b -> p (a b)")
        elif nd == 4:
            flat = dst.rearrange("p a b c -> p (a b c)")
        self.E.p.dma("pool", flat.rearrange("p (a m) -> p a m", a=a),
                     src.rearrange("p (a m) -> p a m", a=a), self.sems[s],
                     writes=[self.bufs[s]])
        self.issued += 1

    def start(self):
        while self.issued < min(self.n, len(self.seq)):
            self._issue()

    def get(self):
        k = self.used
        assert k < self.issued, "weight ring underflow"
        s = k % self.n
        return self.tiles[s], self.bufs[s]

    def done(self):
        self.used += 1
        if self.issued < len(self.seq):
            self._issue()


class Tile_:
    def __init__(self, s0, n, buf):
        self.s0 = s0
        self.n = n
        self.buf = buf


def emit_rstd(E, sq_aps, sq_bufs, n, nfeat, post_bias, rs_ap, rs_buf, bank):
    p = E.p
    ps = E.ps[bank]
    m = len(sq_aps)
    fns = []
    for i, a in enumerate(sq_aps):
        fns.append(lambda e, a=a, i=i: e.matmul(ps[:, :n], lhsT=E.ones, rhs=a,
                                                 start=(i == 0), stop=(i == m - 1)))
    p.group("pe", fns, reads=list(sq_bufs) + [E.onesb], writes=[E.psb[bank]])
    p.op("act", lambda e: e.activation(out=rs_ap, in_=ps[:, :n], func=AF.Ln,
                                       scale=1.0 / nfeat, bias=E.epsc),
         reads=[E.psb[bank], E.constb], writes=[rs_buf])
    bias_ap = E.ln05c if post_bias else E.zeroc
    p.op("act", lambda e: e.activation(out=rs_ap, in_=rs_ap, func=AF.Exp,
                                       scale=-0.5, bias=bias_ap),
         reads=[rs_buf, E.constb], writes=[rs_buf])


def setup_consts(E):
    A = E.A
    E.cst = A.alloc([8], F32)
    E.constb = Buf("const")
    E.epsc = E.cst[:, 0:1]
    E.ln05c = E.cst[:, 1:2]
    E.zeroc = E.cst[:, 2:3]
    p = E.p
    p.op("dve", lambda e: e.memset(E.cst[:, 0:1], EPS), writes=[E.constb])
    p.op("dve", lambda e: e.memset(E.cst[:, 1:2], math.log(0.5)), reads=[E.constb], writes=[E.constb])
    p.op("dve", lambda e: e.memset(E.cst[:, 2:3], 0.0), reads=[E.constb], writes=[E.constb])


def alloc_ffn(E):
    A = E.A
    Fn = Env()
    Fn.NW = 1040
    wid = (512, 512, NPRE)
    Fn.G = A.alloc([NF, Fn.NW], BF16)
    Fn.xn = [A.alloc([NCH, Fn.NW], BF16) for _ in range(2)]
    Fn.F = A.alloc([NCH, Fn.NW], F32)
    Fn.Fbf = Fn.F.rearrange("p a b -> p (a b)").bitcast(BF16)
    Fn.sil = [A.alloc([w], F32) for w in wid]
    Fn.silb = [Buf("sil%d" % i) for i in range(3)]
    Fn.rs = [A.alloc([w], F32) for w in wid]
    Fn.rsb = [Buf("rs%d" % i) for i in range(3)]
    Fn.rq = Fn.rs
    Fn.rqb = Fn.rsb
    Fn.xnb = [[[Buf() for _ in range(3)] for _ in range(NCH)] for _ in range(2)]
    Fn.Gb = [[Buf() for _ in range(3)] for _ in range(NF)]
    Fn.Fb = [[Buf() for _ in range(3)] for _ in range(NCH)]
    Fn.allF = [b_ for row in Fn.Fb for b_ in row]
    return Fn


def _offs(tiles):
    offs = []
    o = 0
    for t in tiles:
        offs.append(o)
        o += t.n
    return offs


def _sqv(Fn, i, n):
    return Fn.Fbf[:, i * 4096:(i + 1) * 4096].rearrange("p (c t) -> p c t", c=NCH)[:, :, :n]


def ffn_pre_sq(E, Fn, tiles):
    p = E.p
    for i, t in enumerate(tiles):
        p.op("act", lambda e, t=t, i=i: e.activation(
            out=_sqv(Fn, i, t.n), in_=E.hT[:, :, t.s0:t.s0 + t.n], func=AF.Square),
            reads=[t.buf], writes=Fn.allF)


def ffn_pre_rest(E, Fn, tiles, gpre, xi):
    p = E.p
    hT = E.hT
    offs = _offs(tiles)
    xn = Fn.xn[xi]
    for i, t in enumerate(tiles):
        n = t.n
        sq = _sqv(Fn, i, n)
        emit_rstd(E, [sq[:, c, :] for c in range(NCH)], Fn.allF, n, D, False,
                  Fn.rs[i][:, :n], Fn.rsb[i], 6 + (i % 2))
        for c in range(NCH):
            p.op("dve", lambda e, c=c, i=i, t=t, n=n: e.scalar_tensor_tensor(
                out=xn[:, c, offs[i]:offs[i] + n], in0=hT[:, c, t.s0:t.s0 + n],
                scalar=vcol(E, gpre, c), in1=Fn.rs[i][:, :n], op0=ALU.mult, op1=ALU.mult),
                reads=[t.buf, Fn.rsb[i], E.vecsb], writes=[Fn.xnb[xi][c][i]])


def ffn_up(E, Fn, tiles, w13, xi, hooks=None):
    p = E.p
    nt = len(tiles)
    offs = _offs(tiles)
    xn = Fn.xn[xi]
    xn_all = [Fn.xnb[xi][c][i] for c in range(NCH) for i in range(nt)]
    for f in range(NF):
        if hooks and f in hooks:
            hooks[f]()
        wt, wb = w13.get()
        for wi in range(2):
            base = 0 if wi == 0 else 3
            fns = []
            for c in range(NCH):
                for i, t in enumerate(tiles):
                    fns.append(lambda e, c=c, i=i, t=t, wi=wi, base=base, wt=wt: e.matmul(
                        E.ps[base + i][:, :t.n], lhsT=wt[:, wi, c, :],
                        rhs=xn[:, c, offs[i]:offs[i] + t.n],
                        start=(c == 0), stop=(c == NCH - 1)))
            p.group("pe", fns, reads=[wb] + xn_all, writes=[E.psb[base + i] for i in range(nt)])
            if wi == 0:
                for i, t in enumerate(tiles):
                    p.op("act", lambda e, i=i, t=t: e.activation(
                        out=Fn.sil[i][:, :t.n], in_=E.ps[i][:, :t.n], func=AF.Silu),
                        reads=[E.psb[i]], writes=[Fn.silb[i]])
            else:
                for i, t in enumerate(tiles):
                    p.op("dve", lambda e, i=i, t=t, f=f: e.tensor_tensor(
                        out=Fn.G[:, f, offs[i]:offs[i] + t.n], in0=E.ps[3 + i][:, :t.n],
                        in1=Fn.sil[i][:, :t.n], op=ALU.mult),
                        reads=[E.psb[3 + i], Fn.silb[i]], writes=[Fn.Gb[f][i]])
        w13.done()


def ffn_down(E, Fn, tiles, w2, gpost, xi):
    p = E.p
    nt = len(tiles)
    offs = _offs(tiles)
    xn = Fn.xn[xi]
    for dc in range(NCH):
        wt, wb = w2.get()
        base = 0 if dc % 2 == 0 else 3
        fns = []
        for f in range(NF):
            for i, t in enumerate(tiles):
                fns.append(lambda e, f=f, i=i, t=t, base=base, wt=wt: e.matmul(
                    E.ps[base + i][:, :t.n], lhsT=wt[:, f, :],
                    rhs=Fn.G[:, f, offs[i]:offs[i] + t.n],
                    start=(f == 0), stop=(f == NF - 1)))
        p.group("pe", fns, reads=[wb] + [Fn.Gb[f][i] for f in range(NF) for i in range(nt)],
                writes=[E.psb[base + i] for i in range(nt)])
        w2.done()
        for i, t in enumerate(tiles):
            p.op("act", lambda e, i=i, t=t, dc=dc, base=base: e.activation(
                out=xn[:, dc, offs[i]:offs[i] + t.n], in_=E.ps[base + i][:, :t.n],
                func=AF.Square),
                reads=[E.psb[base + i]], writes=[Fn.xnb[xi][dc][i]])
            p.op("dve", lambda e, i=i, t=t, dc=dc, base=base: e.tensor_scalar(
                out=Fn.F[:, dc, offs[i]:offs[i] + t.n], in0=E.ps[base + i][:, :t.n],
                scalar1=vcol(E, gpost, dc), scalar2=None, op0=ALU.mult),
                reads=[E.psb[base + i], E.vecsb], writes=[Fn.Fb[dc][i]])


def ffn_post_stats(E, Fn, tiles, xi):
    offs = _offs(tiles)
    xn = Fn.xn[xi]
    for i, t in enumerate(tiles):
        n = t.n
        emit_rstd(E, [xn[:, c, offs[i]:offs[i] + n] for c in range(NCH)],
                  [Fn.xnb[xi][c][i] for c in range(NCH)], n, D, True,
                  Fn.rq[i][:, :n], Fn.rqb[i], 6 + (i % 2))


def ffn_post_resid(E, Fn, tiles):
    p = E.p
    hT = E.hT
    offs = _offs(tiles)
    for i, t in enumerate(tiles):
        n = t.n
        for c in range(NCH):
            fa = Fn.F[:, c, offs[i]:offs[i] + n]
            p.op("dve", lambda e, fa=fa, i=i, n=n: e.tensor_tensor(
                out=fa, in0=fa, in1=Fn.rq[i][:, :n], op=ALU.mult),
                reads=[Fn.Fb[c][i], Fn.rqb[i]], writes=[Fn.Fb[c][i]])
            p.op("dve", lambda e, fa=fa, c=c, t=t, n=n: e.tensor_tensor(
                out=hT[:, c, t.s0:t.s0 + n], in0=hT[:, c, t.s0:t.s0 + n], in1=fa, op=ALU.add),
                reads=[Fn.Fb[c][i], t.buf], writes=[t.buf])


def emit_ffn_chain(E, Fn, units, w13, w2):
    ffn_pre_sq(E, Fn, units[0][0])
    ffn_pre_rest(E, Fn, units[0][0], units[0][1], 0)
    for k, u in enumerate(units):
        tiles, gpre, gpost = u[:3]
        xi = k % 2
        hooks = None
        if k + 1 < len(units):
            nt_, ng_ = units[k + 1][0], units[k + 1][1]
            hooks = {8: (lambda nt_=nt_: ffn_pre_sq(E, Fn, nt_)),
                     11: (lambda nt_=nt_, ng_=ng_, xi=xi: ffn_pre_rest(E, Fn, nt_, ng_, 1 - xi))}
        ffn_up(E, Fn, tiles, w13, xi, hooks)
        ffn_down(E, Fn, tiles, w2, gpost, xi)
        ffn_post_stats(E, Fn, tiles, xi)
        ffn_post_resid(E, Fn, tiles)
        if len(u) > 3 and u[3] is not None:
            u[3]()


def emit_ffn(E, Fn, tiles, w13, w2, gpre, gpost):
    emit_ffn_chain(E, Fn, [(tiles, gpre, gpost)], w13, w2)


def ffn_plan(w13ring, w2ring, w13_dram, w2_dram, nsuper):
    for _ in range(nsuper):
        w13ring.plan([w13_dram[f] for f in range(NF)])
        w2ring.plan([w2_dram[dc] for dc in range(NCH)])


def emit_pool_mixer(E, tiles_all, poolw_dram):
    p = E.p
    A = E.A
    hT = E.hT
    m0 = A.mark()
    NS = NSLOT
    rs = A.alloc([NS], F32)
    rsb = [Buf() for _ in tiles_all]
    sqh = A.alloc([NCH, 512], BF16)
    sqhb = Buf("sqh")
    pw = A.alloc([4, 2, 256], BF16)
    pwb = Buf("pw")
    p.dma("pool", pw.rearrange("p a b c -> p (a b c)").rearrange("p (a m) -> p a m", a=2),
          poolw_dram.rearrange("p (a m) -> p a m", a=2), p.fresh(), writes=[pwb])
    bs = A.alloc([NCH], F32)
    bsb = Buf("bs")
    p.op("dve", lambda e: e.tensor_tensor(out=bs, in0=vcol(E, "pool_b", 0, 8),
                                          in1=vcol(E, "pool_scale", 0, 8), op=ALU.mult),
         reads=[E.vecsb], writes=[bsb])
    for i, t in enumerate(tiles_all):
        n = t.n
        p.op("act", lambda e, t=t, n=n: e.activation(out=sqh[:, :, :n], in_=hT[:, :, t.s0:t.s0 + n],
                                                     func=AF.Square),
             reads=[t.buf], writes=[sqhb])
        emit_rstd(E, [sqh[:, c, :n] for c in range(NCH)], [sqhb], n, D, False,
                  rs[:, t.s0:t.s0 + n], rsb[i], 6 + (i % 2))
    diff = A.alloc([NCH, NS], BF16)
    diffb = [Buf() for _ in range(NCH)]
    hn = [A.alloc([NS], F32)] * 2
    hnb = [Buf()] * 2
    pa = [A.alloc([NS], F32)] * 2
    pab = [Buf()] * 2
    pb_ = [A.alloc([NS], F32)] * 2
    pbb = [Buf()] * 2
    allh = [t.buf for t in tiles_all]
    for c in range(NCH):
        g = c // 2
        win = POOL_WINDOWS[g]
        k = c % 2
        p.op("dve", lambda e, c=c, k=k: e.scalar_tensor_tensor(
            out=hn[k], in0=hT[:, c, :], scalar=vcol(E, ("mpre", 0), c), in1=rs,
            op0=ALU.mult, op1=ALU.mult),
            reads=allh + rsb + [E.vecsb], writes=[hnb[k]])
        src, srcb = hn[k], hnb[k]
        sh = 1
        flip = 0
        while sh < win:
            dst, dstb = (pa[k], pab[k]) if flip == 0 else (pb_[k], pbb[k])
            p.op("dve", lambda e, dst=dst, src=src, sh=sh: e.tensor_tensor(
                out=dst[:, sh:], in0=src[:, sh:], in1=src[:, :NS - sh], op=ALU.add),
                reads=[srcb], writes=[dstb])
            p.op("dve", lambda e, dst=dst, src=src, sh=sh: e.tensor_copy(
                out=dst[:, :sh], in_=src[:, :sh]),
                reads=[srcb, dstb], writes=[dstb])
            src, srcb = dst, dstb
            sh *= 2
            flip ^= 1
        p.op("dve", lambda e, c=c, src=src, k=k, win=win: e.scalar_tensor_tensor(
            out=diff[:, c, :], in0=src, scalar=1.0 / win, in1=hn[k],
            op0=ALU.mult, op1=ALU.subtract),
            reads=[srcb, hnb[k]], writes=[diffb[c]])
        fx = vcol(E, "poolfix", g * 16, 16)
        p.op("dve", lambda e, src=src, fx=fx: e.tensor_tensor(
            out=src[:, :16], in0=src[:, :16], in1=fx, op=ALU.mult),
            reads=[srcb, diffb[c], E.vecsb], writes=[srcb])
        p.op("dve", lambda e, c=c, src=src, k=k: e.tensor_tensor(
            out=diff[:, c, :16], in0=src[:, :16], in1=hn[k][:, :16], op=ALU.subtract),
            reads=[srcb, hnb[k], diffb[c]], writes=[diffb[c]])
    M = [A.alloc([NCH, 512], F32)] * 2
    Mb = [[Buf() for _ in range(NCH)]] * 2
    sqm = [A.alloc([NCH, 512], BF16)] * 2
    sqmb = [[Buf() for _ in range(NCH)]] * 2
    rs2 = [A.alloc([512], F32) for _ in range(2)]
    rs2b = [Buf() for _ in range(2)]
    for i, t in enumerate(tiles_all):
        n = t.n
        k = i % 2
        for c in range(NCH):
            g = c // 2
            bank = c % 6
            fns = []
            for ci in range(2):
                fns.append(lambda e, g=g, ci=ci, c=c, t=t, n=n, bank=bank: e.matmul(
                    E.ps[bank][:, :n], lhsT=pw[:, g, ci, (c % 2) * 128:(c % 2) * 128 + 128],
                    rhs=diff[:, 2 * g + ci, t.s0:t.s0 + n], start=(ci == 0), stop=(ci == 1)))
            p.group("pe", fns, reads=[pwb, diffb[2 * g], diffb[2 * g + 1]], writes=[E.psb[bank]])
            p.op("act", lambda e, c=c, k=k, n=n, bank=bank: e.activation(
                out=sqm[k][:, c, :n], in_=E.ps[bank][:, :n], func=AF.Square,
                scale=vcol(E, "pool_scale", c), bias=bs[:, c:c + 1]),
                reads=[E.psb[bank], bsb, E.vecsb], writes=[sqmb[k][c]])
            p.op("dve", lambda e, c=c, k=k, n=n, bank=bank: e.tensor_scalar(
                out=M[k][:, c, :n], in0=E.ps[bank][:, :n], scalar1=vcol(E, "pool_b", c),
                scalar2=vcol(E, "pool_scale", c), op0=ALU.add, op1=ALU.mult),
                reads=[E.psb[bank], E.vecsb], writes=[Mb[k][c]])
        emit_rstd(E, [sqm[k][:, c, :n] for c in range(NCH)], sqmb[k], n, D, False,
                  rs2[k][:, :n], rs2b[k], 6 + (i % 2))
        for c in range(NCH):
            ma = M[k][:, c, :n]
            p.op("dve", lambda e, ma=ma, c=c, k=k, n=n: e.scalar_tensor_tensor(
                out=ma, in0=ma, scalar=vcol(E, ("mpost", 0), c), in1=rs2[k][:, :n],
                op0=ALU.mult, op1=ALU.mult),
                reads=[Mb[k][c], rs2b[k], E.vecsb], writes=[Mb[k][c]])
            p.op("pool", lambda e, ma=ma, c=c, t=t, n=n: e.tensor_tensor(
                out=hT[:, c, t.s0:t.s0 + n], in0=hT[:, c, t.s0:t.s0 + n], in1=ma, op=ALU.add),
                reads=[Mb[k][c], t.buf], writes=[t.buf])
    p.fence()
    A.release(m0)


def emit_mixnorm_to_bf16(E, tiles_all, out_ap_fn, out_bufs, gkey):
    p = E.p
    A = E.A
    hT = E.hT
    m0 = A.mark()
    sqh = A.alloc([NCH, 512], BF16)
    sqhb = Buf("sqh")
    rs = [A.alloc([512], F32) for _ in range(2)]
    rsb = [Buf() for _ in range(2)]
    for i, t in enumerate(tiles_all):
        n = t.n
        k = i % 2
        p.op("act", lambda e, t=t, n=n: e.activation(out=sqh[:, :, :n], in_=hT[:, :, t.s0:t.s0 + n],
                                                     func=AF.Square),
             reads=[t.buf], writes=[sqhb])
        emit_rstd(E, [sqh[:, c, :n] for c in range(NCH)], [sqhb], n, D, False,
                  rs[k][:, :n], rsb[k], 6 + (i % 2))
        for c in range(NCH):
            p.op("dve", lambda e, c=c, t=t, n=n, k=k: e.scalar_tensor_tensor(
                out=out_ap_fn(c, t.s0, n), in0=hT[:, c, t.s0:t.s0 + n],
                scalar=vcol(E, gkey, c), in1=rs[k][:, :n], op0=ALU.mult, op1=ALU.mult),
                reads=[t.buf, rsb[k], E.vecsb], writes=[out_bufs[i]])
    p.fence()
    A.release(m0)


def emit_attention(E, hn_src, wqkv_dram, o_dst, after_tail=None):
    p = E.p
    A = E.A
    m0 = A.mark()
    lam = A.alloc([8], F32)
    lamb = Buf("lam")
    tmp64 = A.alloc([64], F32)
    tmpb = Buf("tmp64")
    for j, (a, b) in enumerate((("lq1", "lk1"), ("lq2", "lk2"))):
        p.op("dve", lambda e, a=a, b=b: e.tensor_tensor(
            out=tmp64, in0=vcol(E, a, 0, 64), in1=vcol(E, b, 0, 64), op=ALU.mult),
            reads=[E.vecsb, tmpb], writes=[tmpb])
        p.op("dve", lambda e, j=j: e.reduce_sum(out=lam[:, j:j + 1], in_=tmp64,
                                                axis=mybir.AxisListType.X),
             reads=[tmpb, lamb], writes=[lamb])
    p.op("act", lambda e: e.activation(out=lam[:, 2:4], in_=lam[:, 0:2], func=AF.Exp),
         reads=[lamb], writes=[lamb])
    p.op("dve", lambda e: e.scalar_tensor_tensor(
        out=lam[:, 4:5], in0=lam[:, 3:4], scalar=-LAMBDA_INIT, in1=lam[:, 2:3],
        op0=ALU.add, op1=ALU.subtract), reads=[lamb], writes=[lamb])
    p.op("dve", lambda e: e.tensor_scalar(
        out=lam[:, 5:6], in0=vcol(E, "subln"), scalar1=1.0 - LAMBDA_INIT, scalar2=None,
        op0=ALU.mult), reads=[lamb, E.vecsb], writes=[lamb])
    neglam = lam[:, 4:5]
    gsub = lam[:, 5:6]
    tri = A.alloc([128], BF16)
    trib = Buf("tri")
    p.op("pool", lambda e: e.memset(tri, 1.0), writes=[trib])
    p.op("pool", lambda e: e.affine_select(out=tri, in_=tri, pattern=[[1, 128]],
                                           compare_op=ALU.is_ge, fill=0.0, base=0,
                                           channel_multiplier=-1),
         reads=[trib], writes=[trib])

    NKB = 33
    qT = A.alloc([2, 4096], BF16)
    kT = A.alloc([2, LSEQ], BF16)
    V = A.alloc([NKB, 256], BF16)
    W = A.alloc([3, NCH, 256], BF16)
    Wb = Buf("wqkv")
    wsem = E.p.new_dma_sem("wqkv")
    hnt = [A.alloc([NCH, 512], BF16) for _ in range(2)]
    hntb = [Buf() for _ in range(2)]
    hsem = [E.p.new_dma_sem("hn%d" % i) for i in range(2)]
    NPT = 8
    PT = [A.alloc([512], BF16) for _ in range(NPT)]
    PTb = [Buf() for _ in range(NPT)]
    oc = [A.alloc([512], F32) for _ in range(2)]
    ocb = [Buf() for _ in range(2)]
    lc = [A.alloc([512], F32) for _ in range(2)]
    lcb = [Buf() for _ in range(2)]
    od = [A.alloc([512], F32) for _ in range(2)]
    odb = [Buf() for _ in range(2)]
    sqo = [A.alloc([512], BF16) for _ in range(2)]
    sqob = [Buf() for _ in range(2)]
    rso = A.alloc([512], F32)
    rsob = Buf()
    ost = [A.alloc([512], BF16) for _ in range(2)]
    ostb = [Buf() for _ in range(2)]
    ossem = [E.p.new_dma_sem("os%d" % i) for i in range(2)]
    osig = []
    ptk = 0
    nout = 0
    for hp in range(2):
        qb_ = [Buf() for _ in range(8)]
        kb_ = [Buf() for _ in range(9)]
        vb_ = [Buf() for _ in range(NKB)]
        p.dma("pool", W.rearrange("p a b c -> p (a b c)").rearrange("p (a m) -> p a m", a=6),
              wqkv_dram[hp].rearrange("p (a m) -> p a m", a=6), wsem,
              writes=[Wb])
        tl = [(0, 16, True)] + [(16 + 512 * i, 512, False) for i in range(8)]
        for oi_, ti in enumerate((0, 1, 2, 5, 6, 3, 4, 7, 8)):
            s0, n, is_meta = tl[ti]
            hb = oi_ % 2
            src_ap, src_bufs = hn_src(ti)
            p.dma("sp", hnt[hb][:, :, :n], src_ap, hsem[hb], reads=src_bufs, writes=[hntb[hb]])
            for hh in range(2):
                for kind in ((1,) if is_meta else (0, 1)):
                    bank = (2 * hh + kind) % 4
                    fns = []
                    for c in range(NCH):
                        fns.append(lambda e, c=c, kind=kind, hh=hh, hb=hb, n=n, bank=bank: e.matmul(
                            E.ps[bank][:, :n], lhsT=W[:, kind, c, hh * 128:hh * 128 + 128],
                            rhs=hnt[hb][:, c, :n], start=(c == 0), stop=(c == NCH - 1)))
                    p.group("pe", fns, reads=[Wb, hntb[hb]], writes=[E.psb[bank]])
                    if kind == 0:
                        qi = ti - 1
                        p.op("act", lambda e, hh=hh, qi=qi, bank=bank: e.activation(
                            out=qT[:, hh, qi * 512:qi * 512 + 512], in_=E.ps[bank][:, :512],
                            func=AF.Copy), reads=[E.psb[bank]], writes=[qb_[qi]])
                    else:
                        p.op("dve", lambda e, hh=hh, s0=s0, n=n, bank=bank: e.tensor_copy(
                            out=kT[:, hh, s0:s0 + n], in_=E.ps[bank][:, :n]),
                            reads=[E.psb[bank]], writes=[kb_[ti]])
            nblk = 1 if is_meta else 4
            for bi in range(nblk):
                nk = 16 if is_meta else 128
                blk = 0 if is_meta else 1 + (ti - 1) * 4 + bi
                bank = 4 + (bi % 2)
                fns = []
                for c in range(NCH):
                    fns.append(lambda e, c=c, hb=hb, bi=bi, nk=nk, bank=bank: e.matmul(
                        E.ps[bank][:nk, :256], lhsT=hnt[hb][:, c, bi * 128:bi * 128 + nk],
                        rhs=W[:, 2, c, :], start=(c == 0), stop=(c == NCH - 1)))
                p.group("pe", fns, reads=[Wb, hntb[hb]], writes=[E.psb[bank]])
                eng = "act" if bi % 2 == 0 else "dve"
                if eng == "act":
                    p.op("act", lambda e, blk=blk, nk=nk, bank=bank: e.activation(
                        out=V[:nk, blk, :], in_=E.ps[bank][:nk, :256], func=AF.Copy),
                        reads=[E.psb[bank]], writes=[vb_[blk]])
                else:
                    p.op("dve", lambda e, blk=blk, nk=nk, bank=bank: e.tensor_copy(
                        out=V[:nk, blk, :], in_=E.ps[bank][:nk, :256]),
                        reads=[E.psb[bank]], writes=[vb_[blk]])
        jobs = []
        for qb in range(8):
            for hh in range(2):
                blocks = [(0, 16, 0)] + [(1 + kb, 128, 0) for kb in range(4 * qb)] + \
                         [(1 + 4 * qb + i, 128, 128 * i) for i in range(4)]
                nb = len(blocks)
                for bi, (blk, nk, c0) in enumerate(blocks):
                    jobs.append((hh, qb, bi, nb, blk, nk, c0))

        def emit_qk(j):
            hh, qb, bi, nb, blk, nk, c0 = jobs[j]
            sp = (j % 2) * 2
            q0 = qb * 512
            ks0 = 0 if blk == 0 else 16 + (blk - 1) * 128
            kbuf = kb_[0] if blk == 0 else kb_[1 + (blk - 1) // 4]
            fns = []
            for c in range(2):
                fns.append(lambda e, c=c, sp=sp, nk=nk, ks0=ks0, c0=c0, hh=hh, q0=q0: e.matmul(
                    E.ps[sp + c][:nk, c0:512], lhsT=kT[c * 64:c * 64 + 64, hh, ks0:ks0 + nk],
                    rhs=qT[c * 64:c * 64 + 64, hh, q0 + c0:q0 + 512], start=True, stop=True))
            p.group("pe", fns, reads=[kbuf, qb_[qb]], writes=[E.psb[sp], E.psb[sp + 1]])

        def emit_exp(j):
            nonlocal ptk
            hh, qb, bi, nb, blk, nk, c0 = jobs[j]
            sp = (j % 2) * 2
            pts = []
            for c in range(2):
                pi = ptk % NPT
                ptk += 1
                pts.append(pi)
                p.op("act", lambda e, c=c, sp=sp, nk=nk, c0=c0, pi=pi: e.activation(
                    out=PT[pi][:nk, c0:512], in_=E.ps[sp + c][:nk, c0:512], func=AF.Exp,
                    scale=0.125), reads=[E.psb[sp + c]], writes=[PTb[pi]])
                if blk != 0 and blk - 1 >= 4 * qb:
                    p.op("pool", lambda e, pi=pi, c0=c0: e.tensor_tensor(
                        out=PT[pi][:, c0:c0 + 128], in0=PT[pi][:, c0:c0 + 128], in1=tri,
                        op=ALU.mult), reads=[PTb[pi], trib], writes=[PTb[pi]])
            return pts

        def emit_pv(j, pts):
            hh, qb, bi, nb, blk, nk, c0 = jobs[j]
            for c in range(2):
                pi = pts[c]
                fns = [lambda e, c=c, pi=pi: e.matmul(
                           E.ps[4 + c][:, c0:512], lhsT=V[:nk, blk, hh * 128:hh * 128 + 128],
                           rhs=PT[pi][:nk, c0:512], start=(bi == 0), stop=(bi == nb - 1)),
                       lambda e, c=c, pi=pi: e.matmul(
                           E.ps[6 + c][:, c0:512], lhsT=E.ones[:nk, :],
                           rhs=PT[pi][:nk, c0:512], start=(bi == 0), stop=(bi == nb - 1))]
                p.group("pe", fns, reads=[vb_[blk], PTb[pi], E.onesb],
                        writes=[E.psb[4 + c], E.psb[6 + c]])

        def emit_epilogue(hh, qb):
            nonlocal nout
            k = nout % 2
            nout += 1
            for c in range(2):
                p.op("dve", lambda e, c=c: e.tensor_copy(out=oc[c], in_=E.ps[4 + c][:, :]),
                     reads=[E.psb[4 + c]], writes=[ocb[c]])
                p.op("dve", lambda e, c=c: e.tensor_copy(out=lc[c], in_=E.ps[6 + c][:, :]),
                     reads=[E.psb[6 + c]], writes=[lcb[c]])
            for c in range(2):
                p.op("dve", lambda e, c=c: e.reciprocal(out=lc[c], in_=lc[c]),
                     reads=[lcb[c]], writes=[lcb[c]])
            p.op("dve", lambda e: e.tensor_tensor(out=oc[0], in0=oc[0], in1=lc[0], op=ALU.mult),
                 reads=[ocb[0], lcb[0]], writes=[ocb[0]])
            p.op("dve", lambda e: e.tensor_tensor(out=oc[1], in0=oc[1], in1=lc[1], op=ALU.mult),
                 reads=[ocb[1], lcb[1]], writes=[ocb[1]])
            p.op("dve", lambda e, k=k: e.scalar_tensor_tensor(
                out=od[k], in0=oc[1], scalar=neglam, in1=oc[0], op0=ALU.mult, op1=ALU.add),
                reads=[ocb[0], ocb[1], lamb], writes=[odb[k]])
            p.op("dve", lambda e, k=k: e.tensor_tensor(out=sqo[k], in0=od[k], in1=od[k], op=ALU.mult),
                 reads=[odb[k]], writes=[sqob[k]])

            def tail(bank):
                emit_rstd(E, [sqo[k]], [sqob[k]], 512, 128, False, rso, rsob, bank)
                p.op("dve", lambda e: e.scalar_tensor_tensor(
                    out=ost[k], in0=od[k], scalar=gsub, in1=rso, op0=ALU.mult, op1=ALU.mult),
                    reads=[odb[k], rsob, lamb], writes=[ostb[k]])
                dst_ap, dst_bufs = o_dst(hp * 2 + hh, qb)
                osig.append(p.dma("sp", dst_ap, ost[k], ossem[k],
                                  reads=[ostb[k]], writes=dst_bufs))
                if after_tail is not None:
                    after_tail(hp * 2 + hh, qb)
            return tail

        pending = None
        emit_qk(0)
        for j in range(len(jobs)):
            pts = emit_exp(j)
            if j + 1 < len(jobs):
                emit_qk(j + 1)
            emit_pv(j, pts)
            hh, qb, bi, nb = jobs[j][:4]
            if bi == nb - 1:
                newp = emit_epilogue(hh, qb)
                if pending is not None:
                    pending((j % 2) * 2)
                pending = newp
        if pending is not None:
            pending(0)
    p.fence()
    A.release(m0)
    return osig


def emit_wo_residual(E, tiles_main, o_load, wo_dram):
    p = E.p
    A = E.A
    hT = E.hT
    m0 = A.mark()
    Wo = A.alloc([NCH, D], BF16)
    Wob = Buf("wo")
    wsem = E.p.new_dma_sem("wo")
    p.dma("pool", Wo.rearrange("p a b -> p (a b)").rearrange("p (a m) -> p a m", a=8),
          wo_dram.rearrange("p (a m) -> p a m", a=8), wsem, writes=[Wob])
    ot = [A.alloc([NCH, 512], BF16) for _ in range(2)]
    otb = [Buf() for _ in range(2)]
    osem = [E.p.new_dma_sem("ot%d" % i) for i in range(2)]
    M = [A.alloc([NCH, 512], F32) for _ in range(2)]
    Mb = [[Buf() for _ in range(NCH)] for _ in range(2)]
    sqm = [A.alloc([NCH, 512], BF16) for _ in range(2)]
    sqmb = [[Buf() for _ in range(NCH)] for _ in range(2)]
    rs2 = [A.alloc([512], F32) for _ in range(2)]
    rs2b = [Buf() for _ in range(2)]
    for i, t in enumerate(tiles_main):
        n = t.n
        k = i % 2
        m_off = t.s0 - NPRE
        o_load(i, k, m_off, n, ot[k], otb[k], osem[k])
        for c in range(NCH):
            bank = c % 6
            fns = []
            for hc in range(NCH):
                fns.append(lambda e, hc=hc, c=c, k=k, n=n, bank=bank: e.matmul(
                    E.ps[bank][:, :n], lhsT=Wo[:, hc, c * 128:c * 128 + 128],
                    rhs=ot[k][:, hc, :n], start=(hc == 0), stop=(hc == NCH - 1)))
            p.group("pe", fns, reads=[Wob, otb[k]], writes=[E.psb[bank]])
            p.op("act", lambda e, c=c, k=k, n=n, bank=bank: e.activation(
                out=sqm[k][:, c, :n], in_=E.ps[bank][:, :n], func=AF.Square),
                reads=[E.psb[bank]], writes=[sqmb[k][c]])
            p.op("dve", lambda e, c=c, k=k, n=n, bank=bank: e.tensor_scalar(
                out=M[k][:, c, :n], in0=E.ps[bank][:, :n], scalar1=vcol(E, ("mpost", 1), c),
                scalar2=None, op0=ALU.mult),
                reads=[E.psb[bank], E.vecsb], writes=[Mb[k][c]])
        emit_rstd(E, [sqm[k][:, c, :n] for c in range(NCH)], sqmb[k], n, D, False,
                  rs2[k][:, :n], rs2b[k], 6 + (i % 2))
        for c in range(NCH):
            ma = M[k][:, c, :n]
            p.op("dve", lambda e, ma=ma, k=k, n=n: e.tensor_tensor(
                out=ma, in0=ma, in1=rs2[k][:, :n], op=ALU.mult),
                reads=[Mb[k][c], rs2b[k]], writes=[Mb[k][c]])
            p.op("dve", lambda e, ma=ma, c=c, t=t, n=n: e.tensor_tensor(
                out=hT[:, c, t.s0:t.s0 + n], in0=hT[:, c, t.s0:t.s0 + n], in1=ma, op=ALU.add),
                reads=[Mb[k][c], t.buf], writes=[t.buf])
    p.fence()
    A.release(m0)


def emit_wo_residual_sel(E, tiles_main, load_ab, wo_dram, add_eng="pool"):
    p = E.p
    A = E.A
    hT = E.hT
    m0 = A.mark()
    Wa = A.alloc([NCH, D], BF16)
    Wb = A.alloc([NCH, D], BF16)
    Wab = Buf("woa")
    Wbb = Buf("wob")
    wsem = E.p.new_dma_sem("wo")
    p.dma("pool", Wb.rearrange("p a b -> p (a b)").rearrange("p (a m) -> p a m", a=8),
          wo_dram.rearrange("p (a m) -> p a m", a=8), wsem, writes=[Wbb])
    p.op("dve", lambda e: e.tensor_scalar(out=Wa, in0=Wb, scalar1=vcol(E, "selA"), scalar2=None,
                                          op0=ALU.mult), reads=[Wbb, E.vecsb], writes=[Wab])
    p.op("dve", lambda e: e.tensor_scalar(out=Wb, in0=Wb, scalar1=vcol(E, "selB"), scalar2=None,
                                          op0=ALU.mult), reads=[Wbb, Wab, E.vecsb], writes=[Wbb])
    xa = [A.alloc([NCH, 512], BF16) for _ in range(2)]
    xab = [Buf() for _ in range(2)]
    xasem = [p.new_dma_sem("xa%d" % i) for i in range(2)]
    xb = [A.alloc([NCH, 512], BF16) for _ in range(2)]
    xbb = [Buf() for _ in range(2)]
    xbsem = [p.new_dma_sem("xb%d" % i) for i in range(2)]
    M = [A.alloc([NCH, 512], F32) for _ in range(2)]
    Mb = [[Buf() for _ in range(NCH)] for _ in range(2)]
    sqm = [A.alloc([NCH, 512], BF16) for _ in range(2)]
    sqmb = [[Buf() for _ in range(NCH)] for _ in range(2)]
    rs2 = [A.alloc([512], F32) for _ in range(2)]
    rs2b = [Buf() for _ in range(2)]
    for i, t in enumerate(tiles_main):
        n = t.n
        k = i % 2
        load_ab(i, xa[k], xab[k], xasem[k], xb[k], xbb[k], xbsem[k])
        for c in range(NCH):
            bank = c % 6
            fns = []
            for hc in range(NCH):
                fns.append(lambda e, hc=hc, c=c, k=k, n=n, bank=bank: e.matmul(
                    E.ps[bank][:, :n], lhsT=Wa[:, hc, c * 128:c * 128 + 128],
                    rhs=xa[k][:, hc, :n], start=(hc == 0), stop=False))
            for hc in range(NCH):
                fns.append(lambda e, hc=hc, c=c, k=k, n=n, bank=bank: e.matmul(
                    E.ps[bank][:, :n], lhsT=Wb[:, hc, c * 128:c * 128 + 128],
                    rhs=xb[k][:, hc, :n], start=False, stop=(hc == NCH - 1)))
            p.group("pe", fns, reads=[Wab, Wbb, xab[k], xbb[k]], writes=[E.psb[bank]])
            p.op("act", lambda e, c=c, k=k, n=n, bank=bank: e.activation(
                out=sqm[k][:, c, :n], in_=E.ps[bank][:, :n], func=AF.Square),
                reads=[E.psb[bank]], writes=[sqmb[k][c]])
            p.op("dve", lambda e, c=c, k=k, n=n, bank=bank: e.tensor_scalar(
                out=M[k][:, c, :n], in0=E.ps[bank][:, :n], scalar1=vcol(E, ("mpost", 1), c),
                scalar2=None, op0=ALU.mult),
                reads=[E.psb[bank], E.vecsb], writes=[Mb[k][c]])
        emit_rstd(E, [sqm[k][:, c, :n] for c in range(NCH)], sqmb[k], n, D, False,
                  rs2[k][:, :n], rs2b[k], 6 + (i % 2))
        for c in range(NCH):
            ma = M[k][:, c, :n]
            p.op("dve", lambda e, ma=ma, k=k, n=n: e.tensor_tensor(
                out=ma, in0=ma, in1=rs2[k][:, :n], op=ALU.mult),
                reads=[Mb[k][c], rs2b[k]], writes=[Mb[k][c]])
            p.op(add_eng, lambda e, ma=ma, c=c, t=t, n=n: e.tensor_tensor(
                out=hT[:, c, t.s0:t.s0 + n], in0=hT[:, c, t.s0:t.s0 + n], in1=ma, op=ALU.add),
                reads=[Mb[k][c], t.buf], writes=[t.buf])
    p.fence()
    A.release(m0)

def _tiles():
    tm = [Tile_(NPRE + 512 * i, 512, Buf("h%d" % i)) for i in range(4)]
    tp = Tile_(0, NPRE, Buf("hp"))
    return tm, tp


def build_A(stop=None):
    nc = bass.Bass("TRN2", target_bir_lowering=False)
    xT = nc.dram_tensor("xT", [128, NCH, NSLOT], F32, kind="ExternalInput").ap()
    vecs = nc.dram_tensor("vecs", [128, NV], F32, kind="ExternalInput").ap()
    poolw = nc.dram_tensor("poolw", [128, 4 * 2 * 256], F32, kind="ExternalInput").ap()
    w13 = [nc.dram_tensor("w13_%d" % k, [NF, 128, 2048], F32, kind="ExternalInput").ap() for k in range(3)]
    w2 = [nc.dram_tensor("w2_%d" % k, [NCH, 128, DFF], F32, kind="ExternalInput").ap() for k in range(3)]
    h_out = nc.dram_tensor("h_out", [128, NCH, NMAIN], F32, kind="ExternalOutput").ap()
    hn_out = nc.dram_tensor("hn_out", [128, NCH, NSLOT], BF16, kind="ExternalOutput").ap()
    if stop is not None:
        h_dbg = nc.dram_tensor("h_dbg", [128, NCH, NSLOT], F32, kind="ExternalOutput").ap()

    def dbg_out(E, tall):
        sg = [E.p.dma("sp", h_dbg[:, :, t.s0:t.s0 + t.n], E.hT[:, :, t.s0:t.s0 + t.n],
                      E.stsem, reads=[t.buf]) for t in tall]
        E.p.wait_all("sp", sg)
        E.p.flush()

    with ExitStack() as st:
        block = st.enter_context(nc.Block())
        E = make_env(nc, st, block)
        p = E.p
        A = E.A
        setup_consts(E)
        load_vecs(E, vecs)
        E.hT = A.alloc([NCH, NSLOT], F32)
        tm, tp = _tiles()
        tall = tm + [tp]
        for t in tall:
            p.dma("sp", E.hT[:, :, t.s0:t.s0 + t.n], xT[:, :, t.s0:t.s0 + t.n], p.fresh(), writes=[t.buf])
        w13r = WRing(E, "w13r", 2, [2, NCH, 128], 2)
        w2r = WRing(E, "w2r", 2, [NF, 128], 2)
        for k in range(3):
            ffn_plan(w13r, w2r, w13[k], w2[k], 2)
        w13r.start()
        w2r.start()
        S0 = [tm[0], tm[1], tp]
        S1 = [tm[2], tm[3]]
        mF = A.mark()
        if stop == 1:
            dbg_out(E, tall); return nc
        Fn = alloc_ffn(E)
        for si, S in enumerate((S0, S1)):
            emit_ffn(E, Fn, S, w13r, w2r, ("fpre", 0, 0), ("fpost", 0, 0))
            if stop == 2 + si:
                dbg_out(E, tall); return nc
        p.fence()
        A.release(mF)
        emit_pool_mixer(E, tall, poolw)
        if stop == 4:
            dbg_out(E, tall); return nc
        Fn = alloc_ffn(E)
        for S in (S0, S1):
            emit_ffn(E, Fn, S, w13r, w2r, ("fpre", 0, 1), ("fpost", 0, 1))
        for S in (S0, S1):
            emit_ffn(E, Fn, S, w13r, w2r, ("fpre", 1, 0), ("fpost", 1, 0))
        p.fence()
        A.release(mF)
        hn = A.alloc([NCH, NSLOT], BF16)
        hnb = [Buf() for _ in tall]
        emit_mixnorm_to_bf16(E, tall, lambda c, s0, n: hn[:, c, s0:s0 + n], hnb, ("mpre", 1))
        sigs = []
        for i, t in enumerate(tall):
            sigs.append(p.dma("sp", hn_out[:, :, t.s0:t.s0 + t.n], hn[:, :, t.s0:t.s0 + t.n],
                              E.stsem, reads=[hnb[i]]))
        for t in tm:
            sigs.append(p.dma("sp", h_out[:, :, t.s0 - NPRE:t.s0 - NPRE + t.n],
                              E.hT[:, :, t.s0:t.s0 + t.n], E.stsem, reads=[t.buf]))
        p.wait_all("sp", sigs)
        p.flush()
    return nc


def build_B():
    nc = bass.Bass("TRN2", target_bir_lowering=False)
    hn = nc.dram_tensor("hn", [128, NCH, LSEQ], BF16, kind="ExternalInput").ap()
    vecs = nc.dram_tensor("vecs", [128, NV], F32, kind="ExternalInput").ap()
    wqkv = nc.dram_tensor("wqkv", [2, 128, 3 * NCH * 256], F32, kind="ExternalInput").ap()
    o_out = nc.dram_tensor("o_out", [128, 4, 4096], BF16, kind="ExternalOutput").ap()
    with ExitStack() as st:
        block = st.enter_context(nc.Block())
        E = make_env(nc, st, block)
        setup_consts(E)
        load_vecs(E, vecs)
        tl = [(0, 16)] + [(16 + 512 * i, 512) for i in range(8)]
        sigs = emit_attention(E, lambda ti: (hn[:, :, tl[ti][0]:tl[ti][0] + tl[ti][1]], []),
                              wqkv, lambda lh, qb: (o_out[:, lh, qb * 512:qb * 512 + 512], []))
        E.p.wait_all("sp", sigs)
        E.p.flush()
    return nc


def build_C():
    nc = bass.Bass("TRN2", target_bir_lowering=False)
    hT_in = nc.dram_tensor("hT_in", [128, NCH, NMAIN], F32, kind="ExternalInput").ap()
    oT = nc.dram_tensor("oT", [128, NCH, NMAIN], BF16, kind="ExternalInput").ap()
    vecs = nc.dram_tensor("vecs", [128, NV], F32, kind="ExternalInput").ap()
    wo = nc.dram_tensor("wo", [128, NCH * D], F32, kind="ExternalInput").ap()
    w13 = nc.dram_tensor("w13_3", [NF, 128, 2048], F32, kind="ExternalInput").ap()
    w2 = nc.dram_tensor("w2_3", [NCH, 128, DFF], F32, kind="ExternalInput").ap()
    y = nc.dram_tensor("y", [128, NCH, NMAIN], F32, kind="ExternalOutput").ap()
    with ExitStack() as st:
        block = st.enter_context(nc.Block())
        E = make_env(nc, st, block)
        p = E.p
        A = E.A
        setup_consts(E)
        load_vecs(E, vecs)
        E.hT = A.alloc([NCH, NSLOT], F32)
        tm, tp = _tiles()
        for t in tm:
            p.dma("sp", E.hT[:, :, t.s0:t.s0 + t.n], hT_in[:, :, t.s0 - NPRE:t.s0 - NPRE + t.n],
                  p.fresh(), writes=[t.buf])
        w13r = WRing(E, "w13r", 2, [2, NCH, 128], 2)
        w2r = WRing(E, "w2r", 2, [NF, 128], 2)
        ffn_plan(w13r, w2r, w13, w2, 2)
        w13r.start()
        w2r.start()
        emit_wo_residual(E, tm, lambda i, k, m_off, n, dst, dstb, sem: p.dma(
            "sp", dst[:, :, :n], oT[:, :, m_off:m_off + n], sem, writes=[dstb]), wo)
        Fn = alloc_ffn(E)
        for S in ([tm[0], tm[1]], [tm[2], tm[3]]):
            emit_ffn(E, Fn, S, w13r, w2r, ("fpre", 1, 1), ("fpost", 1, 1))
        sigs = []
        for t in tm:
            sigs.append(p.dma("sp", y[:, :, t.s0 - NPRE:t.s0 - NPRE + t.n],
                              E.hT[:, :, t.s0:t.s0 + t.n], E.stsem, reads=[t.buf]))
        p.wait_all("sp", sigs)
        p.flush()
    return nc


def _w13_layout(w1, w3):
    a = np.stack([np.asarray(w1, np.float32), np.asarray(w3, np.float32)], 0)
    a = a.reshape(2, NCH, 128, NF, 128)
    a = a.transpose(3, 2, 0, 1, 4)
    return np.ascontiguousarray(a).reshape(NF, 128, 2048)


def _w2_layout(w2):
    a = np.asarray(w2, np.float32).reshape(NF, 128, NCH, 128)
    a = a.transpose(2, 1, 0, 3)
    return np.ascontiguousarray(a).reshape(NCH, 128, DFF)


def _to_fm(a):
    T = a.shape[0]
    return np.ascontiguousarray(a.reshape(T, NCH, 128).transpose(2, 1, 0))


def _from_fm(a):
    T = a.shape[2]
    return np.ascontiguousarray(a.transpose(2, 1, 0)).reshape(T, D)


_CACHE = {}


def _get(name, fn):
    if name not in _CACHE:
        _CACHE[name] = fn()
    return _CACHE[name]


def kernel_unfused(**inp):
    inp = {k: np.asarray(v) for k, v in inp.items()}
    x = inp["x"].astype(np.float32, copy=False)
    meta = inp["meta_tokens"].astype(np.float32, copy=False)
    B = x.shape[0]
    cores = list(range(8))
    w13 = {}
    w2 = {}
    for k, (i, j) in enumerate(((0, 0), (0, 1), (1, 0), (1, 1))):
        w13[k] = _w13_layout(inp["ffn_w1"][i, j], inp["ffn_w3"][i, j])
        w2[k] = _w2_layout(inp["ffn_w2"][i, j])
    pw = np.asarray(inp["pool_w"][0], np.float32).reshape(4, 2, 128, 256).transpose(2, 0, 1, 3)
    pw = np.ascontiguousarray(pw).reshape(128, 4 * 2 * 256)
    vecs = [_build_vecs(inp, c % 2) for c in cores]
    in_maps = []
    for c in cores:
        b, half = c // 2, c % 2
        hseq = np.concatenate([meta, x[b]], axis=0)
        sl = hseq[half * NMAIN: half * NMAIN + NSLOT]
        m = {"xT": _to_fm(sl), "vecs": vecs[c], "poolw": pw}
        for k in range(3):
            m["w13_%d" % k] = w13[k]
            m["w2_%d" % k] = w2[k]
        in_maps.append(m)
    ncA = _get("A", build_A)
    rA = run_bass_kernel_spmd(ncA, in_maps, core_ids=cores)
    hA = [np.asarray(r["h_out"]) for r in rA.results]
    hnA = [np.asarray(r["hn_out"]) for r in rA.results]
    wq = np.asarray(inp["attn_w_qkv"][0], np.float32)
    in_maps = []
    for c in cores:
        b, hf = c // 2, c % 2
        e, o = hnA[2 * b], hnA[2 * b + 1]
        hn = np.concatenate([e, o[:, :, NPRE:]], axis=2)
        a = wq.reshape(NCH, 128, 3, 2, 2, 256)[:, :, :, hf]
        a = np.ascontiguousarray(a.transpose(3, 1, 2, 0, 4)).reshape(2, 128, 3 * NCH * 256)
        in_maps.append({"hn": np.ascontiguousarray(hn), "vecs": vecs[c], "wqkv": a})
    ncB = _get("B", build_B)
    rB = run_bass_kernel_spmd(ncB, in_maps, core_ids=cores)
    oB = [np.asarray(r["o_out"]) for r in rB.results]
    wo = np.asarray(inp["attn_w_o"][0], np.float32).reshape(NCH, 128, D).transpose(1, 0, 2)
    wo = np.ascontiguousarray(wo).reshape(128, NCH * D)
    in_maps = []
    for c in cores:
        b, half = c // 2, c % 2
        oT = np.concatenate([oB[2 * b][:, :, half * NMAIN:(half + 1) * NMAIN],
                             oB[2 * b + 1][:, :, half * NMAIN:(half + 1) * NMAIN]], axis=1)
        in_maps.append({"hT_in": hA[c], "oT": np.ascontiguousarray(oT), "vecs": vecs[c],
                        "wo": wo, "w13_3": w13[3], "w2_3": w2[3]})
    ncC = _get("C", build_C)
    rC = run_bass_kernel_spmd(ncC, in_maps, core_ids=cores)
    out = np.empty((B, 4096, D), np.float32)
    for c in cores:
        b, half = c // 2, c % 2
        out[b, half * NMAIN:(half + 1) * NMAIN] = _from_fm(np.asarray(rC.results[c]["y"]))
    return out

PAIRS = [[0, 1], [2, 3], [4, 5], [6, 7]]


def build_fused():
    nc = bass.Bass("TRN2", target_bir_lowering=False)
    xT = nc.dram_tensor("xT", [128, NCH, NSLOT], F32, kind="ExternalInput").ap()
    vecs = nc.dram_tensor("vecs", [128, NV], F32, kind="ExternalInput").ap()
    poolw = nc.dram_tensor("poolw", [128, 4 * 2 * 256], F32, kind="ExternalInput").ap()
    w13 = [nc.dram_tensor("w13_%d" % k, [NF, 128, 2048], F32, kind="ExternalInput").ap() for k in range(4)]
    w2 = [nc.dram_tensor("w2_%d" % k, [NCH, 128, DFF], F32, kind="ExternalInput").ap() for k in range(4)]
    wqkv = nc.dram_tensor("wqkv", [2, 128, 3 * NCH * 256], F32, kind="ExternalInput").ap()
    wo = nc.dram_tensor("wo", [128, NCH * D], F32, kind="ExternalInput").ap()
    y = nc.dram_tensor("y", [128, NCH, NMAIN], F32, kind="ExternalOutput").ap()
    xin = [nc.dram_tensor("xin%d" % i, [D, n], BF16).ap() for i, n in enumerate((512, 512, 512, 512, NPRE))]
    xout = [nc.dram_tensor("xout%d" % i, [2 * D, n], BF16).ap() for i, n in enumerate((512, 512, 512, 512, NPRE))]
    oin = [nc.dram_tensor("oin%d" % i, [512, 512], BF16).ap() for i in range(8)]
    oout = [nc.dram_tensor("oout%d" % i, [D, 512], BF16).ap() for i in range(8)]
    with ExitStack() as st:
        block = st.enter_context(nc.Block())
        E = make_env(nc, st, block)
        p = E.p
        A = E.A
        setup_consts(E)
        load_vecs(E, vecs)
        E.hT = A.alloc([NCH, NSLOT], F32)
        tm, tp = _tiles()
        tall = tm + [tp]
        for t in tall:
            p.dma("sp", E.hT[:, :, t.s0:t.s0 + t.n], xT[:, :, t.s0:t.s0 + t.n], p.fresh(), writes=[t.buf])
        w13r = WRing(E, "w13r", 2, [2, NCH, 128], 2)
        w2r = WRing(E, "w2r", 2, [NF, 128], 2)
        for k in range(4):
            ffn_plan(w13r, w2r, w13[k], w2[k], 2)
        w13r.start()
        w2r.start()
        S0 = [tm[0], tm[1], tp]
        S1 = [tm[2], tm[3]]
        mF = A.mark()
        Fn = alloc_ffn(E)
        emit_ffn_chain(E, Fn, [(S, ("fpre", 0, 0), ("fpost", 0, 0)) for S in (S0, S1)], w13r, w2r)
        p.fence()
        A.release(mF)
        emit_pool_mixer(E, tall, poolw)
        Fn = alloc_ffn(E)
        xoutb = [Buf("xout%d" % i) for i in range(5)]
        allF = Fn.allF
        Fbf = Fn.Fbf
        stage = [Fbf[:, j * 4096:(j + 1) * 4096].rearrange("p (c t) -> p c t", c=NCH) for j in range(3)]
        sqx = Fbf[:, 12288:16384].rearrange("p (c t) -> p c t", c=NCH)
        piece = {id(tm[0]): 0, id(tm[1]): 1, id(tm[2]): 2, id(tm[3]): 3, id(tp): 4}

        def mix_exchange(S):
            for i, t in enumerate(S):
                n = t.n
                st_ = stage[i]
                p.op("act", lambda e, t=t, n=n: e.activation(
                    out=sqx[:, :, :n], in_=E.hT[:, :, t.s0:t.s0 + n], func=AF.Square),
                    reads=[t.buf], writes=allF)
                emit_rstd(E, [sqx[:, c, :n] for c in range(NCH)], allF, n, D, False,
                          Fn.rs[i][:, :n], Fn.rsb[i], 6 + (i % 2))
                for c in range(NCH):
                    p.op("dve", lambda e, c=c, t=t, n=n, i=i, st_=st_: e.scalar_tensor_tensor(
                        out=st_[:, c, :n], in0=E.hT[:, c, t.s0:t.s0 + n],
                        scalar=vcol(E, ("mpre", 1), c), in1=Fn.rs[i][:, :n],
                        op0=ALU.mult, op1=ALU.mult),
                        reads=[t.buf, Fn.rsb[i], E.vecsb], writes=allF)
                pi = piece[id(t)]
                xb_ = Buf()
                p.dma("sp", xin[pi].rearrange("(c q) t -> q c t", q=128), st_[:, :, :n],
                      p.fresh(), reads=allF, writes=[xb_])
                p.allgather(xin[pi], xout[pi], PAIRS, reads=[xb_], writes=[xoutb[pi]])

        emit_ffn_chain(E, Fn, [(S0, ("fpre", 0, 1), ("fpost", 0, 1)),
                               (S1, ("fpre", 0, 1), ("fpost", 0, 1)),
                               (S0, ("fpre", 1, 0), ("fpost", 1, 0), lambda: mix_exchange(S0)),
                               (S1, ("fpre", 1, 0), ("fpost", 1, 0), lambda: mix_exchange(S1))],
                       w13r, w2r)
        p.fence(skip_collectives=True)
        A.release(mF)
        xo_v = [x_.rearrange("(r c q) t -> r q c t", r=2, q=128) for x_ in xout]

        def hn_src(ti):
            if ti == 0:
                return xo_v[4][0], [xoutb[4]]
            i = ti - 1
            r, j = i // 4, i % 4
            return xo_v[j][r], [xoutb[j]]

        oinb = [[Buf() for _ in range(4)] for _ in range(8)]
        oin_v = [o.rearrange("(h q) t -> q h t", q=128) for o in oin]

        def o_dst(lh, qb):
            return oin_v[qb][:, lh, :], [oinb[qb][lh]]

        ooutb = [Buf("oout%d" % i) for i in range(8)]
        st2 = {"cnt": [0] * 8, "ready": [], "issued": set()}

        def issue_ready():
            for qb in st2["ready"]:
                if qb not in st2["issued"]:
                    p.allgather(oin[qb], oout[qb], PAIRS, reads=oinb[qb], writes=[ooutb[qb]])
                    st2["issued"].add(qb)

        def after_tail(lh, qb):
            issue_ready()
            st2["cnt"][qb] += 1
            if st2["cnt"][qb] == 4:
                st2["ready"].append(qb)

        emit_attention(E, hn_src, wqkv, o_dst, after_tail)
        issue_ready()
        oo_v = [o.rearrange("(c q) t -> q c t", q=128) for o in oout]
        def load_ab(i, xa_, xab_, xas_, xb_, xbb_, xbs_):
            p.dma("sp", xa_, oo_v[i], xas_, reads=[ooutb[i]], writes=[xab_])
            p.dma("sp", xb_, oo_v[4 + i], xbs_, reads=[ooutb[4 + i]], writes=[xbb_])

        emit_wo_residual_sel(E, tm, load_ab, wo)
        A.release(mF)
        Fn = alloc_ffn(E)
        emit_ffn_chain(E, Fn, [(S, ("fpre", 1, 1), ("fpost", 1, 1))
                               for S in ([tm[0], tm[1]], [tm[2], tm[3]])], w13r, w2r)
        sigs = []
        for t in tm:
            sigs.append(p.dma("sp", y[:, :, t.s0 - NPRE:t.s0 - NPRE + t.n],
                              E.hT[:, :, t.s0:t.s0 + t.n], E.stsem, reads=[t.buf]))
        p.wait_all("sp", sigs)
        p.flush()
    return nc


def kernel(**inp):
    inp = {k: np.asarray(v) for k, v in inp.items()}
    x = inp["x"].astype(np.float32, copy=False)
    meta = inp["meta_tokens"].astype(np.float32, copy=False)
    B = x.shape[0]
    cores = list(range(8))
    w13 = {}
    w2 = {}
    for k, (i, j) in enumerate(((0, 0), (0, 1), (1, 0), (1, 1))):
        w13[k] = _w13_layout(inp["ffn_w1"][i, j], inp["ffn_w3"][i, j])
        w2[k] = _w2_layout(inp["ffn_w2"][i, j])
    pw = np.asarray(inp["pool_w"][0], np.float32).reshape(4, 2, 128, 256).transpose(2, 0, 1, 3)
    pw = np.ascontiguousarray(pw).reshape(128, 4 * 2 * 256)
    wq = np.asarray(inp["attn_w_qkv"][0], np.float32)
    wo = np.asarray(inp["attn_w_o"][0], np.float32).reshape(NCH, 128, D).transpose(1, 0, 2)
    wo = np.ascontiguousarray(wo).reshape(128, NCH * D)
    in_maps = []
    for c in cores:
        b, half = c // 2, c % 2
        hseq = np.concatenate([meta, x[b]], axis=0)
        sl = hseq[half * NMAIN: half * NMAIN + NSLOT]
        a = wq.reshape(NCH, 128, 3, 2, 2, 256)[:, :, :, half]
        a = np.ascontiguousarray(a.transpose(3, 1, 2, 0, 4)).reshape(2, 128, 3 * NCH * 256)
        m = {"xT": _to_fm(sl), "vecs": _build_vecs(inp, half), "poolw": pw, "wqkv": a, "wo": wo}
        for k in range(4):
            m["w13_%d" % k] = w13[k]
            m["w2_%d" % k] = w2[k]
        in_maps.append(m)
    nc = _get("F", build_fused)
    r = run_bass_kernel_spmd(nc, in_maps, core_ids=cores)
    out = np.empty((B, 4096, D), np.float32)
    for c in cores:
        b, half = c // 2, c % 2
        out[b, half * NMAIN:(half + 1) * NMAIN] = _from_fm(np.asarray(r.results[c]["y"]))
    return out
```

```python
import math
from contextlib import ExitStack
import numpy as np
import ml_dtypes
import concourse.bass as bass
import concourse.mybir as mybir
from concourse.bass_utils import run_bass_kernel_spmd

F32 = mybir.dt.float32
BF16 = mybir.dt.bfloat16
ALU = mybir.AluOpType
AF = mybir.ActivationFunctionType

D = 1024
NCH = 8
DFF = 2816
NF = 22
NPRE = 16
NMAIN = 2048
NSLOT = NPRE + NMAIN
LSEQ = 16 + 4096
EPS = 1e-6
LAMBDA_INIT = 0.8 - 0.6 * math.exp(-0.3 * 1)
POOL_WINDOWS = (2, 4, 8, 16)
ENGS = ("pe", "act", "dve", "pool", "sp")
DBG = None


class Buf:
    __slots__ = ("name", "w", "r", "excl")

    def __init__(self, name="", excl=False):
        self.name = name
        self.w = None
        self.r = []
        self.excl = excl


class Sig:
    __slots__ = ("key", "val", "clock")

    def __init__(self, key, val, clock):
        self.key = key
        self.val = val
        self.clock = clock


class Prog:
    SEM_LIMIT = 30000

    def __init__(self, nc, stack, block):
        self.nc = nc
        self.stack = stack
        self.block = block
        self.q = {e: [] for e in ENGS}
        self.sems = {}
        self.epoch = {e: 0 for e in ENGS}
        self.cnt = {e: 0 for e in ENGS}
        self.known = {e: {} for e in ENGS}
        self.nsem = 0
        self.dma_cnt = {}
        self.last = {}
        self.coll_keys = set()

    def sem(self, key):
        s = self.sems.get(key)
        if s is None:
            s = self.stack.enter_context(self.nc.semaphore("s%d" % self.nsem))
            self.nsem += 1
            self.sems[key] = s
        return s

    def new_dma_sem(self, name):
        key = ("dma", name)
        self.sem(key)
        self.dma_cnt[key] = 0
        return key

    def _split(self, reads, writes):
        ex = [b for b in reads if b.excl]
        if ex:
            reads = [b for b in reads if not b.excl]
            writes = list(writes) + ex
        return reads, writes

    def _deps(self, eng, reads, writes):
        need = []
        for b in reads:
            if b.w is not None:
                need.append(b.w)
        for b in writes:
            if b.w is not None:
                need.append(b.w)
            need.extend(b.r)
        kn = self.known[eng]
        best = {}
        for s in need:
            if kn.get(s.key, 0) >= s.val:
                continue
            o = best.get(s.key)
            if o is None or o.val < s.val:
                best[s.key] = s
        waits = []
        for s in best.values():
            waits.append((s.key, s.val))
            for k, v in s.clock.items():
                if kn.get(k, 0) < v:
                    kn[k] = v
        return waits

    def _mark(self, sig, reads, writes):
        for b in reads:
            b.r.append(sig)
        for b in writes:
            b.w = sig
            b.r = []
        self.last[sig.key] = sig

    def _newsig(self, eng):
        if self.cnt[eng] >= self.SEM_LIMIT:
            self.epoch[eng] += 1
            self.cnt[eng] = 0
        self.cnt[eng] += 1
        key = (eng, self.epoch[eng])
        self.sem(key)
        clock = dict(self.known[eng])
        clock[key] = self.cnt[eng]
        return Sig(key, self.cnt[eng], clock)

    def op(self, eng, fn, reads=(), writes=()):
        reads, writes = self._split(reads, writes)
        waits = self._deps(eng, reads, writes)
        sig = self._newsig(eng)
        self._mark(sig, reads, writes)
        self.q[eng].append((waits, fn, sig.key, 1))
        return sig

    def group(self, eng, fns, reads=(), writes=()):
        reads, writes = self._split(reads, writes)
        waits = self._deps(eng, reads, writes)
        sig = self._newsig(eng)
        self._mark(sig, reads, writes)
        n = len(fns)
        for i, fn in enumerate(fns):
            self.q[eng].append((waits if i == 0 else [], fn,
                                sig.key if i == n - 1 else None, 1))
        return sig

    def custom(self, eng, fn, semkey, inc, reads=(), writes=()):
        waits = self._deps(eng, reads, writes)
        self.dma_cnt[semkey] += inc
        clock = dict(self.known[eng])
        clock[semkey] = self.dma_cnt[semkey]
        sig = Sig(semkey, self.dma_cnt[semkey], clock)
        self._mark(sig, reads, writes)
        self.q[eng].append((waits, fn, semkey, inc))
        return sig

    def dma(self, eng, out, in_, semkey, reads=(), writes=(), **kw):
        return self.custom(eng, lambda e: e.dma_start(out=out, in_=in_, **kw), semkey, 16,
                           reads, writes)

    def allgather(self, in_ap, out_ap, groups, reads=(), writes=()):
        key = self.fresh()
        self.coll_keys.add(key)
        return self.custom("pool", lambda e: e.collective_compute(
            "AllGather", ALU.bypass, replica_groups=groups, ins=[in_ap.opt()],
            outs=[out_ap.opt()]), key, 1, reads, writes)

    def wait_all(self, eng, sigs):
        kn = self.known[eng]
        best = {}
        for s in sigs:
            if kn.get(s.key, 0) < s.val:
                o = best.get(s.key)
                if o is None or o.val < s.val:
                    best[s.key] = s
        waits = []
        for s in best.values():
            waits.append((s.key, s.val))
            for k, v in s.clock.items():
                if kn.get(k, 0) < v:
                    kn[k] = v
        if waits:
            self.q[eng].append((waits, None, None, 0))

    def fresh(self):
        self.nfresh = getattr(self, "nfresh", 0) + 1
        return self.new_dma_sem("f%d" % self.nfresh)

    def fence(self, skip_collectives=False):
        sigs = [s for k, s in self.last.items()
                if not (skip_collectives and k in self.coll_keys)]
        for e in ENGS:
            self.wait_all(e, sigs)

    def flush(self):
        sems = self.sems

        def run(e, lst):
            for waits, fn, key, inc in lst:
                for (k, v) in waits:
                    e.wait_ge(sems[k], v)
                if fn is None:
                    continue
                ins = fn(e)
                if key is not None:
                    ins.then_inc(sems[key], inc)

        b = self.block
        for name, starter in (("pe", b.tensor), ("act", b.scalar), ("dve", b.vector),
                              ("pool", b.gpsimd), ("sp", b.sync)):
            lst = self.q[name]
            if lst:
                self.q[name] = []
                starter(lambda e, lst=lst: run(e, lst))


def _resh(ap, shape):
    if len(shape) == 1:
        return ap
    if len(shape) == 2:
        return ap.rearrange("p (a b) -> p a b", a=shape[0], b=shape[1])
    if len(shape) == 3:
        return ap.rearrange("p (a b c) -> p a b c", a=shape[0], b=shape[1], c=shape[2])
    raise ValueError(shape)


class Arena:
    def __init__(self, t, words):
        self.t = t
        self.W = words
        self.top = 0

    def alloc(self, shape, dtype):
        n = int(np.prod(shape))
        nbytes = n * (4 if dtype == F32 else 2)
        words = (nbytes + 31) // 32 * 8
        off = self.top
        self.top += words
        assert self.top <= self.W, ("arena overflow", self.top, self.W)
        return self.view(off, shape, dtype)

    def view(self, off, shape, dtype):
        n = int(np.prod(shape))
        if dtype == F32:
            ap = self.t[:, off:off + n]
        else:
            ap = self.t[:, off:off + (n + 1) // 2].bitcast(BF16)
            if n % 2:
                ap = ap[:, :n]
        return _resh(ap, shape)

    def mark(self):
        return self.top

    def release(self, m):
        self.top = m


def _vec_cols():
    cols = {}
    n = 0
    for i in range(2):
        for j in range(2):
            cols[("fpre", i, j)] = n; n += 8
            cols[("fpost", i, j)] = n; n += 8
    for i in range(2):
        cols[("mpre", i)] = n; n += 8
        cols[("mpost", i)] = n; n += 8
    cols["pool_b"] = n; n += 8
    cols["pool_scale"] = n; n += 8
    cols["subln"] = n; n += 1
    for k in ("lq1", "lk1", "lq2", "lk2"):
        cols[k] = n; n += 64
    cols["poolfix"] = n; n += 64
    cols["selA"] = n; n += 1
    cols["selB"] = n; n += 1
    return cols, n


VCOL, NV = _vec_cols()


def _fm(v):
    return np.ascontiguousarray(np.asarray(v, np.float32).reshape(8, 128).T)


def _build_vecs(inp, half):
    v = np.zeros((128, NV), np.float32)
    for i in range(2):
        for j in range(2):
            c = VCOL[("fpre", i, j)]; v[:, c:c + 8] = _fm(inp["ffn_norm_pre"][i, j])
            c = VCOL[("fpost", i, j)]; v[:, c:c + 8] = _fm(inp["ffn_norm_post"][i, j])
    for i in range(2):
        c = VCOL[("mpre", i)]; v[:, c:c + 8] = _fm(inp["mix_norm_pre"][i])
        c = VCOL[("mpost", i)]; v[:, c:c + 8] = _fm(inp["mix_norm_post"][i])
    c = VCOL["pool_b"]; v[:, c:c + 8] = _fm(np.asarray(inp["pool_b"][0]).reshape(-1))
    c = VCOL["pool_scale"]; v[:, c:c + 8] = _fm(inp["pool_scale"][0])
    c = VCOL["subln"]; v[:, c] = np.asarray(inp["attn_subln"][0], np.float32)
    for k, nm in (("lq1", "attn_lambda_q1"), ("lk1", "attn_lambda_k1"),
                  ("lq2", "attn_lambda_q2"), ("lk2", "attn_lambda_k2")):
        c = VCOL[k]; v[:, c:c + 64] = np.asarray(inp[nm][0], np.float32)[None, :]
    c = VCOL["poolfix"]
    for g, w in enumerate(POOL_WINDOWS):
        for t in range(16):
            cnt = min(t + 1, w) if half == 0 else w
            v[:, c + g * 16 + t] = 1.0 / cnt
    v[:, VCOL["selA"]] = 1.0 if half == 0 else 0.0
    v[:, VCOL["selB"]] = 0.0 if half == 0 else 1.0
    return v


class Env:
    pass


def make_env(nc, st, block, need_w=True):
    E = Env()
    E.nc = nc
    E.p = Prog(nc, st, block)
    words = 52800
    E.arena_t = st.enter_context(nc.sbuf_tensor("arena", [128, words], F32))
    E.A = Arena(E.arena_t, words)
    E.ps = [st.enter_context(nc.psum_tensor("ps%d" % i, [128, 512], F32)) for i in range(8)]
    E.psb = [Buf("ps%d" % i, excl=True) for i in range(8)]
    A = E.A
    E.ones = A.alloc([128], BF16)
    E.onesb = Buf("ones")
    E.vecs = A.alloc([NV], F32)
    E.vecsb = Buf("vecs")
    E.stsem = E.p.new_dma_sem("st")
    E.p.op("dve", lambda e: e.memset(E.ones, 1.0), writes=[E.onesb])
    return E


def load_vecs(E, vecs_dram):
    E.p.dma("sp", E.vecs, vecs_dram, E.p.fresh(), writes=[E.vecsb])


def vcol(E, key, c=0, n=1):
    o = VCOL[key] + c
    return E.vecs[:, o:o + n]


class WRing:
    def __init__(self, E, name, nslots, shape, split):
        self.E = E
        self.n = nslots
        self.split = split
        self.tiles = [E.A.alloc(shape, BF16) for _ in range(nslots)]
        self.bufs = [Buf("%s%d" % (name, i)) for i in range(nslots)]
        self.sems = [E.p.new_dma_sem("%s%d" % (name, i)) for i in range(nslots)]
        self.seq = []
        self.issued = 0
        self.used = 0

    def plan(self, srcs):
        self.seq.extend(srcs)

    def _issue(self):
        k = self.issued
        s = k % self.n
        src = self.seq[k]
        a = self.split
        dst = self.tiles[s]
        flat = dst
        nd = len(dst.shape)
        if nd == 3:
            flat = dst.rearrange("p a b -> p (a b)")
        elif nd == 4:
            flat = dst.rearrange("p a b c -> p (a b c)")
        self.E.p.dma("pool", flat.rearrange("p (a m) -> p a m", a=a),
                     src.rearrange("p (a m) -> p a m", a=a), self.sems[s],
                     writes=[self.bufs[s]])
        self.issued += 1

    def start(self):
        while self.issued < min(self.n, len(self.seq)):
            self._issue()

    def get(self):
        k = self.used
        assert k < self.issued, "weight ring underflow"
        s = k % self.n
        return self.tiles[s], self.bufs[s]

    def done(self):
        self.used += 1
        if self.issued < len(self.seq):
            self._issue()


class Tile_:
    def __init__(self, s0, n, buf):
        self.s0 = s0
        self.n = n
        self.buf = buf


def emit_rstd(E, sq_aps, sq_bufs, n, nfeat, post_bias, rs_ap, rs_buf, bank):
    p = E.p
    ps = E.ps[bank]
    m = len(sq_aps)
    fns = []
    for i, a in enumerate(sq_aps):
        fns.append(lambda e, a=a, i=i: e.matmul(ps[:, :n], lhsT=E.ones, rhs=a,
                                                 start=(i == 0), stop=(i == m - 1)))
    p.group("pe", fns, reads=list(sq_bufs) + [E.onesb], writes=[E.psb[bank]])
    p.op("act", lambda e: e.activation(out=rs_ap, in_=ps[:, :n], func=AF.Ln,
                                       scale=1.0 / nfeat, bias=E.epsc),
         reads=[E.psb[bank], E.constb], writes=[rs_buf])
    bias_ap = E.ln05c if post_bias else E.zeroc
    p.op("act", lambda e: e.activation(out=rs_ap, in_=rs_ap, func=AF.Exp,
                                       scale=-0.5, bias=bias_ap),
         reads=[rs_buf, E.constb], writes=[rs_buf])


def setup_consts(E):
    A = E.A
    E.cst = A.alloc([8], F32)
    E.constb = Buf("const")
    E.epsc = E.cst[:, 0:1]
    E.ln05c = E.cst[:, 1:2]
    E.zeroc = E.cst[:, 2:3]
    p = E.p
    p.op("dve", lambda e: e.memset(E.cst[:, 0:1], EPS), writes=[E.constb])
    p.op("dve", lambda e: e.memset(E.cst[:, 1:2], math.log(0.5)), reads=[E.constb], writes=[E.constb])
    p.op("dve", lambda e: e.memset(E.cst[:, 2:3], 0.0), reads=[E.constb], writes=[E.constb])


def alloc_ffn(E):
    A = E.A
    Fn = Env()
    Fn.NW = 1040
    wid = (512, 512, NPRE)
    Fn.G = A.alloc([NF, Fn.NW], BF16)
    Fn.xn = [A.alloc([NCH, Fn.NW], BF16) for _ in range(2)]
    Fn.F = A.alloc([NCH, Fn.NW], F32)
    Fn.Fbf = Fn.F.rearrange("p a b -> p (a b)").bitcast(BF16)
    Fn.sil = [A.alloc([w], F32) for w in wid]
    Fn.silb = [Buf("sil%d" % i) for i in range(3)]
    Fn.rs = [A.alloc([w], F32) for w in wid]
    Fn.rsb = [Buf("rs%d" % i) for i in range(3)]
    Fn.rq = Fn.rs
    Fn.rqb = Fn.rsb
    Fn.xnb = [[[Buf() for _ in range(3)] for _ in range(NCH)] for _ in range(2)]
    Fn.Gb = [[Buf() for _ in range(3)] for _ in range(NF)]
    Fn.Fb = [[Buf() for _ in range(3)] for _ in range(NCH)]
    Fn.allF = [b_ for row in Fn.Fb for b_ in row]
    return Fn


def _offs(tiles):
    offs = []
    o = 0
    for t in tiles:
        offs.append(o)
        o += t.n
    return offs


def _sqv(Fn, i, n):
    return Fn.Fbf[:, i * 4096:(i + 1) * 4096].rearrange("p (c t) -> p c t", c=NCH)[:, :, :n]


def ffn_pre_sq(E, Fn, tiles):
    p = E.p
    for i, t in enumerate(tiles):
        p.op("act", lambda e, t=t, i=i: e.activation(
            out=_sqv(Fn, i, t.n), in_=E.hT[:, :, t.s0:t.s0 + t.n], func=AF.Square),
            reads=[t.buf], writes=Fn.allF)


def ffn_pre_rest(E, Fn, tiles, gpre, xi):
    p = E.p
    hT = E.hT
    offs = _offs(tiles)
    xn = Fn.xn[xi]
    for i, t in enumerate(tiles):
        n = t.n
        sq = _sqv(Fn, i, n)
        emit_rstd(E, [sq[:, c, :] for c in range(NCH)], Fn.allF, n, D, False,
                  Fn.rs[i][:, :n], Fn.rsb[i], 6 + (i % 2))
        for c in range(NCH):
            p.op("dve", lambda e, c=c, i=i, t=t, n=n: e.scalar_tensor_tensor(
                out=xn[:, c, offs[i]:offs[i] + n], in0=hT[:, c, t.s0:t.s0 + n],
                scalar=vcol(E, gpre, c), in1=Fn.rs[i][:, :n], op0=ALU.mult, op1=ALU.mult),
                reads=[t.buf, Fn.rsb[i], E.vecsb], writes=[Fn.xnb[xi][c][i]])


def ffn_up(E, Fn, tiles, w13, xi, hooks=None):
    p = E.p
    nt = len(tiles)
    offs = _offs(tiles)
    xn = Fn.xn[xi]
    xn_all = [Fn.xnb[xi][c][i] for c in range(NCH) for i in range(nt)]
    for f in range(NF):
        if hooks and f in hooks:
            hooks[f]()
        wt, wb = w13.get()
        for wi in range(2):
            base = 0 if wi == 0 else 3
            fns = []
            for c in range(NCH):
                for i, t in enumerate(tiles):
                    fns.append(lambda e, c=c, i=i, t=t, wi=wi, base=base, wt=wt: e.matmul(
                        E.ps[base + i][:, :t.n], lhsT=wt[:, wi, c, :],
                        rhs=xn[:, c, offs[i]:offs[i] + t.n],
                        start=(c == 0), stop=(c == NCH - 1)))
            p.group("pe", fns, reads=[wb] + xn_all, writes=[E.psb[base + i] for i in range(nt)])
            if wi == 0:
                for i, t in enumerate(tiles):
                    p.op("act", lambda e, i=i, t=t: e.activation(
                        out=Fn.sil[i][:, :t.n], in_=E.ps[i][:, :t.n], func=AF.Silu),
                        reads=[E.psb[i]], writes=[Fn.silb[i]])
            else:
                for i, t in enumerate(tiles):
                    p.op("dve", lambda e, i=i, t=t, f=f: e.tensor_tensor(
                        out=Fn.G[:, f, offs[i]:offs[i] + t.n], in0=E.ps[3 + i][:, :t.n],
                        in1=Fn.sil[i][:, :t.n], op=ALU.mult),
                        reads=[E.psb[3 + i], Fn.silb[i]], writes=[Fn.Gb[f][i]])
        w13.done()


def ffn_down(E, Fn, tiles, w2, gpost, xi):
    p = E.p
    nt = len(tiles)
    offs = _offs(tiles)
    xn = Fn.xn[xi]
    for dc in range(NCH):
        wt, wb = w2.get()
        base = 0 if dc % 2 == 0 else 3
        fns = []
        for f in range(NF):
            for i, t in enumerate(tiles):
                fns.append(lambda e, f=f, i=i, t=t, base=base, wt=wt: e.matmul(
                    E.ps[base + i][:, :t.n], lhsT=wt[:, f, :],
                    rhs=Fn.G[:, f, offs[i]:offs[i] + t.n],
                    start=(f == 0), stop=(f == NF - 1)))
        p.group("pe", fns, reads=[wb] + [Fn.Gb[f][i] for f in range(NF) for i in range(nt)],
                writes=[E.psb[base + i] for i in range(nt)])
        w2.done()
        for i, t in enumerate(tiles):
            p.op("act", lambda e, i=i, t=t, dc=dc, base=base: e.activation(
                out=xn[:, dc, offs[i]:offs[i] + t.n], in_=E.ps[base + i][:, :t.n],
                func=AF.Square),
                reads=[E.psb[base + i]], writes=[Fn.xnb[xi][dc][i]])
            p.op("dve", lambda e, i=i, t=t, dc=dc, base=base: e.tensor_scalar(
                out=Fn.F[:, dc, offs[i]:offs[i] + t.n], in0=E.ps[base + i][:, :t.n],
                scalar1=vcol(E, gpost, dc), scalar2=None, op0=ALU.mult),
                reads=[E.psb[base + i], E.vecsb], writes=[Fn.Fb[dc][i]])


def ffn_post_stats(E, Fn, tiles, xi):
    offs = _offs(tiles)
    xn = Fn.xn[xi]
    for i, t in enumerate(tiles):
        n = t.n
        emit_rstd(E, [xn[:, c, offs[i]:offs[i] + n] for c in range(NCH)],
                  [Fn.xnb[xi][c][i] for c in range(NCH)], n, D, True,
                  Fn.rq[i][:, :n], Fn.rqb[i], 6 + (i % 2))


def ffn_post_resid(E, Fn, tiles):
    p = E.p
    hT = E.hT
    offs = _offs(tiles)
    for i, t in enumerate(tiles):
        n = t.n
        for c in range(NCH):
            fa = Fn.F[:, c, offs[i]:offs[i] + n]
            p.op("dve", lambda e, fa=fa, i=i, n=n: e.tensor_tensor(
                out=fa, in0=fa, in1=Fn.rq[i][:, :n], op=ALU.mult),
                reads=[Fn.Fb[c][i], Fn.rqb[i]], writes=[Fn.Fb[c][i]])
            p.op("dve", lambda e, fa=fa, c=c, t=t, n=n: e.tensor_tensor(
                out=hT[:, c, t.s0:t.s0 + n], in0=hT[:, c, t.s0:t.s0 + n], in1=fa, op=ALU.add),
                reads=[Fn.Fb[c][i], t.buf], writes=[t.buf])


def emit_ffn_chain(E, Fn, units, w13, w2):
    ffn_pre_sq(E, Fn, units[0][0])
    ffn_pre_rest(E, Fn, units[0][0], units[0][1], 0)
    for k, u in enumerate(units):
        tiles, gpre, gpost = u[:3]
        xi = k % 2
        hooks = None
        if k + 1 < len(units):
            nt_, ng_ = units[k + 1][0], units[k + 1][1]
            hooks = {8: (lambda nt_=nt_: ffn_pre_sq(E, Fn, nt_)),
                     11: (lambda nt_=nt_, ng_=ng_, xi=xi: ffn_pre_rest(E, Fn, nt_, ng_, 1 - xi))}
        ffn_up(E, Fn, tiles, w13, xi, hooks)
        ffn_down(E, Fn, tiles, w2, gpost, xi)
        ffn_post_stats(E, Fn, tiles, xi)
        ffn_post_resid(E, Fn, tiles)
        if len(u) > 3 and u[3] is not None:
            u[3]()


def emit_ffn(E, Fn, tiles, w13, w2, gpre, gpost):
    emit_ffn_chain(E, Fn, [(tiles, gpre, gpost)], w13, w2)


def ffn_plan(w13ring, w2ring, w13_dram, w2_dram, nsuper):
    for _ in range(nsuper):
        w13ring.plan([w13_dram[f] for f in range(NF)])
        w2ring.plan([w2_dram[dc] for dc in range(NCH)])


def emit_pool_mixer(E, tiles_all, poolw_dram):
    p = E.p
    A = E.A
    hT = E.hT
    m0 = A.mark()
    NS = NSLOT
    rs = A.alloc([NS], F32)
    rsb = [Buf() for _ in tiles_all]
    pw = A.alloc([4, 2, 256], BF16)
    pwb = Buf("pw")
    p.dma("pool", pw.rearrange("p a b c -> p (a b c)").rearrange("p (a m) -> p a m", a=2),
          poolw_dram.rearrange("p (a m) -> p a m", a=2), p.fresh(), writes=[pwb])
    bs = A.alloc([NCH], F32)
    bsb = Buf("bs")
    p.op("dve", lambda e: e.tensor_tensor(out=bs, in0=vcol(E, "pool_b", 0, 8),
                                          in1=vcol(E, "pool_scale", 0, 8), op=ALU.mult),
         reads=[E.vecsb], writes=[bsb])
    diff = A.alloc([NCH, NS], BF16)
    diffb = [Buf() for _ in range(NCH)]
    m1 = A.mark()
    sqh = A.alloc([NCH, 512], BF16)
    sqhb = Buf("sqh")
    for i, t in enumerate(tiles_all):
        n = t.n
        p.op("act", lambda e, t=t, n=n: e.activation(out=sqh[:, :, :n], in_=hT[:, :, t.s0:t.s0 + n],
                                                     func=AF.Square),
             reads=[t.buf], writes=[sqhb])
        emit_rstd(E, [sqh[:, c, :n] for c in range(NCH)], [sqhb], n, D, False,
                  rs[:, t.s0:t.s0 + n], rsb[i], 6 + (i % 2))
    hn = [A.alloc([NS], F32) for _ in range(2)]
    hnb = [Buf() for _ in range(2)]
    pa = [A.alloc([NS], F32) for _ in range(2)]
    pab = [Buf() for _ in range(2)]
    pb_ = [A.alloc([NS], F32) for _ in range(2)]
    pbb = [Buf() for _ in range(2)]
    allh = [t.buf for t in tiles_all]
    for g in range(4):
        win = POOL_WINDOWS[g]
        cs = (2 * g, 2 * g + 1)
        for k, c in enumerate(cs):
            p.op("dve", lambda e, c=c, k=k: e.scalar_tensor_tensor(
                out=hn[k], in0=hT[:, c, :], scalar=vcol(E, ("mpre", 0), c), in1=rs,
                op0=ALU.mult, op1=ALU.mult),
                reads=allh + rsb + [E.vecsb], writes=[hnb[k]])
        src = [hn[0], hn[1]]
        srcb = [hnb[0], hnb[1]]
        sh = 1
        flip = 0
        while sh < win:
            dst = [pa[k] if flip == 0 else pb_[k] for k in range(2)]
            dstb = [pab[k] if flip == 0 else pbb[k] for k in range(2)]
            for k in range(2):
                p.op("dve", lambda e, d=dst[k], s_=src[k], sh=sh: e.tensor_tensor(
                    out=d[:, sh:], in0=s_[:, sh:], in1=s_[:, :NS - sh], op=ALU.add),
                    reads=[srcb[k]], writes=[dstb[k]])
            for k in range(2):
                p.op("dve", lambda e, d=dst[k], s_=src[k], sh=sh: e.tensor_copy(
                    out=d[:, :sh], in_=s_[:, :sh]),
                    reads=[srcb[k], dstb[k]], writes=[dstb[k]])
            src, srcb = dst, dstb
            sh *= 2
            flip ^= 1
        for k, c in enumerate(cs):
            p.op("dve", lambda e, c=c, s_=src[k], k=k, win=win: e.scalar_tensor_tensor(
                out=diff[:, c, :], in0=s_, scalar=1.0 / win, in1=hn[k],
                op0=ALU.mult, op1=ALU.subtract),
                reads=[srcb[k], hnb[k]], writes=[diffb[c]])
        fx = vcol(E, "poolfix", g * 16, 16)
        for k, c in enumerate(cs):
            p.op("dve", lambda e, s_=src[k], fx=fx: e.tensor_tensor(
                out=s_[:, :16], in0=s_[:, :16], in1=fx, op=ALU.mult),
                reads=[srcb[k], diffb[c], E.vecsb], writes=[srcb[k]])
        for k, c in enumerate(cs):
            p.op("dve", lambda e, c=c, s_=src[k], k=k: e.tensor_tensor(
                out=diff[:, c, :16], in0=s_[:, :16], in1=hn[k][:, :16], op=ALU.subtract),
                reads=[srcb[k], hnb[k], diffb[c]], writes=[diffb[c]])
    p.fence()
    A.release(m1)
    M = [A.alloc([NCH, 512], F32) for _ in range(2)]
    Mb = [[Buf() for _ in range(NCH)] for _ in range(2)]
    sqm = [A.alloc([NCH, 512], BF16) for _ in range(2)]
    sqmb = [[Buf() for _ in range(NCH)] for _ in range(2)]
    rs2 = [A.alloc([512], F32) for _ in range(2)]
    rs2b = [Buf() for _ in range(2)]
    for i, t in enumerate(tiles_all):
        n = t.n
        k = i % 2
        for c in range(NCH):
            g = c // 2
            bank = c % 6
            fns = []
            for ci in range(2):
                fns.append(lambda e, g=g, ci=ci, c=c, t=t, n=n, bank=bank: e.matmul(
                    E.ps[bank][:, :n], lhsT=pw[:, g, ci, (c % 2) * 128:(c % 2) * 128 + 128],
                    rhs=diff[:, 2 * g + ci, t.s0:t.s0 + n], start=(ci == 0), stop=(ci == 1)))
            p.group("pe", fns, reads=[pwb, diffb[2 * g], diffb[2 * g + 1]], writes=[E.psb[bank]])
            p.op("act", lambda e, c=c, k=k, n=n, bank=bank: e.activation(
                out=sqm[k][:, c, :n], in_=E.ps[bank][:, :n], func=AF.Square,
                scale=vcol(E, "pool_scale", c), bias=bs[:, c:c + 1]),
                reads=[E.psb[bank], bsb, E.vecsb], writes=[sqmb[k][c]])
            p.op("dve", lambda e, c=c, k=k, n=n, bank=bank: e.tensor_scalar(
                out=M[k][:, c, :n], in0=E.ps[bank][:, :n], scalar1=vcol(E, "pool_b", c),
                scalar2=vcol(E, "pool_scale", c), op0=ALU.add, op1=ALU.mult),
                reads=[E.psb[bank], E.vecsb], writes=[Mb[k][c]])
        emit_rstd(E, [sqm[k][:, c, :n] for c in range(NCH)], sqmb[k], n, D, False,
                  rs2[k][:, :n], rs2b[k], 6 + (i % 2))
        for c in range(NCH):
            ma = M[k][:, c, :n]
            p.op("dve", lambda e, ma=ma, c=c, k=k, n=n: e.scalar_tensor_tensor(
                out=ma, in0=ma, scalar=vcol(E, ("mpost", 0), c), in1=rs2[k][:, :n],
                op0=ALU.mult, op1=ALU.mult),
                reads=[Mb[k][c], rs2b[k], E.vecsb], writes=[Mb[k][c]])
            p.op("pool", lambda e, ma=ma, c=c, t=t, n=n: e.tensor_tensor(
                out=hT[:, c, t.s0:t.s0 + n], in0=hT[:, c, t.s0:t.s0 + n], in1=ma, op=ALU.add),
                reads=[Mb[k][c], t.buf], writes=[t.buf])
    p.fence()
    A.release(m0)


def emit_mixnorm_to_bf16(E, tiles_all, out_ap_fn, out_bufs, gkey):
    p = E.p
    A = E.A
    hT = E.hT
    m0 = A.mark()
    sqh = A.alloc([NCH, 512], BF16)
    sqhb = Buf("sqh")
    rs = [A.alloc([512], F32) for _ in range(2)]
    rsb = [Buf() for _ in range(2)]
    for i, t in enumerate(tiles_all):
        n = t.n
        k = i % 2
        p.op("act", lambda e, t=t, n=n: e.activation(out=sqh[:, :, :n], in_=hT[:, :, t.s0:t.s0 + n],
                                                     func=AF.Square),
             reads=[t.buf], writes=[sqhb])
        emit_rstd(E, [sqh[:, c, :n] for c in range(NCH)], [sqhb], n, D, False,
                  rs[k][:, :n], rsb[k], 6 + (i % 2))
        for c in range(NCH):
            p.op("dve", lambda e, c=c, t=t, n=n, k=k: e.scalar_tensor_tensor(
                out=out_ap_fn(c, t.s0, n), in0=hT[:, c, t.s0:t.s0 + n],
                scalar=vcol(E, gkey, c), in1=rs[k][:, :n], op0=ALU.mult, op1=ALU.mult),
                reads=[t.buf, rsb[k], E.vecsb], writes=[out_bufs[i]])
    p.fence()
    A.release(m0)


def emit_attention(E, hn_src, wqkv_dram, o_dst, after_tail=None):
    p = E.p
    A = E.A
    m0 = A.mark()
    lam = A.alloc([8], F32)
    lamb = Buf("lam")
    tmp64 = A.alloc([64], F32)
    tmpb = Buf("tmp64")
    for j, (a, b) in enumerate((("lq1", "lk1"), ("lq2", "lk2"))):
        p.op("dve", lambda e, a=a, b=b: e.tensor_tensor(
            out=tmp64, in0=vcol(E, a, 0, 64), in1=vcol(E, b, 0, 64), op=ALU.mult),
            reads=[E.vecsb, tmpb], writes=[tmpb])
        p.op("dve", lambda e, j=j: e.reduce_sum(out=lam[:, j:j + 1], in_=tmp64,
                                                axis=mybir.AxisListType.X),
             reads=[tmpb, lamb], writes=[lamb])
    p.op("act", lambda e: e.activation(out=lam[:, 2:4], in_=lam[:, 0:2], func=AF.Exp),
         reads=[lamb], writes=[lamb])
    p.op("dve", lambda e: e.scalar_tensor_tensor(
        out=lam[:, 4:5], in0=lam[:, 3:4], scalar=-LAMBDA_INIT, in1=lam[:, 2:3],
        op0=ALU.add, op1=ALU.subtract), reads=[lamb], writes=[lamb])
    p.op("dve", lambda e: e.tensor_scalar(
        out=lam[:, 5:6], in0=vcol(E, "subln"), scalar1=1.0 - LAMBDA_INIT, scalar2=None,
        op0=ALU.mult), reads=[lamb, E.vecsb], writes=[lamb])
    neglam = lam[:, 4:5]
    gsub = lam[:, 5:6]
    tri = A.alloc([128], BF16)
    trib = Buf("tri")
    p.op("pool", lambda e: e.memset(tri, 1.0), writes=[trib])
    p.op("pool", lambda e: e.affine_select(out=tri, in_=tri, pattern=[[1, 128]],
                                           compare_op=ALU.is_ge, fill=0.0, base=0,
                                           channel_multiplier=-1),
         reads=[trib], writes=[trib])

    NKB = 33
    qT = A.alloc([2, 4096], BF16)
    kT = A.alloc([2, LSEQ], BF16)
    V = A.alloc([NKB, 256], BF16)
    W = A.alloc([3, NCH, 256], BF16)
    Wb = Buf("wqkv")
    wsem = E.p.new_dma_sem("wqkv")
    hnt = [A.alloc([NCH, 512], BF16) for _ in range(2)]
    hntb = [Buf() for _ in range(2)]
    hsem = [E.p.new_dma_sem("hn%d" % i) for i in range(2)]
    NPT = 8
    PT = [A.alloc([512], BF16) for _ in range(NPT)]
    PTb = [Buf() for _ in range(NPT)]
    oc = [A.alloc([512], F32) for _ in range(2)]
    ocb = [Buf() for _ in range(2)]
    lc = [A.alloc([512], F32) for _ in range(2)]
    lcb = [Buf() for _ in range(2)]
    od = [A.alloc([512], F32) for _ in range(2)]
    odb = [Buf() for _ in range(2)]
    sqo = [A.alloc([512], BF16) for _ in range(2)]
    sqob = [Buf() for _ in range(2)]
    rso = A.alloc([512], F32)
    rsob = Buf()
    ost = [A.alloc([512], BF16) for _ in range(2)]
    ostb = [Buf() for _ in range(2)]
    ossem = [E.p.new_dma_sem("os%d" % i) for i in range(2)]
    osig = []
    ptk = 0
    nout = 0
    for hp in range(2):
        qb_ = [Buf() for _ in range(8)]
        kb_ = [Buf() for _ in range(9)]
        vb_ = [Buf() for _ in range(NKB)]
        p.dma("pool", W.rearrange("p a b c -> p (a b c)").rearrange("p (a m) -> p a m", a=6),
              wqkv_dram[hp].rearrange("p (a m) -> p a m", a=6), wsem,
              writes=[Wb])
        tl = [(0, 16, True)] + [(16 + 512 * i, 512, False) for i in range(8)]
        for oi_, ti in enumerate((0, 1, 2, 5, 6, 3, 4, 7, 8)):
            s0, n, is_meta = tl[ti]
            hb = oi_ % 2
            src_ap, src_bufs = hn_src(ti)
            p.dma("sp", hnt[hb][:, :, :n], src_ap, hsem[hb], reads=src_bufs, writes=[hntb[hb]])
            for hh in range(2):
                for kind in ((1,) if is_meta else (0, 1)):
                    bank = (2 * hh + kind) % 4
                    fns = []
                    for c in range(NCH):
                        fns.append(lambda e, c=c, kind=kind, hh=hh, hb=hb, n=n, bank=bank: e.matmul(
                            E.ps[bank][:, :n], lhsT=W[:, kind, c, hh * 128:hh * 128 + 128],
                            rhs=hnt[hb][:, c, :n], start=(c == 0), stop=(c == NCH - 1)))
                    p.group("pe", fns, reads=[Wb, hntb[hb]], writes=[E.psb[bank]])
                    if kind == 0:
                        qi = ti - 1
                        p.op("act", lambda e, hh=hh, qi=qi, bank=bank: e.activation(
                            out=qT[:, hh, qi * 512:qi * 512 + 512], in_=E.ps[bank][:, :512],
                            func=AF.Copy), reads=[E.psb[bank]], writes=[qb_[qi]])
                    else:
                        p.op("dve", lambda e, hh=hh, s0=s0, n=n, bank=bank: e.tensor_copy(
                            out=kT[:, hh, s0:s0 + n], in_=E.ps[bank][:, :n]),
                            reads=[E.psb[bank]], writes=[kb_[ti]])
            nblk = 1 if is_meta else 4
            for bi in range(nblk):
                nk = 16 if is_meta else 128
                blk = 0 if is_meta else 1 + (ti - 1) * 4 + bi
                bank = 4 + (bi % 2)
                fns = []
                for c in range(NCH):
                    fns.append(lambda e, c=c, hb=hb, bi=bi, nk=nk, bank=bank: e.matmul(
                        E.ps[bank][:nk, :256], lhsT=hnt[hb][:, c, bi * 128:bi * 128 + nk],
                        rhs=W[:, 2, c, :], start=(c == 0), stop=(c == NCH - 1)))
                p.group("pe", fns, reads=[Wb, hntb[hb]], writes=[E.psb[bank]])
                eng = "act" if bi % 2 == 0 else "dve"
                if eng == "act":
                    p.op("act", lambda e, blk=blk, nk=nk, bank=bank: e.activation(
                        out=V[:nk, blk, :], in_=E.ps[bank][:nk, :256], func=AF.Copy),
                        reads=[E.psb[bank]], writes=[vb_[blk]])
                else:
                    p.op("dve", lambda e, blk=blk, nk=nk, bank=bank: e.tensor_copy(
                        out=V[:nk, blk, :], in_=E.ps[bank][:nk, :256]),
                        reads=[E.psb[bank]], writes=[vb_[blk]])
        jobs = []
        for qb in range(8):
            for hh in range(2):
                blocks = [(0, 16, 0)] + [(1 + kb, 128, 0) for kb in range(4 * qb)] + \
                         [(1 + 4 * qb + i, 128, 128 * i) for i in range(4)]
                nb = len(blocks)
                for bi, (blk, nk, c0) in enumerate(blocks):
                    jobs.append((hh, qb, bi, nb, blk, nk, c0))

        def emit_qk(j):
            hh, qb, bi, nb, blk, nk, c0 = jobs[j]
            sp = (j % 2) * 2
            q0 = qb * 512
            ks0 = 0 if blk == 0 else 16 + (blk - 1) * 128
            kbuf = kb_[0] if blk == 0 else kb_[1 + (blk - 1) // 4]
            fns = []
            for c in range(2):
                fns.append(lambda e, c=c, sp=sp, nk=nk, ks0=ks0, c0=c0, hh=hh, q0=q0: e.matmul(
                    E.ps[sp + c][:nk, c0:512], lhsT=kT[c * 64:c * 64 + 64, hh, ks0:ks0 + nk],
                    rhs=qT[c * 64:c * 64 + 64, hh, q0 + c0:q0 + 512], start=True, stop=True))
            p.group("pe", fns, reads=[kbuf, qb_[qb]], writes=[E.psb[sp], E.psb[sp + 1]])

        def emit_exp(j):
            nonlocal ptk
            hh, qb, bi, nb, blk, nk, c0 = jobs[j]
            sp = (j % 2) * 2
            pts = []
            for c in range(2):
                pi = ptk % NPT
                ptk += 1
                pts.append(pi)
                p.op("act", lambda e, c=c, sp=sp, nk=nk, c0=c0, pi=pi: e.activation(
                    out=PT[pi][:nk, c0:512], in_=E.ps[sp + c][:nk, c0:512], func=AF.Exp,
                    scale=0.125), reads=[E.psb[sp + c]], writes=[PTb[pi]])
                if blk != 0 and blk - 1 >= 4 * qb:
                    p.op("pool", lambda e, pi=pi, c0=c0: e.tensor_tensor(
                        out=PT[pi][:, c0:c0 + 128], in0=PT[pi][:, c0:c0 + 128], in1=tri,
                        op=ALU.mult), reads=[PTb[pi], trib], writes=[PTb[pi]])
            return pts

        def emit_pv(j, pts):
            hh, qb, bi, nb, blk, nk, c0 = jobs[j]
            for c in range(2):
                pi = pts[c]
                fns = [lambda e, c=c, pi=pi: e.matmul(
                           E.ps[4 + c][:, c0:512], lhsT=V[:nk, blk, hh * 128:hh * 128 + 128],
                           rhs=PT[pi][:nk, c0:512], start=(bi == 0), stop=(bi == nb - 1)),
                       lambda e, c=c, pi=pi: e.matmul(
                           E.ps[6 + c][:, c0:512], lhsT=E.ones[:nk, :],
                           rhs=PT[pi][:nk, c0:512], start=(bi == 0), stop=(bi == nb - 1))]
                p.group("pe", fns, reads=[vb_[blk], PTb[pi], E.onesb],
                        writes=[E.psb[4 + c], E.psb[6 + c]])

        def emit_epilogue(hh, qb):
            nonlocal nout
            k = nout % 2
            nout += 1
            for c in range(2):
                p.op("dve", lambda e, c=c: e.tensor_copy(out=oc[c], in_=E.ps[4 + c][:, :]),
                     reads=[E.psb[4 + c]], writes=[ocb[c]])
                p.op("dve", lambda e, c=c: e.tensor_copy(out=lc[c], in_=E.ps[6 + c][:, :]),
                     reads=[E.psb[6 + c]], writes=[lcb[c]])
            for c in range(2):
                p.op("dve", lambda e, c=c: e.reciprocal(out=lc[c], in_=lc[c]),
                     reads=[lcb[c]], writes=[lcb[c]])
            p.op("dve", lambda e: e.tensor_tensor(out=oc[0], in0=oc[0], in1=lc[0], op=ALU.mult),
                 reads=[ocb[0], lcb[0]], writes=[ocb[0]])
            p.op("dve", lambda e: e.tensor_tensor(out=oc[1], in0=oc[1], in1=lc[1], op=ALU.mult),
                 reads=[ocb[1], lcb[1]], writes=[ocb[1]])
            p.op("dve", lambda e, k=k: e.scalar_tensor_tensor(
                out=od[k], in0=oc[1], scalar=neglam, in1=oc[0], op0=ALU.mult, op1=ALU.add),
                reads=[ocb[0], ocb[1], lamb], writes=[odb[k]])
            p.op("dve", lambda e, k=k: e.tensor_tensor(out=sqo[k], in0=od[k], in1=od[k], op=ALU.mult),
                 reads=[odb[k]], writes=[sqob[k]])

            def tail(bank):
                emit_rstd(E, [sqo[k]], [sqob[k]], 512, 128, False, rso, rsob, bank)
                p.op("dve", lambda e: e.scalar_tensor_tensor(
                    out=ost[k], in0=od[k], scalar=gsub, in1=rso, op0=ALU.mult, op1=ALU.mult),
                    reads=[odb[k], rsob, lamb], writes=[ostb[k]])
                dst_ap, dst_bufs = o_dst(hp * 2 + hh, qb)
                osig.append(p.dma("sp", dst_ap, ost[k], ossem[k],
                                  reads=[ostb[k]], writes=dst_bufs))
                if after_tail is not None:
                    after_tail(hp * 2 + hh, qb)
            return tail

        pending = None
        emit_qk(0)
        for j in range(len(jobs)):
            pts = emit_exp(j)
            if j + 1 < len(jobs):
                emit_qk(j + 1)
            emit_pv(j, pts)
            hh, qb, bi, nb = jobs[j][:4]
            if bi == nb - 1:
                newp = emit_epilogue(hh, qb)
                if pending is not None:
                    pending((j % 2) * 2)
                pending = newp
        if pending is not None:
            pending(0)
    p.fence()
    A.release(m0)
    return osig


def emit_wo_residual(E, tiles_main, o_load, wo_dram):
    p = E.p
    A = E.A
    hT = E.hT
    m0 = A.mark()
    Wo = A.alloc([NCH, D], BF16)
    Wob = Buf("wo")
    wsem = E.p.new_dma_sem("wo")
    p.dma("pool", Wo.rearrange("p a b -> p (a b)").rearrange("p (a m) -> p a m", a=8),
          wo_dram.rearrange("p (a m) -> p a m", a=8), wsem, writes=[Wob])
    ot = [A.alloc([NCH, 512], BF16) for _ in range(2)]
    otb = [Buf() for _ in range(2)]
    osem = [E.p.new_dma_sem("ot%d" % i) for i in range(2)]
    M = [A.alloc([NCH, 512], F32) for _ in range(2)]
    Mb = [[Buf() for _ in range(NCH)] for _ in range(2)]
    sqm = [A.alloc([NCH, 512], BF16) for _ in range(2)]
    sqmb = [[Buf() for _ in range(NCH)] for _ in range(2)]
    rs2 = [A.alloc([512], F32) for _ in range(2)]
    rs2b = [Buf() for _ in range(2)]
    for i, t in enumerate(tiles_main):
        n = t.n
        k = i % 2
        m_off = t.s0 - NPRE
        o_load(i, k, m_off, n, ot[k], otb[k], osem[k])
        for c in range(NCH):
            bank = c % 6
            fns = []
            for hc in range(NCH):
                fns.append(lambda e, hc=hc, c=c, k=k, n=n, bank=bank: e.matmul(
                    E.ps[bank][:, :n], lhsT=Wo[:, hc, c * 128:c * 128 + 128],
                    rhs=ot[k][:, hc, :n], start=(hc == 0), stop=(hc == NCH - 1)))
            p.group("pe", fns, reads=[Wob, otb[k]], writes=[E.psb[bank]])
            p.op("act", lambda e, c=c, k=k, n=n, bank=bank: e.activation(
                out=sqm[k][:, c, :n], in_=E.ps[bank][:, :n], func=AF.Square),
                reads=[E.psb[bank]], writes=[sqmb[k][c]])
            p.op("dve", lambda e, c=c, k=k, n=n, bank=bank: e.tensor_scalar(
                out=M[k][:, c, :n], in0=E.ps[bank][:, :n], scalar1=vcol(E, ("mpost", 1), c),
                scalar2=None, op0=ALU.mult),
                reads=[E.psb[bank], E.vecsb], writes=[Mb[k][c]])
        emit_rstd(E, [sqm[k][:, c, :n] for c in range(NCH)], sqmb[k], n, D, False,
                  rs2[k][:, :n], rs2b[k], 6 + (i % 2))
        for c in range(NCH):
            ma = M[k][:, c, :n]
            p.op("dve", lambda e, ma=ma, k=k, n=n: e.tensor_tensor(
                out=ma, in0=ma, in1=rs2[k][:, :n], op=ALU.mult),
                reads=[Mb[k][c], rs2b[k]], writes=[Mb[k][c]])
            p.op("dve", lambda e, ma=ma, c=c, t=t, n=n: e.tensor_tensor(
                out=hT[:, c, t.s0:t.s0 + n], in0=hT[:, c, t.s0:t.s0 + n], in1=ma, op=ALU.add),
                reads=[Mb[k][c], t.buf], writes=[t.buf])
    p.fence()
    A.release(m0)


def emit_wo_residual_sel(E, tiles_main, load_ab, wo_dram, add_eng="pool"):
    p = E.p
    A = E.A
    hT = E.hT
    m0 = A.mark()
    Wa = A.alloc([NCH, D], BF16)
    Wb = A.alloc([NCH, D], BF16)
    Wab = Buf("woa")
    Wbb = Buf("wob")
    wsem = E.p.new_dma_sem("wo")
    p.dma("pool", Wb.rearrange("p a b -> p (a b)").rearrange("p (a m) -> p a m", a=8),
          wo_dram.rearrange("p (a m) -> p a m", a=8), wsem, writes=[Wbb])
    p.op("dve", lambda e: e.tensor_scalar(out=Wa, in0=Wb, scalar1=vcol(E, "selA"), scalar2=None,
                                          op0=ALU.mult), reads=[Wbb, E.vecsb], writes=[Wab])
    p.op("dve", lambda e: e.tensor_scalar(out=Wb, in0=Wb, scalar1=vcol(E, "selB"), scalar2=None,
                                          op0=ALU.mult), reads=[Wbb, Wab, E.vecsb], writes=[Wbb])
    xa = [A.alloc([NCH, 512], BF16) for _ in range(2)]
    xab = [Buf() for _ in range(2)]
    xasem = [p.new_dma_sem("xa%d" % i) for i in range(2)]
    xb = [A.alloc([NCH, 512], BF16) for _ in range(2)]
    xbb = [Buf() for _ in range(2)]
    xbsem = [p.new_dma_sem("xb%d" % i) for i in range(2)]
    M = [A.alloc([NCH, 512], F32) for _ in range(2)]
    Mb = [[Buf() for _ in range(NCH)] for _ in range(2)]
    sqm = [A.alloc([NCH, 512], BF16) for _ in range(2)]
    sqmb = [[Buf() for _ in range(NCH)] for _ in range(2)]
    rs2 = [A.alloc([512], F32) for _ in range(2)]
    rs2b = [Buf() for _ in range(2)]
    for i, t in enumerate(tiles_main):
        n = t.n
        k = i % 2
        load_ab(i, xa[k], xab[k], xasem[k], xb[k], xbb[k], xbsem[k])
        for c in range(NCH):
            bank = c % 6
            fns = []
            for hc in range(NCH):
                fns.append(lambda e, hc=hc, c=c, k=k, n=n, bank=bank: e.matmul(
                    E.ps[bank][:, :n], lhsT=Wa[:, hc, c * 128:c * 128 + 128],
                    rhs=xa[k][:, hc, :n], start=(hc == 0), stop=False))
            for hc in range(NCH):
                fns.append(lambda e, hc=hc, c=c, k=k, n=n, bank=bank: e.matmul(
                    E.ps[bank][:, :n], lhsT=Wb[:, hc, c * 128:c * 128 + 128],
                    rhs=xb[k][:, hc, :n], start=False, stop=(hc == NCH - 1)))
            p.group("pe", fns, reads=[Wab, Wbb, xab[k], xbb[k]], writes=[E.psb[bank]])
            p.op("act", lambda e, c=c, k=k, n=n, bank=bank: e.activation(
                out=sqm[k][:, c, :n], in_=E.ps[bank][:, :n], func=AF.Square),
                reads=[E.psb[bank]], writes=[sqmb[k][c]])
            p.op("dve", lambda e, c=c, k=k, n=n, bank=bank: e.tensor_scalar(
                out=M[k][:, c, :n], in0=E.ps[bank][:, :n], scalar1=vcol(E, ("mpost", 1), c),
                scalar2=None, op0=ALU.mult),
                reads=[E.psb[bank], E.vecsb], writes=[Mb[k][c]])
        emit_rstd(E, [sqm[k][:, c, :n] for c in range(NCH)], sqmb[k], n, D, False,
                  rs2[k][:, :n], rs2b[k], 6 + (i % 2))
        for c in range(NCH):
            ma = M[k][:, c, :n]
            p.op("dve", lambda e, ma=ma, k=k, n=n: e.tensor_tensor(
                out=ma, in0=ma, in1=rs2[k][:, :n], op=ALU.mult),
                reads=[Mb[k][c], rs2b[k]], writes=[Mb[k][c]])
            p.op(add_eng, lambda e, ma=ma, c=c, t=t, n=n: e.tensor_tensor(
                out=hT[:, c, t.s0:t.s0 + n], in0=hT[:, c, t.s0:t.s0 + n], in1=ma, op=ALU.add),
                reads=[Mb[k][c], t.buf], writes=[t.buf])
    p.fence()
    A.release(m0)

def _tiles():
    tm = [Tile_(NPRE + 512 * i, 512, Buf("h%d" % i)) for i in range(4)]
    tp = Tile_(0, NPRE, Buf("hp"))
    return tm, tp


def build_A(stop=None):
    nc = bass.Bass("TRN2", target_bir_lowering=False)
    xT = nc.dram_tensor("xT", [128, NCH, NSLOT], F32, kind="ExternalInput").ap()
    vecs = nc.dram_tensor("vecs", [128, NV], F32, kind="ExternalInput").ap()
    poolw = nc.dram_tensor("poolw", [128, 4 * 2 * 256], F32, kind="ExternalInput").ap()
    w13 = [nc.dram_tensor("w13_%d" % k, [NF, 128, 2048], F32, kind="ExternalInput").ap() for k in range(3)]
    w2 = [nc.dram_tensor("w2_%d" % k, [NCH, 128, DFF], F32, kind="ExternalInput").ap() for k in range(3)]
    h_out = nc.dram_tensor("h_out", [128, NCH, NMAIN], F32, kind="ExternalOutput").ap()
    hn_out = nc.dram_tensor("hn_out", [128, NCH, NSLOT], BF16, kind="ExternalOutput").ap()
    if stop is not None:
        h_dbg = nc.dram_tensor("h_dbg", [128, NCH, NSLOT], F32, kind="ExternalOutput").ap()

    def dbg_out(E, tall):
        sg = [E.p.dma("sp", h_dbg[:, :, t.s0:t.s0 + t.n], E.hT[:, :, t.s0:t.s0 + t.n],
                      E.stsem, reads=[t.buf]) for t in tall]
        E.p.wait_all("sp", sg)
        E.p.flush()

    with ExitStack() as st:
        block = st.enter_context(nc.Block())
        E = make_env(nc, st, block)
        p = E.p
        A = E.A
        setup_consts(E)
        load_vecs(E, vecs)
        E.hT = A.alloc([NCH, NSLOT], F32)
        tm, tp = _tiles()
        tall = tm + [tp]
        for t in tall:
            p.dma("sp", E.hT[:, :, t.s0:t.s0 + t.n], xT[:, :, t.s0:t.s0 + t.n], p.fresh(), writes=[t.buf])
        w13r = WRing(E, "w13r", 2, [2, NCH, 128], 2)
        w2r = WRing(E, "w2r", 2, [NF, 128], 2)
        for k in range(3):
            ffn_plan(w13r, w2r, w13[k], w2[k], 2)
        w13r.start()
        w2r.start()
        S0 = [tm[0], tm[1], tp]
        S1 = [tm[2], tm[3]]
        mF = A.mark()
        if stop == 1:
            dbg_out(E, tall); return nc
        Fn = alloc_ffn(E)
        for si, S in enumerate((S0, S1)):
            emit_ffn(E, Fn, S, w13r, w2r, ("fpre", 0, 0), ("fpost", 0, 0))
            if stop == 2 + si:
                dbg_out(E, tall); return nc
        p.fence()
        A.release(mF)
        emit_pool_mixer(E, tall, poolw)
        if stop == 4:
            dbg_out(E, tall); return nc
        Fn = alloc_ffn(E)
        for S in (S0, S1):
            emit_ffn(E, Fn, S, w13r, w2r, ("fpre", 0, 1), ("fpost", 0, 1))
        for S in (S0, S1):
            emit_ffn(E, Fn, S, w13r, w2r, ("fpre", 1, 0), ("fpost", 1, 0))
        p.fence()
        A.release(mF)
        hn = A.alloc([NCH, NSLOT], BF16)
        hnb = [Buf() for _ in tall]
        emit_mixnorm_to_bf16(E, tall, lambda c, s0, n: hn[:, c, s0:s0 + n], hnb, ("mpre", 1))
        sigs = []
        for i, t in enumerate(tall):
            sigs.append(p.dma("sp", hn_out[:, :, t.s0:t.s0 + t.n], hn[:, :, t.s0:t.s0 + t.n],
                              E.stsem, reads=[hnb[i]]))
        for t in tm:
            sigs.append(p.dma("sp", h_out[:, :, t.s0 - NPRE:t.s0 - NPRE + t.n],
                              E.hT[:, :, t.s0:t.s0 + t.n], E.stsem, reads=[t.buf]))
        p.wait_all("sp", sigs)
        p.flush()
    return nc


def build_B():
    nc = bass.Bass("TRN2", target_bir_lowering=False)
    hn = nc.dram_tensor("hn", [128, NCH, LSEQ], BF16, kind="ExternalInput").ap()
    vecs = nc.dram_tensor("vecs", [128, NV], F32, kind="ExternalInput").ap()
    wqkv = nc.dram_tensor("wqkv", [2, 128, 3 * NCH * 256], F32, kind="ExternalInput").ap()
    o_out = nc.dram_tensor("o_out", [128, 4, 4096], BF16, kind="ExternalOutput").ap()
    with ExitStack() as st:
        block = st.enter_context(nc.Block())
        E = make_env(nc, st, block)
        setup_consts(E)
        load_vecs(E, vecs)
        tl = [(0, 16)] + [(16 + 512 * i, 512) for i in range(8)]
        sigs = emit_attention(E, lambda ti: (hn[:, :, tl[ti][0]:tl[ti][0] + tl[ti][1]], []),
                              wqkv, lambda lh, qb: (o_out[:, lh, qb * 512:qb * 512 + 512], []))
        E.p.wait_all("sp", sigs)
        E.p.flush()
    return nc


def build_C():
    nc = bass.Bass("TRN2", target_bir_lowering=False)
    hT_in = nc.dram_tensor("hT_in", [128, NCH, NMAIN], F32, kind="ExternalInput").ap()
    oT = nc.dram_tensor("oT", [128, NCH, NMAIN], BF16, kind="ExternalInput").ap()
    vecs = nc.dram_tensor("vecs", [128, NV], F32, kind="ExternalInput").ap()
    wo = nc.dram_tensor("wo", [128, NCH * D], F32, kind="ExternalInput").ap()
    w13 = nc.dram_tensor("w13_3", [NF, 128, 2048], F32, kind="ExternalInput").ap()
    w2 = nc.dram_tensor("w2_3", [NCH, 128, DFF], F32, kind="ExternalInput").ap()
    y = nc.dram_tensor("y", [128, NCH, NMAIN], F32, kind="ExternalOutput").ap()
    with ExitStack() as st:
        block = st.enter_context(nc.Block())
        E = make_env(nc, st, block)
        p = E.p
        A = E.A
        setup_consts(E)
        load_vecs(E, vecs)
        E.hT = A.alloc([NCH, NSLOT], F32)
        tm, tp = _tiles()
        for t in tm:
            p.dma("sp", E.hT[:, :, t.s0:t.s0 + t.n], hT_in[:, :, t.s0 - NPRE:t.s0 - NPRE + t.n],
                  p.fresh(), writes=[t.buf])
        w13r = WRing(E, "w13r", 2, [2, NCH, 128], 2)
        w2r = WRing(E, "w2r", 2, [NF, 128], 2)
        ffn_plan(w13r, w2r, w13, w2, 2)
        w13r.start()
        w2r.start()
        emit_wo_residual(E, tm, lambda i, k, m_off, n, dst, dstb, sem: p.dma(
            "sp", dst[:, :, :n], oT[:, :, m_off:m_off + n], sem, writes=[dstb]), wo)
        Fn = alloc_ffn(E)
        for S in ([tm[0], tm[1]], [tm[2], tm[3]]):
            emit_ffn(E, Fn, S, w13r, w2r, ("fpre", 1, 1), ("fpost", 1, 1))
        sigs = []
        for t in tm:
            sigs.append(p.dma("sp", y[:, :, t.s0 - NPRE:t.s0 - NPRE + t.n],
                              E.hT[:, :, t.s0:t.s0 + t.n], E.stsem, reads=[t.buf]))
        p.wait_all("sp", sigs)
        p.flush()
    return nc


def _w13_layout(w1, w3):
    a = np.stack([np.asarray(w1, np.float32), np.asarray(w3, np.float32)], 0)
    a = a.reshape(2, NCH, 128, NF, 128)
    a = a.transpose(3, 2, 0, 1, 4)
    return np.ascontiguousarray(a).reshape(NF, 128, 2048)


def _w2_layout(w2):
    a = np.asarray(w2, np.float32).reshape(NF, 128, NCH, 128)
    a = a.transpose(2, 1, 0, 3)
    return np.ascontiguousarray(a).reshape(NCH, 128, DFF)


def _to_fm(a):
    T = a.shape[0]
    return np.ascontiguousarray(a.reshape(T, NCH, 128).transpose(2, 1, 0))


def _from_fm(a):
    T = a.shape[2]
    return np.ascontiguousarray(a.transpose(2, 1, 0)).reshape(T, D)


_CACHE = {}


def _get(name, fn):
    if name not in _CACHE:
        _CACHE[name] = fn()
    return _CACHE[name]


def kernel_unfused(**inp):
    inp = {k: np.asarray(v) for k, v in inp.items()}
    x = inp["x"].astype(np.float32, copy=False)
    meta = inp["meta_tokens"].astype(np.float32, copy=False)
    B = x.shape[0]
    cores = list(range(8))
    w13 = {}
    w2 = {}
    for k, (i, j) in enumerate(((0, 0), (0, 1), (1, 0), (1, 1))):
        w13[k] = _w13_layout(inp["ffn_w1"][i, j], inp["ffn_w3"][i, j])
        w2[k] = _w2_layout(inp["ffn_w2"][i, j])
    pw = np.asarray(inp["pool_w"][0], np.float32).reshape(4, 2, 128, 256).transpose(2, 0, 1, 3)
    pw = np.ascontiguousarray(pw).reshape(128, 4 * 2 * 256)
    vecs = [_build_vecs(inp, c % 2) for c in cores]
    in_maps = []
    for c in cores:
        b, half = c // 2, c % 2
        hseq = np.concatenate([meta, x[b]], axis=0)
        sl = hseq[half * NMAIN: half * NMAIN + NSLOT]
        m = {"xT": _to_fm(sl), "vecs": vecs[c], "poolw": pw}
        for k in range(3):
            m["w13_%d" % k] = w13[k]
            m["w2_%d" % k] = w2[k]
        in_maps.append(m)
    ncA = _get("A", build_A)
    rA = run_bass_kernel_spmd(ncA, in_maps, core_ids=cores)
    hA = [np.asarray(r["h_out"]) for r in rA.results]
    hnA = [np.asarray(r["hn_out"]) for r in rA.results]
    wq = np.asarray(inp["attn_w_qkv"][0], np.float32)
    in_maps = []
    for c in cores:
        b, hf = c // 2, c % 2
        e, o = hnA[2 * b], hnA[2 * b + 1]
        hn = np.concatenate([e, o[:, :, NPRE:]], axis=2)
        a = wq.reshape(NCH, 128, 3, 2, 2, 256)[:, :, :, hf]
        a = np.ascontiguousarray(a.transpose(3, 1, 2, 0, 4)).reshape(2, 128, 3 * NCH * 256)
        in_maps.append({"hn": np.ascontiguousarray(hn), "vecs": vecs[c], "wqkv": a})
    ncB = _get("B", build_B)
    rB = run_bass_kernel_spmd(ncB, in_maps, core_ids=cores)
    oB = [np.asarray(r["o_out"]) for r in rB.results]
    wo = np.asarray(inp["attn_w_o"][0], np.float32).reshape(NCH, 128, D).transpose(1, 0, 2)
    wo = np.ascontiguousarray(wo).reshape(128, NCH * D)
    in_maps = []
    for c in cores:
        b, half = c // 2, c % 2
        oT = np.concatenate([oB[2 * b][:, :, half * NMAIN:(half + 1) * NMAIN],
                             oB[2 * b + 1][:, :, half * NMAIN:(half + 1) * NMAIN]], axis=1)
        in_maps.append({"hT_in": hA[c], "oT": np.ascontiguousarray(oT), "vecs": vecs[c],
                        "wo": wo, "w13_3": w13[3], "w2_3": w2[3]})
    ncC = _get("C", build_C)
    rC = run_bass_kernel_spmd(ncC, in_maps, core_ids=cores)
    out = np.empty((B, 4096, D), np.float32)
    for c in cores:
        b, half = c // 2, c % 2
        out[b, half * NMAIN:(half + 1) * NMAIN] = _from_fm(np.asarray(rC.results[c]["y"]))
    return out

PAIRS = [[0, 1], [2, 3], [4, 5], [6, 7]]


def build_fused():
    nc = bass.Bass("TRN2", target_bir_lowering=False)
    xT = nc.dram_tensor("xT", [128, NCH, NSLOT], F32, kind="ExternalInput").ap()
    vecs = nc.dram_tensor("vecs", [128, NV], F32, kind="ExternalInput").ap()
    poolw = nc.dram_tensor("poolw", [128, 4 * 2 * 256], F32, kind="ExternalInput").ap()
    w13 = [nc.dram_tensor("w13_%d" % k, [NF, 128, 2048], F32, kind="ExternalInput").ap() for k in range(4)]
    w2 = [nc.dram_tensor("w2_%d" % k, [NCH, 128, DFF], F32, kind="ExternalInput").ap() for k in range(4)]
    wqkv = nc.dram_tensor("wqkv", [2, 128, 3 * NCH * 256], F32, kind="ExternalInput").ap()
    wo = nc.dram_tensor("wo", [128, NCH * D], F32, kind="ExternalInput").ap()
    y = nc.dram_tensor("y", [128, NCH, NMAIN], F32, kind="ExternalOutput").ap()
    xin = [nc.dram_tensor("xin%d" % i, [D, n], BF16).ap() for i, n in enumerate((512, 512, 512, 512, NPRE))]
    xout = [nc.dram_tensor("xout%d" % i, [2 * D, n], BF16).ap() for i, n in enumerate((512, 512, 512, 512, NPRE))]
    oin = [nc.dram_tensor("oin%d" % i, [512, 512], BF16).ap() for i in range(8)]
    oout = [nc.dram_tensor("oout%d" % i, [D, 512], BF16).ap() for i in range(8)]
    with ExitStack() as st:
        block = st.enter_context(nc.Block())
        E = make_env(nc, st, block)
        p = E.p
        A = E.A
        setup_consts(E)
        load_vecs(E, vecs)
        E.hT = A.alloc([NCH, NSLOT], F32)
        tm, tp = _tiles()
        tall = tm + [tp]
        for t in tall:
            p.dma("sp", E.hT[:, :, t.s0:t.s0 + t.n], xT[:, :, t.s0:t.s0 + t.n], p.fresh(), writes=[t.buf])
        w13r = WRing(E, "w13r", 2, [2, NCH, 128], 2)
        w2r = WRing(E, "w2r", 2, [NF, 128], 2)
        for k in range(4):
            ffn_plan(w13r, w2r, w13[k], w2[k], 2)
        w13r.start()
        w2r.start()
        S0 = [tm[0], tm[1], tp]
        S1 = [tm[2], tm[3]]
        mF = A.mark()
        Fn = alloc_ffn(E)
        emit_ffn_chain(E, Fn, [(S, ("fpre", 0, 0), ("fpost", 0, 0)) for S in (S0, S1)], w13r, w2r)
        p.fence()
        A.release(mF)
        emit_pool_mixer(E, tall, poolw)
        Fn = alloc_ffn(E)
        xoutb = [Buf("xout%d" % i) for i in range(5)]
        allF = Fn.allF
        Fbf = Fn.Fbf
        stage = [Fbf[:, j * 4096:(j + 1) * 4096].rearrange("p (c t) -> p c t", c=NCH) for j in range(3)]
        sqx = Fbf[:, 12288:16384].rearrange("p (c t) -> p c t", c=NCH)
        piece = {id(tm[0]): 0, id(tm[1]): 1, id(tm[2]): 2, id(tm[3]): 3, id(tp): 4}

        def mix_exchange(S):
            for i, t in enumerate(S):
                n = t.n
                st_ = stage[i]
                p.op("act", lambda e, t=t, n=n: e.activation(
                    out=sqx[:, :, :n], in_=E.hT[:, :, t.s0:t.s0 + n], func=AF.Square),
                    reads=[t.buf], writes=allF)
                emit_rstd(E, [sqx[:, c, :n] for c in range(NCH)], allF, n, D, False,
                          Fn.rs[i][:, :n], Fn.rsb[i], 6 + (i % 2))
                for c in range(NCH):
                    p.op("dve", lambda e, c=c, t=t, n=n, i=i, st_=st_: e.scalar_tensor_tensor(
                        out=st_[:, c, :n], in0=E.hT[:, c, t.s0:t.s0 + n],
                        scalar=vcol(E, ("mpre", 1), c), in1=Fn.rs[i][:, :n],
                        op0=ALU.mult, op1=ALU.mult),
                        reads=[t.buf, Fn.rsb[i], E.vecsb], writes=allF)
                pi = piece[id(t)]
                xb_ = Buf()
                p.dma("sp", xin[pi].rearrange("(c q) t -> q c t", q=128), st_[:, :, :n],
                      p.fresh(), reads=allF, writes=[xb_])
                p.allgather(xin[pi], xout[pi], PAIRS, reads=[xb_], writes=[xoutb[pi]])

        emit_ffn_chain(E, Fn, [(S0, ("fpre", 0, 1), ("fpost", 0, 1)),
                               (S1, ("fpre", 0, 1), ("fpost", 0, 1)),
                               (S0, ("fpre", 1, 0), ("fpost", 1, 0), lambda: mix_exchange(S0)),
                               (S1, ("fpre", 1, 0), ("fpost", 1, 0), lambda: mix_exchange(S1))],
                       w13r, w2r)
        p.fence(skip_collectives=True)
        A.release(mF)
        xo_v = [x_.rearrange("(r c q) t -> r q c t", r=2, q=128) for x_ in xout]

        def hn_src(ti):
            if ti == 0:
                return xo_v[4][0], [xoutb[4]]
            i = ti - 1
            r, j = i // 4, i % 4
            return xo_v[j][r], [xoutb[j]]

        oinb = [[Buf() for _ in range(4)] for _ in range(8)]
        oin_v = [o.rearrange("(h q) t -> q h t", q=128) for o in oin]

        def o_dst(lh, qb):
            return oin_v[qb][:, lh, :], [oinb[qb][lh]]

        ooutb = [Buf("oout%d" % i) for i in range(8)]
        st2 = {"cnt": [0] * 8, "ready": [], "issued": set()}

        def issue_ready():
            for qb in st2["ready"]:
                if qb not in st2["issued"]:
                    p.allgather(oin[qb], oout[qb], PAIRS, reads=oinb[qb], writes=[ooutb[qb]])
                    st2["issued"].add(qb)

        def after_tail(lh, qb):
            issue_ready()
            st2["cnt"][qb] += 1
            if st2["cnt"][qb] == 4:
                st2["ready"].append(qb)

        emit_attention(E, hn_src, wqkv, o_dst, after_tail)
        issue_ready()
        oo_v = [o.rearrange("(c q) t -> q c t", q=128) for o in oout]
        def load_ab(i, xa_, xab_, xas_, xb_, xbb_, xbs_):
            p.dma("sp", xa_, oo_v[i], xas_, reads=[ooutb[i]], writes=[xab_])
            p.dma("sp", xb_, oo_v[4 + i], xbs_, reads=[ooutb[4 + i]], writes=[xbb_])

        emit_wo_residual_sel(E, tm, load_ab, wo)
        A.release(mF)
        Fn = alloc_ffn(E)
        emit_ffn_chain(E, Fn, [(S, ("fpre", 1, 1), ("fpost", 1, 1))
                               for S in ([tm[0], tm[1]], [tm[2], tm[3]])], w13r, w2r)
        sigs = []
        for t in tm:
            sigs.append(p.dma("sp", y[:, :, t.s0 - NPRE:t.s0 - NPRE + t.n],
                              E.hT[:, :, t.s0:t.s0 + t.n], E.stsem, reads=[t.buf]))
        p.wait_all("sp", sigs)
        p.flush()
    return nc


def kernel(**inp):
    inp = {k: np.asarray(v) for k, v in inp.items()}
    x = inp["x"].astype(np.float32, copy=False)
    meta = inp["meta_tokens"].astype(np.float32, copy=False)
    B = x.shape[0]
    cores = list(range(8))
    w13 = {}
    w2 = {}
    for k, (i, j) in enumerate(((0, 0), (0, 1), (1, 0), (1, 1))):
        w13[k] = _w13_layout(inp["ffn_w1"][i, j], inp["ffn_w3"][i, j])
        w2[k] = _w2_layout(inp["ffn_w2"][i, j])
    pw = np.asarray(inp["pool_w"][0], np.float32).reshape(4, 2, 128, 256).transpose(2, 0, 1, 3)
    pw = np.ascontiguousarray(pw).reshape(128, 4 * 2 * 256)
    wq = np.asarray(inp["attn_w_qkv"][0], np.float32)
    wo = np.asarray(inp["attn_w_o"][0], np.float32).reshape(NCH, 128, D).transpose(1, 0, 2)
    wo = np.ascontiguousarray(wo).reshape(128, NCH * D)
    in_maps = []
    for c in cores:
        b, half = c // 2, c % 2
        hseq = np.concatenate([meta, x[b]], axis=0)
        sl = hseq[half * NMAIN: half * NMAIN + NSLOT]
        a = wq.reshape(NCH, 128, 3, 2, 2, 256)[:, :, :, half]
        a = np.ascontiguousarray(a.transpose(3, 1, 2, 0, 4)).reshape(2, 128, 3 * NCH * 256)
        m = {"xT": _to_fm(sl), "vecs": _build_vecs(inp, half), "poolw": pw, "wqkv": a, "wo": wo}
        for k in range(4):
            m["w13_%d" % k] = w13[k]
            m["w2_%d" % k] = w2[k]
        in_maps.append(m)
    nc = _get("F", build_fused)
    r = run_bass_kernel_spmd(nc, in_maps, core_ids=cores)
    out = np.empty((B, 4096, D), np.float32)
    for c in cores:
        b, half = c // 2, c % 2
        out[b, half * NMAIN:(half + 1) * NMAIN] = _from_fm(np.asarray(r.results[c]["y"]))
    return out
```

```python
import math
from contextlib import ExitStack
import numpy as np
import ml_dtypes
import concourse.bass as bass
import concourse.mybir as mybir
from concourse.bass_utils import run_bass_kernel_spmd

F32 = mybir.dt.float32
BF16 = mybir.dt.bfloat16
ALU = mybir.AluOpType
AF = mybir.ActivationFunctionType

D = 1024
NCH = 8
DFF = 2816
NF = 22
NPRE = 16
NMAIN = 2048
NSLOT = NPRE + NMAIN
LSEQ = 16 + 4096
EPS = 1e-6
LAMBDA_INIT = 0.8 - 0.6 * math.exp(-0.3 * 1)
POOL_WINDOWS = (2, 4, 8, 16)
ENGS = ("pe", "act", "dve", "pool", "sp")
DBG = None


class Buf:
    __slots__ = ("name", "w", "r", "excl")

    def __init__(self, name="", excl=False):
        self.name = name
        self.w = None
        self.r = []
        self.excl = excl


class Sig:
    __slots__ = ("key", "val", "clock")

    def __init__(self, key, val, clock):
        self.key = key
        self.val = val
        self.clock = clock


class Prog:
    SEM_LIMIT = 30000

    def __init__(self, nc, stack, block):
        self.nc = nc
        self.stack = stack
        self.block = block
        self.q = {e: [] for e in ENGS}
        self.sems = {}
        self.epoch = {e: 0 for e in ENGS}
        self.cnt = {e: 0 for e in ENGS}
        self.known = {e: {} for e in ENGS}
        self.nsem = 0
        self.dma_cnt = {}
        self.last = {}
        self.coll_keys = set()

    def sem(self, key):
        s = self.sems.get(key)
        if s is None:
            s = self.stack.enter_context(self.nc.semaphore("s%d" % self.nsem))
            self.nsem += 1
            self.sems[key] = s
        return s

    def new_dma_sem(self, name):
        key = ("dma", name)
        self.sem(key)
        self.dma_cnt[key] = 0
        return key

    def _split(self, reads, writes):
        ex = [b for b in reads if b.excl]
        if ex:
            reads = [b for b in reads if not b.excl]
            writes = list(writes) + ex
        return reads, writes

    def _deps(self, eng, reads, writes):
        need = []
        for b in reads:
            if b.w is not None:
                need.append(b.w)
        for b in writes:
            if b.w is not None:
                need.append(b.w)
            need.extend(b.r)
        kn = self.known[eng]
        best = {}
        for s in need:
            if kn.get(s.key, 0) >= s.val:
                continue
            o = best.get(s.key)
            if o is None or o.val < s.val:
                best[s.key] = s
        waits = []
        for s in best.values():
            waits.append((s.key, s.val))
            for k, v in s.clock.items():
                if kn.get(k, 0) < v:
                    kn[k] = v
        return waits

    def _mark(self, sig, reads, writes):
        for b in reads:
            b.r.append(sig)
        for b in writes:
            b.w = sig
            b.r = []
        self.last[sig.key] = sig

    def _newsig(self, eng):
        if self.cnt[eng] >= self.SEM_LIMIT:
            self.epoch[eng] += 1
            self.cnt[eng] = 0
        self.cnt[eng] += 1
        key = (eng, self.epoch[eng])
        self.sem(key)
        clock = dict(self.known[eng])
        clock[key] = self.cnt[eng]
        return Sig(key, self.cnt[eng], clock)

    def op(self, eng, fn, reads=(), writes=()):
        reads, writes = self._split(reads, writes)
        waits = self._deps(eng, reads, writes)
        sig = self._newsig(eng)
        self._mark(sig, reads, writes)
        self.q[eng].append((waits, fn, sig.key, 1))
        return sig

    def group(self, eng, fns, reads=(), writes=()):
        reads, writes = self._split(reads, writes)
        waits = self._deps(eng, reads, writes)
        sig = self._newsig(eng)
        self._mark(sig, reads, writes)
        n = len(fns)
        for i, fn in enumerate(fns):
            self.q[eng].append((waits if i == 0 else [], fn,
                                sig.key if i == n - 1 else None, 1))
        return sig

    def custom(self, eng, fn, semkey, inc, reads=(), writes=()):
        waits = self._deps(eng, reads, writes)
        self.dma_cnt[semkey] += inc
        clock = dict(self.known[eng])
        clock[semkey] = self.dma_cnt[semkey]
        sig = Sig(semkey, self.dma_cnt[semkey], clock)
        self._mark(sig, reads, writes)
        self.q[eng].append((waits, fn, semkey, inc))
        return sig

    def dma(self, eng, out, in_, semkey, reads=(), writes=(), **kw):
        return self.custom(eng, lambda e: e.dma_start(out=out, in_=in_, **kw), semkey, 16,
                           reads, writes)

    def allgather(self, in_ap, out_ap, groups, reads=(), writes=()):
        key = self.fresh()
        self.coll_keys.add(key)
        return self.custom("pool", lambda e: e.collective_compute(
            "AllGather", ALU.bypass, replica_groups=groups, ins=[in_ap.opt()],
            outs=[out_ap.opt()]), key, 1, reads, writes)

    def wait_all(self, eng, sigs):
        kn = self.known[eng]
        best = {}
        for s in sigs:
            if kn.get(s.key, 0) < s.val:
                o = best.get(s.key)
                if o is None or o.val < s.val:
                    best[s.key] = s
        waits = []
        for s in best.values():
            waits.append((s.key, s.val))
            for k, v in s.clock.items():
                if kn.get(k, 0) < v:
                    kn[k] = v
        if waits:
            self.q[eng].append((waits, None, None, 0))

    def fresh(self):
        self.nfresh = getattr(self, "nfresh", 0) + 1
        return self.new_dma_sem("f%d" % self.nfresh)

    def fence(self, skip_collectives=False):
        sigs = [s for k, s in self.last.items()
                if not (skip_collectives and k in self.coll_keys)]
        for e in ENGS:
            self.wait_all(e, sigs)

    def flush(self):
        sems = self.sems

        def run(e, lst):
            for waits, fn, key, inc in lst:
                for (k, v) in waits:
                    e.wait_ge(sems[k], v)
                if fn is None:
                    continue
                ins = fn(e)
                if key is not None:
                    ins.then_inc(sems[key], inc)

        b = self.block
        for name, starter in (("pe", b.tensor), ("act", b.scalar), ("dve", b.vector),
                              ("pool", b.gpsimd), ("sp", b.sync)):
            lst = self.q[name]
            if lst:
                self.q[name] = []
                starter(lambda e, lst=lst: run(e, lst))


def _resh(ap, shape):
    if len(shape) == 1:
        return ap
    if len(shape) == 2:
        return ap.rearrange("p (a b) -> p a b", a=shape[0], b=shape[1])
    if len(shape) == 3:
        return ap.rearrange("p (a b c) -> p a b c", a=shape[0], b=shape[1], c=shape[2])
    raise ValueError(shape)


class Arena:
    def __init__(self, t, words):
        self.t = t
        self.W = words
        self.top = 0

    def alloc(self, shape, dtype):
        n = int(np.prod(shape))
        nbytes = n * (4 if dtype == F32 else 2)
        words = (nbytes + 31) // 32 * 8
        off = self.top
        self.top += words
        assert self.top <= self.W, ("arena overflow", self.top, self.W)
        return self.view(off, shape, dtype)

    def view(self, off, shape, dtype):
        n = int(np.prod(shape))
        if dtype == F32:
            ap = self.t[:, off:off + n]
        else:
            ap = self.t[:, off:off + (n + 1) // 2].bitcast(BF16)
            if n % 2:
                ap = ap[:, :n]
        return _resh(ap, shape)

    def mark(self):
        return self.top

    def release(self, m):
        self.top = m


def _vec_cols():
    cols = {}
    n = 0
    for i in range(2):
        for j in range(2):
            cols[("fpre", i, j)] = n; n += 8
            cols[("fpost", i, j)] = n; n += 8
    for i in range(2):
        cols[("mpre", i)] = n; n += 8
        cols[("mpost", i)] = n; n += 8
    cols["pool_b"] = n; n += 8
    cols["pool_scale"] = n; n += 8
    cols["subln"] = n; n += 1
    for k in ("lq1", "lk1", "lq2", "lk2"):
        cols[k] = n; n += 64
    cols["poolfix"] = n; n += 64
    cols["selA"] = n; n += 1
    cols["selB"] = n; n += 1
    return cols, n


VCOL, NV = _vec_cols()


def _fm(v):
    return np.ascontiguousarray(np.asarray(v, np.float32).reshape(8, 128).T)


def _build_vecs(inp, half):
    v = np.zeros((128, NV), np.float32)
    for i in range(2):
        for j in range(2):
            c = VCOL[("fpre", i, j)]; v[:, c:c + 8] = _fm(inp["ffn_norm_pre"][i, j])
            c = VCOL[("fpost", i, j)]; v[:, c:c + 8] = _fm(inp["ffn_norm_post"][i, j])
    for i in range(2):
        c = VCOL[("mpre", i)]; v[:, c:c + 8] = _fm(inp["mix_norm_pre"][i])
        c = VCOL[("mpost", i)]; v[:, c:c + 8] = _fm(inp["mix_norm_post"][i])
    c = VCOL["pool_b"]; v[:, c:c + 8] = _fm(np.asarray(inp["pool_b"][0]).reshape(-1))
    c = VCOL["pool_scale"]; v[:, c:c + 8] = _fm(inp["pool_scale"][0])
    c = VCOL["subln"]; v[:, c] = np.asarray(inp["attn_subln"][0], np.float32)
    for k, nm in (("lq1", "attn_lambda_q1"), ("lk1", "attn_lambda_k1"),
                  ("lq2", "attn_lambda_q2"), ("lk2", "attn_lambda_k2")):
        c = VCOL[k]; v[:, c:c + 64] = np.asarray(inp[nm][0], np.float32)[None, :]
    c = VCOL["poolfix"]
    for g, w in enumerate(POOL_WINDOWS):
        for t in range(16):
            cnt = min(t + 1, w) if half == 0 else w
            v[:, c + g * 16 + t] = 1.0 / cnt
    v[:, VCOL["selA"]] = 1.0 if half == 0 else 0.0
    v[:, VCOL["selB"]] = 0.0 if half == 0 else 1.0
    return v


class Env:
    pass


def make_env(nc, st, block, need_w=True):
    E = Env()
    E.nc = nc
    E.p = Prog(nc, st, block)
    words = 52800
    E.arena_t = st.enter_context(nc.sbuf_tensor("arena", [128, words], F32))
    E.A = Arena(E.arena_t, words)
    E.ps = [st.enter_context(nc.psum_tensor("ps%d" % i, [128, 512], F32)) for i in range(8)]
    E.psb = [Buf("ps%d" % i, excl=True) for i in range(8)]
    A = E.A
    E.ones = A.alloc([128], BF16)
    E.onesb = Buf("ones")
    E.vecs = A.alloc([NV], F32)
    E.vecsb = Buf("vecs")
    E.stsem = E.p.new_dma_sem("st")
    E.p.op("dve", lambda e: e.memset(E.ones, 1.0), writes=[E.onesb])
    return E


def load_vecs(E, vecs_dram):
    E.p.dma("sp", E.vecs, vecs_dram, E.p.fresh(), writes=[E.vecsb])


def vcol(E, key, c=0, n=1):
    o = VCOL[key] + c
    return E.vecs[:, o:o + n]


class WRing:
    def __init__(self, E, name, nslots, shape, split):
        self.E = E
        self.n = nslots
        self.split = split
        self.tiles = [E.A.alloc(shape, BF16) for _ in range(nslots)]
        self.bufs = [Buf("%s%d" % (name, i)) for i in range(nslots)]
        self.sems = [E.p.new_dma_sem("%s%d" % (name, i)) for i in range(nslots)]
        self.seq = []
        self.issued = 0
        self.used = 0

    def plan(self, srcs):
        self.seq.extend(srcs)

    def _issue(self):
        k = self.issued
        s = k % self.n
        src = self.seq[k]
        a = self.split
        dst = self.tiles[s]
        flat = dst
        nd = len(dst.shape)
        if nd == 3:
            flat = dst.rearrange("p a b -> p (a b)")
        elif nd == 4:
            flat = dst.rearrange("p a b c -> p (a b c)")
        self.E.p.dma("pool", flat.rearrange("p (a m) -> p a m", a=a),
                     src.rearrange("p (a m) -> p a m", a=a), self.sems[s],
                     writes=[self.bufs[s]])
        self.issued += 1

    def start(self):
        while self.issued < min(self.n, len(self.seq)):
            self._issue()

    def get(self):
        k = self.used
        assert k < self.issued, "weight ring underflow"
        s = k % self.n
        return self.tiles[s], self.bufs[s]

    def done(self):
        self.used += 1
        if self.issued < len(self.seq):
            self._issue()


class Tile_:
    def __init__(self, s0, n, buf):
        self.s0 = s0
        self.n = n
        self.buf = buf


def emit_rstd(E, sq_aps, sq_bufs, n, nfeat, post_bias, rs_ap, rs_buf, bank):
    p = E.p
    ps = E.ps[bank]
    m = len(sq_aps)
    fns = []
    for i, a in enumerate(sq_aps):
        fns.append(lambda e, a=a, i=i: e.matmul(ps[:, :n], lhsT=E.ones, rhs=a,
                                                 start=(i == 0), stop=(i == m - 1)))
    p.group("pe", fns, reads=list(sq_bufs) + [E.onesb], writes=[E.psb[bank]])
    p.op("act", lambda e: e.activation(out=rs_ap, in_=ps[:, :n], func=AF.Ln,
                                       scale=1.0 / nfeat, bias=E.epsc),
         reads=[E.psb[bank], E.constb], writes=[rs_buf])
    bias_ap = E.ln05c if post_bias else E.zeroc
    p.op("act", lambda e: e.activation(out=rs_ap, in_=rs_ap, func=AF.Exp,
                                       scale=-0.5, bias=bias_ap),
         reads=[rs_buf, E.constb], writes=[rs_buf])


def setup_consts(E):
    A = E.A
    E.cst = A.alloc([8], F32)
    E.constb = Buf("const")
    E.epsc = E.cst[:, 0:1]
    E.ln05c = E.cst[:, 1:2]
    E.zeroc = E.cst[:, 2:3]
    p = E.p
    p.op("dve", lambda e: e.memset(E.cst[:, 0:1], EPS), writes=[E.constb])
    p.op("dve", lambda e: e.memset(E.cst[:, 1:2], math.log(0.5)), reads=[E.constb], writes=[E.constb])
    p.op("dve", lambda e: e.memset(E.cst[:, 2:3], 0.0), reads=[E.constb], writes=[E.constb])


def alloc_ffn(E):
    A = E.A
    Fn = Env()
    Fn.NW = 1040
    wid = (512, 512, NPRE)
    Fn.G = A.alloc([NF, Fn.NW], BF16)
    Fn.xn = [A.alloc([NCH, Fn.NW], BF16) for _ in range(2)]
    Fn.F = A.alloc([NCH, Fn.NW], F32)
    Fn.Fbf = Fn.F.rearrange("p a b -> p (a b)").bitcast(BF16)
    Fn.sil = [A.alloc([w], F32) for w in wid]
    Fn.silb = [Buf("sil%d" % i) for i in range(3)]
    Fn.rs = [A.alloc([w], F32) for w in wid]
    Fn.rsb = [Buf("rs%d" % i) for i in range(3)]
    Fn.rq = Fn.rs
    Fn.rqb = Fn.rsb
    Fn.xnb = [[[Buf() for _ in range(3)] for _ in range(NCH)] for _ in range(2)]
    Fn.Gb = [[Buf() for _ in range(3)] for _ in range(NF)]
    Fn.Fb = [[Buf() for _ in range(3)] for _ in range(NCH)]
    Fn.allF = [b_ for row in Fn.Fb for b_ in row]
    return Fn


def _offs(tiles):
    offs = []
    o = 0
    for t in tiles:
        offs.append(o)
        o += t.n
    return offs


def _sqv(Fn, i, n):
    return Fn.Fbf[:, i * 4096:(i + 1) * 4096].rearrange("p (c t) -> p c t", c=NCH)[:, :, :n]


def ffn_pre_sq(E, Fn, tiles):
    p = E.p
    for i, t in enumerate(tiles):
        p.op("act", lambda e, t=t, i=i: e.activation(
            out=_sqv(Fn, i, t.n), in_=E.hT[:, :, t.s0:t.s0 + t.n], func=AF.Square),
            reads=[t.buf], writes=Fn.allF)


def ffn_pre_rest(E, Fn, tiles, gpre, xi):
    p = E.p
    hT = E.hT
    offs = _offs(tiles)
    xn = Fn.xn[xi]
    for i, t in enumerate(tiles):
        n = t.n
        sq = _sqv(Fn, i, n)
        emit_rstd(E, [sq[:, c, :] for c in range(NCH)], Fn.allF, n, D, False,
                  Fn.rs[i][:, :n], Fn.rsb[i], 6 + (i % 2))
        for c in range(NCH):
            p.op("dve", lambda e, c=c, i=i, t=t, n=n: e.scalar_tensor_tensor(
                out=xn[:, c, offs[i]:offs[i] + n], in0=hT[:, c, t.s0:t.s0 + n],
                scalar=vcol(E, gpre, c), in1=Fn.rs[i][:, :n], op0=ALU.mult, op1=ALU.mult),
                reads=[t.buf, Fn.rsb[i], E.vecsb], writes=[Fn.xnb[xi][c][i]])


def ffn_up(E, Fn, tiles, w13, xi, hooks=None):
    p = E.p
    nt = len(tiles)
    offs = _offs(tiles)
    xn = Fn.xn[xi]
    xn_all = [Fn.xnb[xi][c][i] for c in range(NCH) for i in range(nt)]
    for f in range(NF):
        if hooks and f in hooks:
            hooks[f]()
        wt, wb = w13.get()
        for wi in range(2):
            base = 0 if wi == 0 else 3
            fns = []
            for c in range(NCH):
                for i, t in enumerate(tiles):
                    fns.append(lambda e, c=c, i=i, t=t, wi=wi, base=base, wt=wt: e.matmul(
                        E.ps[base + i][:, :t.n], lhsT=wt[:, wi, c, :],
                        rhs=xn[:, c, offs[i]:offs[i] + t.n],
                        start=(c == 0), stop=(c == NCH - 1)))
            p.group("pe", fns, reads=[wb] + xn_all, writes=[E.psb[base + i] for i in range(nt)])
            if wi == 0:
                for i, t in enumerate(tiles):
                    p.op("act", lambda e, i=i, t=t: e.activation(
                        out=Fn.sil[i][:, :t.n], in_=E.ps[i][:, :t.n], func=AF.Silu),
                        reads=[E.psb[i]], writes=[Fn.silb[i]])
            else:
                for i, t in enumerate(tiles):
                    p.op("dve", lambda e, i=i, t=t, f=f: e.tensor_tensor(
                        out=Fn.G[:, f, offs[i]:offs[i] + t.n], in0=E.ps[3 + i][:, :t.n],
                        in1=Fn.sil[i][:, :t.n], op=ALU.mult),
                        reads=[E.psb[3 + i], Fn.silb[i]], writes=[Fn.Gb[f][i]])
        w13.done()


def ffn_down(E, Fn, tiles, w2, gpost, xi):
    p = E.p
    nt = len(tiles)
    offs = _offs(tiles)
    xn = Fn.xn[xi]
    for dc in range(NCH):
        wt, wb = w2.get()
        base = 0 if dc % 2 == 0 else 3
        fns = []
        for f in range(NF):
            for i, t in enumerate(tiles):
                fns.append(lambda e, f=f, i=i, t=t, base=base, wt=wt: e.matmul(
                    E.ps[base + i][:, :t.n], lhsT=wt[:, f, :],
                    rhs=Fn.G[:, f, offs[i]:offs[i] + t.n],
                    start=(f == 0), stop=(f == NF - 1)))
        p.group("pe", fns, reads=[wb] + [Fn.Gb[f][i] for f in range(NF) for i in range(nt)],
                writes=[E.psb[base + i] for i in range(nt)])
        w2.done()
        for i, t in enumerate(tiles):
            p.op("act", lambda e, i=i, t=t, dc=dc, base=base: e.activation(
                out=xn[:, dc, offs[i]:offs[i] + t.n], in_=E.ps[base + i][:, :t.n],
                func=AF.Square),
                reads=[E.psb[base + i]], writes=[Fn.xnb[xi][dc][i]])
            p.op("dve", lambda e, i=i, t=t, dc=dc, base=base: e.tensor_scalar(
                out=Fn.F[:, dc, offs[i]:offs[i] + t.n], in0=E.ps[base + i][:, :t.n],
                scalar1=vcol(E, gpost, dc), scalar2=None, op0=ALU.mult),
                reads=[E.psb[base + i], E.vecsb], writes=[Fn.Fb[dc][i]])


def ffn_post_stats(E, Fn, tiles, xi):
    offs = _offs(tiles)
    xn = Fn.xn[xi]
    for i, t in enumerate(tiles):
        n = t.n
        emit_rstd(E, [xn[:, c, offs[i]:offs[i] + n] for c in range(NCH)],
                  [Fn.xnb[xi][c][i] for c in range(NCH)], n, D, True,
                  Fn.rq[i][:, :n], Fn.rqb[i], 6 + (i % 2))


def ffn_post_resid(E, Fn, tiles):
    p = E.p
    hT = E.hT
    offs = _offs(tiles)
    for i, t in enumerate(tiles):
        n = t.n
        for c in range(NCH):
            fa = Fn.F[:, c, offs[i]:offs[i] + n]
            p.op("dve", lambda e, fa=fa, i=i, n=n: e.tensor_tensor(
                out=fa, in0=fa, in1=Fn.rq[i][:, :n], op=ALU.mult),
                reads=[Fn.Fb[c][i], Fn.rqb[i]], writes=[Fn.Fb[c][i]])
            p.op("dve", lambda e, fa=fa, c=c, t=t, n=n: e.tensor_tensor(
                out=hT[:, c, t.s0:t.s0 + n], in0=hT[:, c, t.s0:t.s0 + n], in1=fa, op=ALU.add),
                reads=[Fn.Fb[c][i], t.buf], writes=[t.buf])


def emit_ffn_chain(E, Fn, units, w13, w2):
    ffn_pre_sq(E, Fn, units[0][0])
    ffn_pre_rest(E, Fn, units[0][0], units[0][1], 0)
    for k, u in enumerate(units):
        tiles, gpre, gpost = u[:3]
        xi = k % 2
        hooks = None
        if k + 1 < len(units):
            nt_, ng_ = units[k + 1][0], units[k + 1][1]
            hooks = {8: (lambda nt_=nt_: ffn_pre_sq(E, Fn, nt_)),
                     11: (lambda nt_=nt_, ng_=ng_, xi=xi: ffn_pre_rest(E, Fn, nt_, ng_, 1 - xi))}
        ffn_up(E, Fn, tiles, w13, xi, hooks)
        ffn_down(E, Fn, tiles, w2, gpost, xi)
        ffn_post_stats(E, Fn, tiles, xi)
        ffn_post_resid(E, Fn, tiles)
        if len(u) > 3 and u[3] is not None:
            u[3]()


def emit_ffn(E, Fn, tiles, w13, w2, gpre, gpost):
    emit_ffn_chain(E, Fn, [(tiles, gpre, gpost)], w13, w2)


def ffn_plan(w13ring, w2ring, w13_dram, w2_dram, nsuper):
    for _ in range(nsuper):
        w13ring.plan([w13_dram[f] for f in range(NF)])
        w2ring.plan([w2_dram[dc] for dc in range(NCH)])


def emit_pool_mixer(E, tiles_all, poolw_dram):
    p = E.p
    A = E.A
    hT = E.hT
    m0 = A.mark()
    NS = NSLOT
    rs = A.alloc([NS], F32)
    rsb = [Buf() for _ in tiles_all]
    pw = A.alloc([4, 2, 256], BF16)
    pwb = Buf("pw")
    p.dma("pool", pw.rearrange("p a b c -> p (a b c)").rearrange("p (a m) -> p a m", a=2),
          poolw_dram.rearrange("p (a m) -> p a m", a=2), p.fresh(), writes=[pwb])
    bs = A.alloc([NCH], F32)
    bsb = Buf("bs")
    p.op("dve", lambda e: e.tensor_tensor(out=bs, in0=vcol(E, "pool_b", 0, 8),
                                          in1=vcol(E, "pool_scale", 0, 8), op=ALU.mult),
         reads=[E.vecsb], writes=[bsb])
    diff = A.alloc([NCH, NS], BF16)
    diffb = [Buf() for _ in range(NCH)]
    m1 = A.mark()
    sqh = A.alloc([NCH, 512], BF16)
    sqhb = Buf("sqh")
    for i, t in enumerate(tiles_all):
        n = t.n
        p.op("act", lambda e, t=t, n=n: e.activation(out=sqh[:, :, :n], in_=hT[:, :, t.s0:t.s0 + n],
                                                     func=AF.Square),
             reads=[t.buf], writes=[sqhb])
        emit_rstd(E, [sqh[:, c, :n] for c in range(NCH)], [sqhb], n, D, False,
                  rs[:, t.s0:t.s0 + n], rsb[i], 6 + (i % 2))
    hn = [A.alloc([NS], F32) for _ in range(2)]
    hnb = [Buf() for _ in range(2)]
    pa = [A.alloc([NS], F32) for _ in range(2)]
    pab = [Buf() for _ in range(2)]
    pb_ = [A.alloc([NS], F32) for _ in range(2)]
    pbb = [Buf() for _ in range(2)]
    allh = [t.buf for t in tiles_all]
    for g in range(4):
        win = POOL_WINDOWS[g]
        cs = (2 * g, 2 * g + 1)
        for k, c in enumerate(cs):
            p.op("dve", lambda e, c=c, k=k: e.scalar_tensor_tensor(
                out=hn[k], in0=hT[:, c, :], scalar=vcol(E, ("mpre", 0), c), in1=rs,
                op0=ALU.mult, op1=ALU.mult),
                reads=allh + rsb + [E.vecsb], writes=[hnb[k]])
        src = [hn[0], hn[1]]
        srcb = [hnb[0], hnb[1]]
        sh = 1
        flip = 0
        while sh < win:
            dst = [pa[k] if flip == 0 else pb_[k] for k in range(2)]
            dstb = [pab[k] if flip == 0 else pbb[k] for k in range(2)]
            for k in range(2):
                p.op("dve", lambda e, d=dst[k], s_=src[k], sh=sh: e.tensor_tensor(
                    out=d[:, sh:], in0=s_[:, sh:], in1=s_[:, :NS - sh], op=ALU.add),
                    reads=[srcb[k]], writes=[dstb[k]])
            for k in range(2):
                p.op("dve", lambda e, d=dst[k], s_=src[k], sh=sh: e.tensor_copy(
                    out=d[:, :sh], in_=s_[:, :sh]),
                    reads=[srcb[k], dstb[k]], writes=[dstb[k]])
            src, srcb = dst, dstb
            sh *= 2
            flip ^= 1
        for k, c in enumerate(cs):
            p.op("dve", lambda e, c=c, s_=src[k], k=k, win=win: e.scalar_tensor_tensor(
                out=diff[:, c, :], in0=s_, scalar=1.0 / win, in1=hn[k],
                op0=ALU.mult, op1=ALU.subtract),
                reads=[srcb[k], hnb[k]], writes=[diffb[c]])
        fx = vcol(E, "poolfix", g * 16, 16)
        for k, c in enumerate(cs):
            p.op("dve", lambda e, s_=src[k], fx=fx: e.tensor_tensor(
                out=s_[:, :16], in0=s_[:, :16], in1=fx, op=ALU.mult),
                reads=[srcb[k], diffb[c], E.vecsb], writes=[srcb[k]])
        for k, c in enumerate(cs):
            p.op("dve", lambda e, c=c, s_=src[k], k=k: e.tensor_tensor(
                out=diff[:, c, :16], in0=s_[:, :16], in1=hn[k][:, :16], op=ALU.subtract),
                reads=[srcb[k], hnb[k], diffb[c]], writes=[diffb[c]])
    p.fence()
    A.release(m1)
    M = [A.alloc([NCH, 512], F32) for _ in range(2)]
    Mb = [[Buf() for _ in range(NCH)] for _ in range(2)]
    sqm = [A.alloc([NCH, 512], BF16) for _ in range(2)]
    sqmb = [[Buf() for _ in range(NCH)] for _ in range(2)]
    rs2 = [A.alloc([512], F32) for _ in range(2)]
    rs2b = [Buf() for _ in range(2)]
    for i, t in enumerate(tiles_all):
        n = t.n
        k = i % 2
        for c in range(NCH):
            g = c // 2
            bank = c % 6
            fns = []
            for ci in range(2):
                fns.append(lambda e, g=g, ci=ci, c=c, t=t, n=n, bank=bank: e.matmul(
                    E.ps[bank][:, :n], lhsT=pw[:, g, ci, (c % 2) * 128:(c % 2) * 128 + 128],
                    rhs=diff[:, 2 * g + ci, t.s0:t.s0 + n], start=(ci == 0), stop=(ci == 1)))
            p.group("pe", fns, reads=[pwb, diffb[2 * g], diffb[2 * g + 1]], writes=[E.psb[bank]])
            p.op("act", lambda e, c=c, k=k, n=n, bank=bank: e.activation(
                out=sqm[k][:, c, :n], in_=E.ps[bank][:, :n], func=AF.Square,
                scale=vcol(E, "pool_scale", c), bias=bs[:, c:c + 1]),
                reads=[E.psb[bank], bsb, E.vecsb], writes=[sqmb[k][c]])
            p.op("dve", lambda e, c=c, k=k, n=n, bank=bank: e.tensor_scalar(
                out=M[k][:, c, :n], in0=E.ps[bank][:, :n], scalar1=vcol(E, "pool_b", c),
                scalar2=vcol(E, "pool_scale", c), op0=ALU.add, op1=ALU.mult),
                reads=[E.psb[bank], E.vecsb], writes=[Mb[k][c]])
        emit_rstd(E, [sqm[k][:, c, :n] for c in range(NCH)], sqmb[k], n, D, False,
                  rs2[k][:, :n], rs2b[k], 6 + (i % 2))
        for c in range(NCH):
            ma = M[k][:, c, :n]
            p.op("dve", lambda e, ma=ma, c=c, k=k, n=n: e.scalar_tensor_tensor(
                out=ma, in0=ma, scalar=vcol(E, ("mpost", 0), c), in1=rs2[k][:, :n],
                op0=ALU.mult, op1=ALU.mult),
                reads=[Mb[k][c], rs2b[k], E.vecsb], writes=[Mb[k][c]])
            p.op("pool", lambda e, ma=ma, c=c, t=t, n=n: e.tensor_tensor(
                out=hT[:, c, t.s0:t.s0 + n], in0=hT[:, c, t.s0:t.s0 + n], in1=ma, op=ALU.add),
                reads=[Mb[k][c], t.buf], writes=[t.buf])
    p.fence()
    A.release(m0)


def emit_mixnorm_to_bf16(E, tiles_all, out_ap_fn, out_bufs, gkey):
    p = E.p
    A = E.A
    hT = E.hT
    m0 = A.mark()
    sqh = A.alloc([NCH, 512], BF16)
    sqhb = Buf("sqh")
    rs = [A.alloc([512], F32) for _ in range(2)]
    rsb = [Buf() for _ in range(2)]
    for i, t in enumerate(tiles_all):
        n = t.n
        k = i % 2
        p.op("act", lambda e, t=t, n=n: e.activation(out=sqh[:, :, :n], in_=hT[:, :, t.s0:t.s0 + n],
                                                     func=AF.Square),
             reads=[t.buf], writes=[sqhb])
        emit_rstd(E, [sqh[:, c, :n] for c in range(NCH)], [sqhb], n, D, False,
                  rs[k][:, :n], rsb[k], 6 + (i % 2))
        for c in range(NCH):
            p.op("dve", lambda e, c=c, t=t, n=n, k=k: e.scalar_tensor_tensor(
                out=out_ap_fn(c, t.s0, n), in0=hT[:, c, t.s0:t.s0 + n],
                scalar=vcol(E, gkey, c), in1=rs[k][:, :n], op0=ALU.mult, op1=ALU.mult),
                reads=[t.buf, rsb[k], E.vecsb], writes=[out_bufs[i]])
    p.fence()
    A.release(m0)


def emit_attention(E, hn_src, wqkv_dram, o_dst, after_tail=None):
    p = E.p
    A = E.A
    m0 = A.mark()
    lam = A.alloc([8], F32)
    lamb = Buf("lam")
    tmp64 = A.alloc([64], F32)
    tmpb = Buf("tmp64")
    for j, (a, b) in enumerate((("lq1", "lk1"), ("lq2", "lk2"))):
        p.op("dve", lambda e, a=a, b=b: e.tensor_tensor(
            out=tmp64, in0=vcol(E, a, 0, 64), in1=vcol(E, b, 0, 64), op=ALU.mult),
            reads=[E.vecsb, tmpb], writes=[tmpb])
        p.op("dve", lambda e, j=j: e.reduce_sum(out=lam[:, j:j + 1], in_=tmp64,
                                                axis=mybir.AxisListType.X),
             reads=[tmpb, lamb], writes=[lamb])
    p.op("act", lambda e: e.activation(out=lam[:, 2:4], in_=lam[:, 0:2], func=AF.Exp),
         reads=[lamb], writes=[lamb])
    p.op("dve", lambda e: e.scalar_tensor_tensor(
        out=lam[:, 4:5], in0=lam[:, 3:4], scalar=-LAMBDA_INIT, in1=lam[:, 2:3],
        op0=ALU.add, op1=ALU.subtract), reads=[lamb], writes=[lamb])
    p.op("dve", lambda e: e.tensor_scalar(
        out=lam[:, 5:6], in0=vcol(E, "subln"), scalar1=1.0 - LAMBDA_INIT, scalar2=None,
        op0=ALU.mult), reads=[lamb, E.vecsb], writes=[lamb])
    neglam = lam[:, 4:5]
    gsub = lam[:, 5:6]
    tri = A.alloc([128], BF16)
    trib = Buf("tri")
    p.op("pool", lambda e: e.memset(tri, 1.0), writes=[trib])
    p.op("pool", lambda e: e.affine_select(out=tri, in_=tri, pattern=[[1, 128]],
                                           compare_op=ALU.is_ge, fill=0.0, base=0,
                                           channel_multiplier=-1),
         reads=[trib], writes=[trib])

    NKB = 33
    qT = A.alloc([2, 4096], BF16)
    kT = A.alloc([2, LSEQ], BF16)
    V = A.alloc([NKB, 256], BF16)
    W = A.alloc([3, NCH, 256], BF16)
    Wb = Buf("wqkv")
    wsem = E.p.new_dma_sem("wqkv")
    hnt = [A.alloc([NCH, 512], BF16) for _ in range(2)]
    hntb = [Buf() for _ in range(2)]
    hsem = [E.p.new_dma_sem("hn%d" % i) for i in range(2)]
    NPT = 8
    PT = [A.alloc([512], BF16) for _ in range(NPT)]
    PTb = [Buf() for _ in range(NPT)]
    oc = [A.alloc([512], F32) for _ in range(2)]
    ocb = [Buf() for _ in range(2)]
    lc = [A.alloc([512], F32) for _ in range(2)]
    lcb = [Buf() for _ in range(2)]
    od = [A.alloc([512], F32) for _ in range(2)]
    odb = [Buf() for _ in range(2)]
    sqo = [A.alloc([512], BF16) for _ in range(2)]
    sqob = [Buf() for _ in range(2)]
    rso = A.alloc([512], F32)
    rsob = Buf()
    ost = [A.alloc([512], BF16) for _ in range(2)]
    ostb = [Buf() for _ in range(2)]
    ossem = [E.p.new_dma_sem("os%d" % i) for i in range(2)]
    osig = []
    ptk = 0
    nout = 0
    for hp in range(2):
        qb_ = [Buf() for _ in range(8)]
        kb_ = [Buf() for _ in range(9)]
        vb_ = [Buf() for _ in range(NKB)]
        p.dma("pool", W.rearrange("p a b c -> p (a b c)").rearrange("p (a m) -> p a m", a=6),
              wqkv_dram[hp].rearrange("p (a m) -> p a m", a=6), wsem,
              writes=[Wb])
        tl = [(0, 16, True)] + [(16 + 512 * i, 512, False) for i in range(8)]
        for oi_, ti in enumerate((0, 1, 2, 5, 6, 3, 4, 7, 8)):
            s0, n, is_meta = tl[ti]
            hb = oi_ % 2
            src_ap, src_bufs = hn_src(ti)
            p.dma("sp", hnt[hb][:, :, :n], src_ap, hsem[hb], reads=src_bufs, writes=[hntb[hb]])
            for hh in range(2):
                for kind in ((1,) if is_meta else (0, 1)):
                    bank = (2 * hh + kind) % 4
                    fns = []
                    for c in range(NCH):
                        fns.append(lambda e, c=c, kind=kind, hh=hh, hb=hb, n=n, bank=bank: e.matmul(
                            E.ps[bank][:, :n], lhsT=W[:, kind, c, hh * 128:hh * 128 + 128],
                            rhs=hnt[hb][:, c, :n], start=(c == 0), stop=(c == NCH - 1)))
                    p.group("pe", fns, reads=[Wb, hntb[hb]], writes=[E.psb[bank]])
                    if kind == 0:
                        qi = ti - 1
                        p.op("act", lambda e, hh=hh, qi=qi, bank=bank: e.activation(
                            out=qT[:, hh, qi * 512:qi * 512 + 512], in_=E.ps[bank][:, :512],
                            func=AF.Copy), reads=[E.psb[bank]], writes=[qb_[qi]])
                    else:
                        p.op("dve", lambda e, hh=hh, s0=s0, n=n, bank=bank: e.tensor_copy(
                            out=kT[:, hh, s0:s0 + n], in_=E.ps[bank][:, :n]),
                            reads=[E.psb[bank]], writes=[kb_[ti]])
            nblk = 1 if is_meta else 4
            for bi in range(nblk):
                nk = 16 if is_meta else 128
                blk = 0 if is_meta else 1 + (ti - 1) * 4 + bi
                bank = 4 + (bi % 2)
                fns = []
                for c in range(NCH):
                    fns.append(lambda e, c=c, hb=hb, bi=bi, nk=nk, bank=bank: e.matmul(
                        E.ps[bank][:nk, :256], lhsT=hnt[hb][:, c, bi * 128:bi * 128 + nk],
                        rhs=W[:, 2, c, :], start=(c == 0), stop=(c == NCH - 1)))
                p.group("pe", fns, reads=[Wb, hntb[hb]], writes=[E.psb[bank]])
                eng = "act" if bi % 2 == 0 else "dve"
                if eng == "act":
                    p.op("act", lambda e, blk=blk, nk=nk, bank=bank: e.activation(
                        out=V[:nk, blk, :], in_=E.ps[bank][:nk, :256], func=AF.Copy),
                        reads=[E.psb[bank]], writes=[vb_[blk]])
                else:
                    p.op("dve", lambda e, blk=blk, nk=nk, bank=bank: e.tensor_copy(
                        out=V[:nk, blk, :], in_=E.ps[bank][:nk, :256]),
                        reads=[E.psb[bank]], writes=[vb_[blk]])
        jobs = []
        for qb in range(8):
            for hh in range(2):
                blocks = [(0, 16, 0)] + [(1 + kb, 128, 0) for kb in range(4 * qb)] + \
                         [(1 + 4 * qb + i, 128, 128 * i) for i in range(4)]
                nb = len(blocks)
                for bi, (blk, nk, c0) in enumerate(blocks):
                    jobs.append((hh, qb, bi, nb, blk, nk, c0))

        def emit_qk(j):
            hh, qb, bi, nb, blk, nk, c0 = jobs[j]
            sp = (j % 2) * 2
            q0 = qb * 512
            ks0 = 0 if blk == 0 else 16 + (blk - 1) * 128
            kbuf = kb_[0] if blk == 0 else kb_[1 + (blk - 1) // 4]
            fns = []
            for c in range(2):
                fns.append(lambda e, c=c, sp=sp, nk=nk, ks0=ks0, c0=c0, hh=hh, q0=q0: e.matmul(
                    E.ps[sp + c][:nk, c0:512], lhsT=kT[c * 64:c * 64 + 64, hh, ks0:ks0 + nk],
                    rhs=qT[c * 64:c * 64 + 64, hh, q0 + c0:q0 + 512], start=True, stop=True))
            p.group("pe", fns, reads=[kbuf, qb_[qb]], writes=[E.psb[sp], E.psb[sp + 1]])

        def emit_exp(j):
            nonlocal ptk
            hh, qb, bi, nb, blk, nk, c0 = jobs[j]
            sp = (j % 2) * 2
            pts = []
            for c in range(2):
                pi = ptk % NPT
                ptk += 1
                pts.append(pi)
                p.op("act", lambda e, c=c, sp=sp, nk=nk, c0=c0, pi=pi: e.activation(
                    out=PT[pi][:nk, c0:512], in_=E.ps[sp + c][:nk, c0:512], func=AF.Exp,
                    scale=0.125), reads=[E.psb[sp + c]], writes=[PTb[pi]])
                if blk != 0 and blk - 1 >= 4 * qb:
                    p.op("pool", lambda e, pi=pi, c0=c0: e.tensor_tensor(
                        out=PT[pi][:, c0:c0 + 128], in0=PT[pi][:, c0:c0 + 128], in1=tri,
                        op=ALU.mult), reads=[PTb[pi], trib], writes=[PTb[pi]])
            return pts

        def emit_pv(j, pts):
            hh, qb, bi, nb, blk, nk, c0 = jobs[j]
            for c in range(2):
                pi = pts[c]
                fns = [lambda e, c=c, pi=pi: e.matmul(
                           E.ps[4 + c][:, c0:512], lhsT=V[:nk, blk, hh * 128:hh * 128 + 128],
                           rhs=PT[pi][:nk, c0:512], start=(bi == 0), stop=(bi == nb - 1)),
                       lambda e, c=c, pi=pi: e.matmul(
                           E.ps[6 + c][:, c0:512], lhsT=E.ones[:nk, :],
                           rhs=PT[pi][:nk, c0:512], start=(bi == 0), stop=(bi == nb - 1))]
                p.group("pe", fns, reads=[vb_[blk], PTb[pi], E.onesb],
                        writes=[E.psb[4 + c], E.psb[6 + c]])

        def emit_epilogue(hh, qb):
            nonlocal nout
            k = nout % 2
            nout += 1
            for c in range(2):
                p.op("dve", lambda e, c=c: e.tensor_copy(out=oc[c], in_=E.ps[4 + c][:, :]),
                     reads=[E.psb[4 + c]], writes=[ocb[c]])
                p.op("dve", lambda e, c=c: e.tensor_copy(out=lc[c], in_=E.ps[6 + c][:, :]),
                     reads=[E.psb[6 + c]], writes=[lcb[c]])
            for c in range(2):
                p.op("dve", lambda e, c=c: e.reciprocal(out=lc[c], in_=lc[c]),
                     reads=[lcb[c]], writes=[lcb[c]])
            p.op("dve", lambda e: e.tensor_tensor(out=oc[0], in0=oc[0], in1=lc[0], op=ALU.mult),
                 reads=[ocb[0], lcb[0]], writes=[ocb[0]])
            p.op("dve", lambda e: e.tensor_tensor(out=oc[1], in0=oc[1], in1=lc[1], op=ALU.mult),
                 reads=[ocb[1], lcb[1]], writes=[ocb[1]])
            p.op("dve", lambda e, k=k: e.scalar_tensor_tensor(
                out=od[k], in0=oc[1], scalar=neglam, in1=oc[0], op0=ALU.mult, op1=ALU.add),
                reads=[ocb[0], ocb[1], lamb], writes=[odb[k]])
            p.op("dve", lambda e, k=k: e.tensor_tensor(out=sqo[k], in0=od[k], in1=od[k], op=ALU.mult),
                 reads=[odb[k]], writes=[sqob[k]])

            def tail(bank):
                emit_rstd(E, [sqo[k]], [sqob[k]], 512, 128, False, rso, rsob, bank)
                p.op("dve", lambda e: e.scalar_tensor_tensor(
                    out=ost[k], in0=od[k], scalar=gsub, in1=rso, op0=ALU.mult, op1=ALU.mult),
                    reads=[odb[k], rsob, lamb], writes=[ostb[k]])
                dst_ap, dst_bufs = o_dst(hp * 2 + hh, qb)
                osig.append(p.dma("sp", dst_ap, ost[k], ossem[k],
                                  reads=[ostb[k]], writes=dst_bufs))
                if after_tail is not None:
                    after_tail(hp * 2 + hh, qb)
            return tail

        pending = None
        emit_qk(0)
        for j in range(len(jobs)):
            pts = emit_exp(j)
            if j + 1 < len(jobs):
                emit_qk(j + 1)
            emit_pv(j, pts)
            hh, qb, bi, nb = jobs[j][:4]
            if bi == nb - 1:
                newp = emit_epilogue(hh, qb)
                if pending is not None:
                    pending((j % 2) * 2)
                pending = newp
        if pending is not None:
            pending(0)
    p.fence()
    A.release(m0)
    return osig


def emit_wo_residual(E, tiles_main, o_load, wo_dram):
    p = E.p
    A = E.A
    hT = E.hT
    m0 = A.mark()
    Wo = A.alloc([NCH, D], BF16)
    Wob = Buf("wo")
    wsem = E.p.new_dma_sem("wo")
    p.dma("pool", Wo.rearrange("p a b -> p (a b)").rearrange("p (a m) -> p a m", a=8),
          wo_dram.rearrange("p (a m) -> p a m", a=8), wsem, writes=[Wob])
    ot = [A.alloc([NCH, 512], BF16) for _ in range(2)]
    otb = [Buf() for _ in range(2)]
    osem = [E.p.new_dma_sem("ot%d" % i) for i in range(2)]
    M = [A.alloc([NCH, 512], F32) for _ in range(2)]
    Mb = [[Buf() for _ in range(NCH)] for _ in range(2)]
    sqm = [A.alloc([NCH, 512], BF16) for _ in range(2)]
    sqmb = [[Buf() for _ in range(NCH)] for _ in range(2)]
    rs2 = [A.alloc([512], F32) for _ in range(2)]
    rs2b = [Buf() for _ in range(2)]
    for i, t in enumerate(tiles_main):
        n = t.n
        k = i % 2
        m_off = t.s0 - NPRE
        o_load(i, k, m_off, n, ot[k], otb[k], osem[k])
        for c in range(NCH):
            bank = c % 6
            fns = []
            for hc in range(NCH):
                fns.append(lambda e, hc=hc, c=c, k=k, n=n, bank=bank: e.matmul(
                    E.ps[bank][:, :n], lhsT=Wo[:, hc, c * 128:c * 128 + 128],
                    rhs=ot[k][:, hc, :n], start=(hc == 0), stop=(hc == NCH - 1)))
            p.group("pe", fns, reads=[Wob, otb[k]], writes=[E.psb[bank]])
            p.op("act", lambda e, c=c, k=k, n=n, bank=bank: e.activation(
                out=sqm[k][:, c, :n], in_=E.ps[bank][:, :n], func=AF.Square),
                reads=[E.psb[bank]], writes=[sqmb[k][c]])
            p.op("dve", lambda e, c=c, k=k, n=n, bank=bank: e.tensor_scalar(
                out=M[k][:, c, :n], in0=E.ps[bank][:, :n], scalar1=vcol(E, ("mpost", 1), c),
                scalar2=None, op0=ALU.mult),
                reads=[E.psb[bank], E.vecsb], writes=[Mb[k][c]])
        emit_rstd(E, [sqm[k][:, c, :n] for c in range(NCH)], sqmb[k], n, D, False,
                  rs2[k][:, :n], rs2b[k], 6 + (i % 2))
        for c in range(NCH):
            ma = M[k][:, c, :n]
            p.op("dve", lambda e, ma=ma, k=k, n=n: e.tensor_tensor(
                out=ma, in0=ma, in1=rs2[k][:, :n], op=ALU.mult),
                reads=[Mb[k][c], rs2b[k]], writes=[Mb[k][c]])
            p.op("dve", lambda e, ma=ma, c=c, t=t, n=n: e.tensor_tensor(
                out=hT[:, c, t.s0:t.s0 + n], in0=hT[:, c, t.s0:t.s0 + n], in1=ma, op=ALU.add),
                reads=[Mb[k][c], t.buf], writes=[t.buf])
    p.fence()
    A.release(m0)


def emit_wo_residual_sel(E, tiles_main, load_ab, wo_dram, add_eng="pool"):
    p = E.p
    A = E.A
    hT = E.hT
    m0 = A.mark()
    Wa = A.alloc([NCH, D], BF16)
    Wb = A.alloc([NCH, D], BF16)
    Wab = Buf("woa")
    Wbb = Buf("wob")
    wsem = E.p.new_dma_sem("wo")
    p.dma("pool", Wb.rearrange("p a b -> p (a b)").rearrange("p (a m) -> p a m", a=8),
          wo_dram.rearrange("p (a m) -> p a m", a=8), wsem, writes=[Wbb])
    p.op("dve", lambda e: e.tensor_scalar(out=Wa, in0=Wb, scalar1=vcol(E, "selA"), scalar2=None,
                                          op0=ALU.mult), reads=[Wbb, E.vecsb], writes=[Wab])
    p.op("dve", lambda e: e.tensor_scalar(out=Wb, in0=Wb, scalar1=vcol(E, "selB"), scalar2=None,
                                          op0=ALU.mult), reads=[Wbb, Wab, E.vecsb], writes=[Wbb])
    xa = [A.alloc([NCH, 512], BF16) for _ in range(2)]
    xab = [Buf() for _ in range(2)]
    xasem = [p.new_dma_sem("xa%d" % i) for i in range(2)]
    xb = [A.alloc([NCH, 512], BF16) for _ in range(2)]
    xbb = [Buf() for _ in range(2)]
    xbsem = [p.new_dma_sem("xb%d" % i) for i in range(2)]
    M = [A.alloc([NCH, 512], F32) for _ in range(2)]
    Mb = [[Buf() for _ in range(NCH)] for _ in range(2)]
    sqm = [A.alloc([NCH, 512], BF16) for _ in range(2)]
    sqmb = [[Buf() for _ in range(NCH)] for _ in range(2)]
    rs2 = [A.alloc([512], F32) for _ in range(2)]
    rs2b = [Buf() for _ in range(2)]
    for i, t in enumerate(tiles_main):
        n = t.n
        k = i % 2
        load_ab(i, xa[k], xab[k], xasem[k], xb[k], xbb[k], xbsem[k])
        for c in range(NCH):
            bank = c % 6
            fns = []
            for hc in range(NCH):
                fns.append(lambda e, hc=hc, c=c, k=k, n=n, bank=bank: e.matmul(
                    E.ps[bank][:, :n], lhsT=Wa[:, hc, c * 128:c * 128 + 128],
                    rhs=xa[k][:, hc, :n], start=(hc == 0), stop=False))
            for hc in range(NCH):
                fns.append(lambda e, hc=hc, c=c, k=k, n=n, bank=bank: e.matmul(
                    E.ps[bank][:, :n], lhsT=Wb[:, hc, c * 128:c * 128 + 128],
                    rhs=xb[k][:, hc, :n], start=False, stop=(hc == NCH - 1)))
            p.group("pe", fns, reads=[Wab, Wbb, xab[k], xbb[k]], writes=[E.psb[bank]])
            p.op("act", lambda e, c=c, k=k, n=n, bank=bank: e.activation(
                out=sqm[k][:, c, :n], in_=E.ps[bank][:, :n], func=AF.Square),
                reads=[E.psb[bank]], writes=[sqmb[k][c]])
            p.op("dve", lambda e, c=c, k=k, n=n, bank=bank: e.tensor_scalar(
                out=M[k][:, c, :n], in0=E.ps[bank][:, :n], scalar1=vcol(E, ("mpost", 1), c),
                scalar2=None, op0=ALU.mult),
                reads=[E.psb[bank], E.vecsb], writes=[Mb[k][c]])
        emit_rstd(E, [sqm[k][:, c, :n] for c in range(NCH)], sqmb[k], n, D, False,
                  rs2[k][:, :n], rs2b[k], 6 + (i % 2))
        for c in range(NCH):
            ma = M[k][:, c, :n]
            p.op("dve", lambda e, ma=ma, k=k, n=n: e.tensor_tensor(
                out=ma, in0=ma, in1=rs2[k][:, :n], op=ALU.mult),
                reads=[Mb[k][c], rs2b[k]], writes=[Mb[k][c]])
            p.op(add_eng, lambda e, ma=ma, c=c, t=t, n=n: e.tensor_tensor(
                out=hT[:, c, t.s0:t.s0 + n], in0=hT[:, c, t.s0:t.s0 + n], in1=ma, op=ALU.add),
                reads=[Mb[k][c], t.buf], writes=[t.buf])
    p.fence()
    A.release(m0)

def _tiles():
    tm = [Tile_(NPRE + 512 * i, 512, Buf("h%d" % i)) for i in range(4)]
    tp = Tile_(0, NPRE, Buf("hp"))
    return tm, tp


def build_A(stop=None):
    nc = bass.Bass("TRN2", target_bir_lowering=False)
    xT = nc.dram_tensor("xT", [128, NCH, NSLOT], F32, kind="ExternalInput").ap()
    vecs = nc.dram_tensor("vecs", [128, NV], F32, kind="ExternalInput").ap()
    poolw = nc.dram_tensor("poolw", [128, 4 * 2 * 256], F32, kind="ExternalInput").ap()
    w13 = [nc.dram_tensor("w13_%d" % k, [NF, 128, 2048], F32, kind="ExternalInput").ap() for k in range(3)]
    w2 = [nc.dram_tensor("w2_%d" % k, [NCH, 128, DFF], F32, kind="ExternalInput").ap() for k in range(3)]
    h_out = nc.dram_tensor("h_out", [128, NCH, NMAIN], F32, kind="ExternalOutput").ap()
    hn_out = nc.dram_tensor("hn_out", [128, NCH, NSLOT], BF16, kind="ExternalOutput").ap()
    if stop is not None:
        h_dbg = nc.dram_tensor("h_dbg", [128, NCH, NSLOT], F32, kind="ExternalOutput").ap()

    def dbg_out(E, tall):
        sg = [E.p.dma("sp", h_dbg[:, :, t.s0:t.s0 + t.n], E.hT[:, :, t.s0:t.s0 + t.n],
                      E.stsem, reads=[t.buf]) for t in tall]
        E.p.wait_all("sp", sg)
        E.p.flush()

    with ExitStack() as st:
        block = st.enter_context(nc.Block())
        E = make_env(nc, st, block)
        p = E.p
        A = E.A
        setup_consts(E)
        load_vecs(E, vecs)
        E.hT = A.alloc([NCH, NSLOT], F32)
        tm, tp = _tiles()
        tall = tm + [tp]
        for t in tall:
            p.dma("sp", E.hT[:, :, t.s0:t.s0 + t.n], xT[:, :, t.s0:t.s0 + t.n], p.fresh(), writes=[t.buf])
        w13r = WRing(E, "w13r", 2, [2, NCH, 128], 2)
        w2r = WRing(E, "w2r", 2, [NF, 128], 2)
        for k in range(3):
            ffn_plan(w13r, w2r, w13[k], w2[k], 2)
        w13r.start()
        w2r.start()
        S0 = [tm[0], tm[1], tp]
        S1 = [tm[2], tm[3]]
        mF = A.mark()
        if stop == 1:
            dbg_out(E, tall); return nc
        Fn = alloc_ffn(E)
        for si, S in enumerate((S0, S1)):
            emit_ffn(E, Fn, S, w13r, w2r, ("fpre", 0, 0), ("fpost", 0, 0))
            if stop == 2 + si:
                dbg_out(E, tall); return nc
        p.fence()
        A.release(mF)
        emit_pool_mixer(E, tall, poolw)
        if stop == 4:
            dbg_out(E, tall); return nc
        Fn = alloc_ffn(E)
        for S in (S0, S1):
            emit_ffn(E, Fn, S, w13r, w2r, ("fpre", 0, 1), ("fpost", 0, 1))
        for S in (S0, S1):
            emit_ffn(E, Fn, S, w13r, w2r, ("fpre", 1, 0), ("fpost", 1, 0))
        p.fence()
        A.release(mF)
        hn = A.alloc([NCH, NSLOT], BF16)
        hnb = [Buf() for _ in tall]
        emit_mixnorm_to_bf16(E, tall, lambda c, s0, n: hn[:, c, s0:s0 + n], hnb, ("mpre", 1))
        sigs = []
        for i, t in enumerate(tall):
            sigs.append(p.dma("sp", hn_out[:, :, t.s0:t.s0 + t.n], hn[:, :, t.s0:t.s0 + t.n],
                              E.stsem, reads=[hnb[i]]))
        for t in tm:
            sigs.append(p.dma("sp", h_out[:, :, t.s0 - NPRE:t.s0 - NPRE + t.n],
                              E.hT[:, :, t.s0:t.s0 + t.n], E.stsem, reads=[t.buf]))
        p.wait_all("sp", sigs)
        p.flush()
    return nc


def build_B():
    nc = bass.Bass("TRN2", target_bir_lowering=False)
    hn = nc.dram_tensor("hn", [128, NCH, LSEQ], BF16, kind="ExternalInput").ap()
    vecs = nc.dram_tensor("vecs", [128, NV], F32, kind="ExternalInput").ap()
    wqkv = nc.dram_tensor("wqkv", [2, 128, 3 * NCH * 256], F32, kind="ExternalInput").ap()
    o_out = nc.dram_tensor("o_out", [128, 4, 4096], BF16, kind="ExternalOutput").ap()
    with ExitStack() as st:
        block = st.enter_context(nc.Block())
        E = make_env(nc, st, block)
        setup_consts(E)
        load_vecs(E, vecs)
        tl = [(0, 16)] + [(16 + 512 * i, 512) for i in range(8)]
        sigs = emit_attention(E, lambda ti: (hn[:, :, tl[ti][0]:tl[ti][0] + tl[ti][1]], []),
                              wqkv, lambda lh, qb: (o_out[:, lh, qb * 512:qb * 512 + 512], []))
        E.p.wait_all("sp", sigs)
        E.p.flush()
    return nc


def build_C():
    nc = bass.Bass("TRN2", target_bir_lowering=False)
    hT_in = nc.dram_tensor("hT_in", [128, NCH, NMAIN], F32, kind="ExternalInput").ap()
    oT = nc.dram_tensor("oT", [128, NCH, NMAIN], BF16, kind="ExternalInput").ap()
    vecs = nc.dram_tensor("vecs", [128, NV], F32, kind="ExternalInput").ap()
    wo = nc.dram_tensor("wo", [128, NCH * D], F32, kind="ExternalInput").ap()
    w13 = nc.dram_tensor("w13_3", [NF, 128, 2048], F32, kind="ExternalInput").ap()
    w2 = nc.dram_tensor("w2_3", [NCH, 128, DFF], F32, kind="ExternalInput").ap()
    y = nc.dram_tensor("y", [128, NCH, NMAIN], F32, kind="ExternalOutput").ap()
    with ExitStack() as st:
        block = st.enter_context(nc.Block())
        E = make_env(nc, st, block)
        p = E.p
        A = E.A
        setup_consts(E)
        load_vecs(E, vecs)
        E.hT = A.alloc([NCH, NSLOT], F32)
        tm, tp = _tiles()
        for t in tm:
            p.dma("sp", E.hT[:, :, t.s0:t.s0 + t.n], hT_in[:, :, t.s0 - NPRE:t.s0 - NPRE + t.n],
                  p.fresh(), writes=[t.buf])
        w13r = WRing(E, "w13r", 2, [2, NCH, 128], 2)
        w2r = WRing(E, "w2r", 2, [NF, 128], 2)
        ffn_plan(w13r, w2r, w13, w2, 2)
        w13r.start()
        w2r.start()
        emit_wo_residual(E, tm, lambda i, k, m_off, n, dst, dstb, sem: p.dma(
            "sp", dst[:, :, :n], oT[:, :, m_off:m_off + n], sem, writes=[dstb]), wo)
        Fn = alloc_ffn(E)
        for S in ([tm[0], tm[1]], [tm[2], tm[3]]):
            emit_ffn(E, Fn, S, w13r, w2r, ("fpre", 1, 1), ("fpost", 1, 1))
        sigs = []
        for t in tm:
            sigs.append(p.dma("sp", y[:, :, t.s0 - NPRE:t.s0 - NPRE + t.n],
                              E.hT[:, :, t.s0:t.s0 + t.n], E.stsem, reads=[t.buf]))
        p.wait_all("sp", sigs)
        p.flush()
    return nc


def _w13_layout(w1, w3):
    a = np.stack([np.asarray(w1, np.float32), np.asarray(w3, np.float32)], 0)
    a = a.reshape(2, NCH, 128, NF, 128)
    a = a.transpose(3, 2, 0, 1, 4)
    return np.ascontiguousarray(a).reshape(NF, 128, 2048)


def _w2_layout(w2):
    a = np.asarray(w2, np.float32).reshape(NF, 128, NCH, 128)
    a = a.transpose(2, 1, 0, 3)
    return np.ascontiguousarray(a).reshape(NCH, 128, DFF)


def _to_fm(a):
    T = a.shape[0]
    return np.ascontiguousarray(a.reshape(T, NCH, 128).transpose(2, 1, 0))


def _from_fm(a):
    T = a.shape[2]
    return np.ascontiguousarray(a.transpose(2, 1, 0)).reshape(T, D)


_CACHE = {}


def _get(name, fn):
    if name not in _CACHE:
        _CACHE[name] = fn()
    return _CACHE[name]


def kernel_unfused(**inp):
    inp = {k: np.asarray(v) for k, v in inp.items()}
    x = inp["x"].astype(np.float32, copy=False)
    meta = inp["meta_tokens"].astype(np.float32, copy=False)
    B = x.shape[0]
    cores = list(range(8))
    w13 = {}
    w2 = {}
    for k, (i, j) in enumerate(((0, 0), (0, 1), (1, 0), (1, 1))):
        w13[k] = _w13_layout(inp["ffn_w1"][i, j], inp["ffn_w3"][i, j])
        w2[k] = _w2_layout(inp["ffn_w2"][i, j])
    pw = np.asarray(inp["pool_w"][0], np.float32).reshape(4, 2, 128, 256).transpose(2, 0, 1, 3)
    pw = np.ascontiguousarray(pw).reshape(128, 4 * 2 * 256)
    vecs = [_build_vecs(inp, c % 2) for c in cores]
    in_maps = []
    for c in cores:
        b, half = c // 2, c % 2
        hseq = np.concatenate([meta, x[b]], axis=0)
        sl = hseq[half * NMAIN: half * NMAIN + NSLOT]
        m = {"xT": _to_fm(sl), "vecs": vecs[c], "poolw": pw}
        for k in range(3):
            m["w13_%d" % k] = w13[k]
            m["w2_%d" % k] = w2[k]
        in_maps.append(m)
    ncA = _get("A", build_A)
    rA = run_bass_kernel_spmd(ncA, in_maps, core_ids=cores)
    hA = [np.asarray(r["h_out"]) for r in rA.results]
    hnA = [np.asarray(r["hn_out"]) for r in rA.results]
    wq = np.asarray(inp["attn_w_qkv"][0], np.float32)
    in_maps = []
    for c in cores:
        b, hf = c // 2, c % 2
        e, o = hnA[2 * b], hnA[2 * b + 1]
        hn = np.concatenate([e, o[:, :, NPRE:]], axis=2)
        a = wq.reshape(NCH, 128, 3, 2, 2, 256)[:, :, :, hf]
        a = np.ascontiguousarray(a.transpose(3, 1, 2, 0, 4)).reshape(2, 128, 3 * NCH * 256)
        in_maps.append({"hn": np.ascontiguousarray(hn), "vecs": vecs[c], "wqkv": a})
    ncB = _get("B", build_B)
    rB = run_bass_kernel_spmd(ncB, in_maps, core_ids=cores)
    oB = [np.asarray(r["o_out"]) for r in rB.results]
    wo = np.asarray(inp["attn_w_o"][0], np.float32).reshape(NCH, 128, D).transpose(1, 0, 2)
    wo = np.ascontiguousarray(wo).reshape(128, NCH * D)
    in_maps = []
    for c in cores:
        b, half = c // 2, c % 2
        oT = np.concatenate([oB[2 * b][:, :, half * NMAIN:(half + 1) * NMAIN],
                             oB[2 * b + 1][:, :, half * NMAIN:(half + 1) * NMAIN]], axis=1)
        in_maps.append({"hT_in": hA[c], "oT": np.ascontiguousarray(oT), "vecs": vecs[c],
                        "wo": wo, "w13_3": w13[3], "w2_3": w2[3]})
    ncC = _get("C", build_C)
    rC = run_bass_kernel_spmd(ncC, in_maps, core_ids=cores)
    out = np.empty((B, 4096, D), np.float32)
    for c in cores:
        b, half = c // 2, c % 2
        out[b, half * NMAIN:(half + 1) * NMAIN] = _from_fm(np.asarray(rC.results[c]["y"]))
    return out

PAIRS = [[0, 1], [2, 3], [4, 5], [6, 7]]


def build_fused():
    nc = bass.Bass("TRN2", target_bir_lowering=False)
    xT = nc.dram_tensor("xT", [128, NCH, NSLOT], F32, kind="ExternalInput").ap()
    vecs = nc.dram_tensor("vecs", [128, NV], F32, kind="ExternalInput").ap()
    poolw = nc.dram_tensor("poolw", [128, 4 * 2 * 256], F32, kind="ExternalInput").ap()
    w13 = [nc.dram_tensor("w13_%d" % k, [NF, 128, 2048], F32, kind="ExternalInput").ap() for k in range(4)]
    w2 = [nc.dram_tensor("w2_%d" % k, [NCH, 128, DFF], F32, kind="ExternalInput").ap() for k in range(4)]
    wqkv = nc.dram_tensor("wqkv", [2, 128, 3 * NCH * 256], F32, kind="ExternalInput").ap()
    wo = nc.dram_tensor("wo", [128, NCH * D], F32, kind="ExternalInput").ap()
    y = nc.dram_tensor("y", [128, NCH, NMAIN], F32, kind="ExternalOutput").ap()
    xin = [nc.dram_tensor("xin%d" % i, [D, n], BF16).ap() for i, n in enumerate((512, 512, 512, 512, NPRE))]
    xout = [nc.dram_tensor("xout%d" % i, [2 * D, n], BF16).ap() for i, n in enumerate((512, 512, 512, 512, NPRE))]
    oin = [nc.dram_tensor("oin%d" % i, [512, 512], BF16).ap() for i in range(8)]
    oout = [nc.dram_tensor("oout%d" % i, [D, 512], BF16).ap() for i in range(8)]
    with ExitStack() as st:
        block = st.enter_context(nc.Block())
        E = make_env(nc, st, block)
        p = E.p
        A = E.A
        setup_consts(E)
        load_vecs(E, vecs)
        E.hT = A.alloc([NCH, NSLOT], F32)
        tm, tp = _tiles()
        tall = tm + [tp]
        for t in (tm[0], tm[1], tp, tm[2], tm[3]):
            p.dma("sp", E.hT[:, :, t.s0:t.s0 + t.n], xT[:, :, t.s0:t.s0 + t.n], p.fresh(), writes=[t.buf])
        w13r = WRing(E, "w13r", 2, [2, NCH, 128], 2)
        w2r = WRing(E, "w2r", 2, [NF, 128], 2)
        for k in range(4):
            ffn_plan(w13r, w2r, w13[k], w2[k], 2)
        w13r.start()
        w2r.start()
        S0 = [tm[0], tm[1], tp]
        S1 = [tm[2], tm[3]]
        mF = A.mark()
        Fn = alloc_ffn(E)
        emit_ffn_chain(E, Fn, [(S, ("fpre", 0, 0), ("fpost", 0, 0)) for S in (S0, S1)], w13r, w2r)
        p.fence()
        A.release(mF)
        emit_pool_mixer(E, tall, poolw)
        Fn = alloc_ffn(E)
        xoutb = [Buf("xout%d" % i) for i in range(5)]
        allF = Fn.allF
        Fbf = Fn.Fbf
        stage = [Fbf[:, j * 4096:(j + 1) * 4096].rearrange("p (c t) -> p c t", c=NCH) for j in range(3)]
        sqx = Fbf[:, 12288:16384].rearrange("p (c t) -> p c t", c=NCH)
        piece = {id(tm[0]): 0, id(tm[1]): 1, id(tm[2]): 2, id(tm[3]): 3, id(tp): 4}

        def mix_exchange(S):
            for i, t in enumerate(S):
                n = t.n
                st_ = stage[i]
                p.op("act", lambda e, t=t, n=n: e.activation(
                    out=sqx[:, :, :n], in_=E.hT[:, :, t.s0:t.s0 + n], func=AF.Square),
                    reads=[t.buf], writes=allF)
                emit_rstd(E, [sqx[:, c, :n] for c in range(NCH)], allF, n, D, False,
                          Fn.rs[i][:, :n], Fn.rsb[i], 6 + (i % 2))
                for c in range(NCH):
                    p.op("dve", lambda e, c=c, t=t, n=n, i=i, st_=st_: e.scalar_tensor_tensor(
                        out=st_[:, c, :n], in0=E.hT[:, c, t.s0:t.s0 + n],
                        scalar=vcol(E, ("mpre", 1), c), in1=Fn.rs[i][:, :n],
                        op0=ALU.mult, op1=ALU.mult),
                        reads=[t.buf, Fn.rsb[i], E.vecsb], writes=allF)
                pi = piece[id(t)]
                xb_ = Buf()
                p.dma("sp", xin[pi].rearrange("(c q) t -> q c t", q=128), st_[:, :, :n],
                      p.fresh(), reads=allF, writes=[xb_])
                p.allgather(xin[pi], xout[pi], PAIRS, reads=[xb_], writes=[xoutb[pi]])

        emit_ffn_chain(E, Fn, [(S0, ("fpre", 0, 1), ("fpost", 0, 1)),
                               (S1, ("fpre", 0, 1), ("fpost", 0, 1)),
                               (S0, ("fpre", 1, 0), ("fpost", 1, 0), lambda: mix_exchange(S0)),
                               (S1, ("fpre", 1, 0), ("fpost", 1, 0), lambda: mix_exchange(S1))],
                       w13r, w2r)
        p.fence(skip_collectives=True)
        A.release(mF)
        xo_v = [x_.rearrange("(r c q) t -> r q c t", r=2, q=128) for x_ in xout]

        def hn_src(ti):
            if ti == 0:
                return xo_v[4][0], [xoutb[4]]
            i = ti - 1
            r, j = i // 4, i % 4
            return xo_v[j][r], [xoutb[j]]

        oinb = [[Buf() for _ in range(4)] for _ in range(8)]
        oin_v = [o.rearrange("(h q) t -> q h t", q=128) for o in oin]

        def o_dst(lh, qb):
            return oin_v[qb][:, lh, :], [oinb[qb][lh]]

        ooutb = [Buf("oout%d" % i) for i in range(8)]
        st2 = {"cnt": [0] * 8, "ready": [], "issued": set()}

        def issue_ready():
            for qb in st2["ready"]:
                if qb not in st2["issued"]:
                    p.allgather(oin[qb], oout[qb], PAIRS, reads=oinb[qb], writes=[ooutb[qb]])
                    st2["issued"].add(qb)

        def after_tail(lh, qb):
            issue_ready()
            st2["cnt"][qb] += 1
            if st2["cnt"][qb] == 4:
                st2["ready"].append(qb)

        emit_attention(E, hn_src, wqkv, o_dst, after_tail)
        issue_ready()
        oo_v = [o.rearrange("(c q) t -> q c t", q=128) for o in oout]
        def load_ab(i, xa_, xab_, xas_, xb_, xbb_, xbs_):
            p.dma("sp", xa_, oo_v[i], xas_, reads=[ooutb[i]], writes=[xab_])
            p.dma("sp", xb_, oo_v[4 + i], xbs_, reads=[ooutb[4 + i]], writes=[xbb_])

        emit_wo_residual_sel(E, tm, load_ab, wo)
        A.release(mF)
        Fn = alloc_ffn(E)
        emit_ffn_chain(E, Fn, [(S, ("fpre", 1, 1), ("fpost", 1, 1))
                               for S in ([tm[0], tm[1]], [tm[2], tm[3]])], w13r, w2r)
        sigs = []
        for t in tm:
            sigs.append(p.dma("sp", y[:, :, t.s0 - NPRE:t.s0 - NPRE + t.n],
                              E.hT[:, :, t.s0:t.s0 + t.n], E.stsem, reads=[t.buf]))
        p.wait_all("sp", sigs)
        p.flush()
    return nc


def kernel(**inp):
    inp = {k: np.asarray(v) for k, v in inp.items()}
    x = inp["x"].astype(np.float32, copy=False)
    meta = inp["meta_tokens"].astype(np.float32, copy=False)
    B = x.shape[0]
    cores = list(range(8))
    w13 = {}
    w2 = {}
    for k, (i, j) in enumerate(((0, 0), (0, 1), (1, 0), (1, 1))):
        w13[k] = _w13_layout(inp["ffn_w1"][i, j], inp["ffn_w3"][i, j])
        w2[k] = _w2_layout(inp["ffn_w2"][i, j])
    pw = np.asarray(inp["pool_w"][0], np.float32).reshape(4, 2, 128, 256).transpose(2, 0, 1, 3)
    pw = np.ascontiguousarray(pw).reshape(128, 4 * 2 * 256)
    wq = np.asarray(inp["attn_w_qkv"][0], np.float32)
    wo = np.asarray(inp["attn_w_o"][0], np.float32).reshape(NCH, 128, D).transpose(1, 0, 2)
    wo = np.ascontiguousarray(wo).reshape(128, NCH * D)
    in_maps = []
    for c in cores:
        b, half = c // 2, c % 2
        hseq = np.concatenate([meta, x[b]], axis=0)
        sl = hseq[half * NMAIN: half * NMAIN + NSLOT]
        a = wq.reshape(NCH, 128, 3, 2, 2, 256)[:, :, :, half]
        a = np.ascontiguousarray(a.transpose(3, 1, 2, 0, 4)).reshape(2, 128, 3 * NCH * 256)
        m = {"xT": _to_fm(sl), "vecs": _build_vecs(inp, half), "poolw": pw, "wqkv": a, "wo": wo}
        for k in range(4):
            m["w13_%d" % k] = w13[k]
            m["w2_%d" % k] = w2[k]
        in_maps.append(m)
    nc = _get("F", build_fused)
    r = run_bass_kernel_spmd(nc, in_maps, core_ids=cores)
    out = np.empty((B, 4096, D), np.float32)
    for c in cores:
        b, half = c // 2, c % 2
        out[b, half * NMAIN:(half + 1) * NMAIN] = _from_fm(np.asarray(r.results[c]["y"]))
    return out
```
